# Optimizing a Trainium2 kernel written in Bass

```python
import jax, jax.numpy as jnp
from jax import lax
import numpy as np

D_MODEL = 1024
BATCH = 16
SEQ = 2048
DEPTH = 4

CTX_LEN = 256
GRID_W = 64
N_MOD = 9
EPS = 1e-6
FFN_RESIDUAL = 0.5
D_FF = 2816
HEAD_DIM = 64
ROPE_PAIRS = HEAD_DIM // 4
ROPE_THETA = 10000.0
BLOCK = 128
A_HEADS = 8
A_KV = 2
WINDOW = 128
B_HEADS = 8
B_KV = 2
C_HEADS = 4
C_HEAD_DIM = 128
C_CONV = 5
MLSTM_CHUNK = 64
F_BIAS_LO = 3.0
F_BIAS_HI = 6.0
A_WIDTH = A_HEADS * HEAD_DIM
A_KV_WIDTH = A_KV * HEAD_DIM
B_WIDTH = B_HEADS * HEAD_DIM
B_KV_WIDTH = B_KV * HEAD_DIM
C_WIDTH = C_HEADS * C_HEAD_DIM
BRANCH_WIDTH = 512
N_BRANCH = 3
IN_SIZES = (A_WIDTH, A_KV_WIDTH, A_KV_WIDTH, B_WIDTH, B_KV_WIDTH, B_KV_WIDTH, C_WIDTH, C_WIDTH, C_WIDTH, C_WIDTH, 4 * C_HEADS, N_BRANCH * D_MODEL)
IN_WIDTH = sum(IN_SIZES)

kernel_name = 'hybrid_diffusion_trunk'


def rms_norm(x, g):
    xf = x.astype(jnp.float32)
    y = xf * lax.rsqrt(jnp.mean(xf * xf, axis=-1, keepdims=True) + EPS)
    return (y * g.astype(jnp.float32)).astype(x.dtype)


def swiglu(x, w_gate, w_up, w_down):
    return (jax.nn.silu(x @ w_gate) * (x @ w_up)) @ w_down


def modulated_norm(x, mod, k, norm_g):
    return rms_norm(x, norm_g[2 * k]) * (1 + mod[:, :, 3 * k + 1]) + mod[:, :, 3 * k]


def gated_residual(x, y, mod, k, norm_g, weight):
    return x + weight * mod[:, :, 3 * k + 2] * rms_norm(y, norm_g[2 * k + 1])


def half_ffn(x, mod, k, norm_g, w):
    h = modulated_norm(x, mod, k, norm_g)
    return gated_residual(x, swiglu(h, *w), mod, k, norm_g, FFN_RESIDUAL)


def split_columns(p):
    outs, start = [], 0
    for size in IN_SIZES:
        outs.append(p[..., start:start + size])
        start += size
    return outs


def heads(t, n):
    return t.reshape(t.shape[:-1] + (n, -1))


def group(t, kv):
    return t.reshape(t.shape[:2] + (kv, -1, t.shape[-1]))


def axial_rope_tables(rows):
    row = jnp.repeat(jnp.arange(rows), GRID_W)
    col = jnp.tile(jnp.arange(GRID_W), rows)
    freqs = ROPE_THETA ** (-jnp.arange(ROPE_PAIRS, dtype=jnp.float32) / ROPE_PAIRS)
    ang = jnp.stack([row[:, None] * freqs, col[:, None] * freqs], axis=1)
    return jnp.cos(ang), jnp.sin(ang)


def apply_rope(x, cos, sin):
    xs = x.astype(jnp.float32).reshape(x.shape[:-1] + (2, 2, ROPE_PAIRS))
    x1, x2 = xs[..., 0, :], xs[..., 1, :]
    c, s = cos[None, :, None], sin[None, :, None]
    out = jnp.stack([x1 * c - x2 * s, x2 * c + x1 * s], axis=-2)
    return out.reshape(x.shape).astype(x.dtype)


def gqa_softmax(q, k, v, mask, sink):
    kv, g = q.shape[2], q.shape[3]
    s = jnp.einsum('bqkgd,bskd->bkgqs', q, k).astype(jnp.float32) * (q.shape[-1] ** -0.5)
    if mask is not None:
        s = jnp.where(mask, s, -jnp.inf)
    if sink is not None:
        sink_col = jnp.broadcast_to(sink.astype(jnp.float32).reshape(1, kv, g, 1, 1), s.shape[:-1] + (1,))
        p = jax.nn.softmax(jnp.concatenate([s, sink_col], axis=-1), axis=-1)[..., :-1]
    else:
        p = jax.nn.softmax(s, axis=-1)
    o = jnp.einsum('bkgqs,bskd->bqkgd', p.astype(v.dtype), v)
    return o.reshape(o.shape[:2] + (-1,))


def blockwise_queries(fn, q):
    bsz, t_lat = q.shape[:2]
    nb = t_lat // BLOCK
    qb = jnp.swapaxes(q.reshape((bsz, nb, BLOCK) + q.shape[2:]), 0, 1)
    out = lax.map(fn, (qb, jnp.arange(nb)))
    return jnp.swapaxes(out, 0, 1).reshape(bsz, t_lat, -1)


def windowed_attention(q, k, v, k_ctx, v_ctx, sink):
    t_lat, t_ctx = q.shape[1], k_ctx.shape[1]
    span = BLOCK + 2 * WINDOW
    pad = ((0, 0), (WINDOW, WINDOW), (0, 0), (0, 0))
    kp, vp = jnp.pad(k, pad), jnp.pad(v, pad)
    r, j = jnp.arange(BLOCK), jnp.arange(span)
    near = jnp.abs(r[:, None] + WINDOW - j[None, :]) <= WINDOW
    ctx_cols = jnp.ones((BLOCK, t_ctx), bool)

    def one_block(args):
        qi, i = args
        start = i * BLOCK
        kw = lax.dynamic_slice_in_dim(kp, start, span, axis=1)
        vw = lax.dynamic_slice_in_dim(vp, start, span, axis=1)
        s_pos = start - WINDOW + j
        in_range = (s_pos >= 0) & (s_pos < t_lat)
        mask = jnp.concatenate([ctx_cols, near & in_range[None, :]], axis=1)
        return gqa_softmax(qi, jnp.concatenate([k_ctx, kw], axis=1), jnp.concatenate([v_ctx, vw], axis=1), mask, sink)

    return blockwise_queries(one_block, q)


def dense_attention(q, k_all, v_all):
    return blockwise_queries(lambda args: gqa_softmax(args[0], k_all, v_all, None, None), q)


def centred_conv(x, w, b):
    y = lax.conv_general_dilated(x, w[:, None, :].astype(x.dtype), window_strides=(1,),
                                 padding=[(C_CONV // 2, C_CONV // 2)],
                                 dimension_numbers=('NWC', 'WIO', 'NWC'), feature_group_count=x.shape[-1])
    return y + b


def mlstm_scan(q, k, v, li, lf, state):
    bsz, nh, t_len, dk = q.shape
    nc = t_len // MLSTM_CHUNK

    def chunks(t):
        t = t.astype(jnp.float32).reshape((bsz, nh, nc, MLSTM_CHUNK) + t.shape[3:])
        return jnp.moveaxis(t, 2, 0)

    causal = jnp.tril(jnp.ones((MLSTM_CHUNK, MLSTM_CHUNK), bool))

    def step(carry, inp):
        C, n, m = carry
        qc, kc, vc, ic, fc = inp
        b = jnp.cumsum(fc, axis=-1)
        d_log = jnp.where(causal, b[..., :, None] - b[..., None, :] + ic[..., None, :], -jnp.inf)
        inter = b + m[..., None]
        m_t = jnp.maximum(inter, jnp.max(d_log, axis=-1))
        w_intra = jnp.exp(d_log - m_t[..., None])
        w_inter = jnp.exp(inter - m_t)
        s = jnp.einsum('bhtd,bhsd->bhts', qc, kc) * w_intra
        num = jnp.einsum('bhts,bhsv->bhtv', s, vc) + w_inter[..., None] * jnp.einsum('bhtd,bhdv->bhtv', qc, C)
        den = jnp.sum(s, axis=-1) + w_inter * jnp.einsum('bhtd,bhd->bht', qc, n)
        h = num / jnp.maximum(jnp.abs(den), jnp.exp(-m_t))[..., None]
        m_new = m_t[..., -1]
        w_end = jnp.exp(b[..., -1:] - b + ic - m_new[..., None])
        decay = jnp.exp(b[..., -1] + m - m_new)
        C = decay[..., None, None] * C + jnp.einsum('bhs,bhsd,bhsv->bhdv', w_end, kc, vc)
        n = decay[..., None] * n + jnp.einsum('bhs,bhsd->bhd', w_end, kc)
        return (C, n, m_new), h

    state, h = lax.scan(step, state, (chunks(q) * (dk ** -0.5), chunks(k), chunks(v), chunks(li), chunks(lf)))
    h = jnp.moveaxis(h, 0, 2).reshape(bsz, nh, t_len, -1)
    return h.astype(v.dtype), state


def mlstm_prep(q, k, v, g, conv_w, conv_b, gate_b):
    qk = jax.nn.silu(centred_conv(jnp.concatenate([q, k], axis=-1), conv_w, conv_b))
    q, k = jnp.split(qk, 2, axis=-1)
    q, k, v = (jnp.transpose(heads(t, C_HEADS), (0, 2, 1, 3)) for t in (q, k, v))
    gp = jnp.transpose((g + gate_b).astype(jnp.float32), (0, 2, 1))
    i_f, f_f, i_b, f_b = jnp.split(gp, 4, axis=1)
    fwd = (q, k, v, i_f, jax.nn.log_sigmoid(f_f))
    bwd = tuple(jnp.flip(t, axis=2) for t in (q, k, v, i_b, jax.nn.log_sigmoid(f_b)))
    return fwd, bwd


def mlstm_mixer(lat, ctx_in, need_ctx, conv_w, conv_b, gate_b, head_g):
    q, k, v, o, g = lat
    qc, kc, vc, oc, gc = ctx_in
    fwd, bwd = mlstm_prep(q, k, v, g, conv_w, conv_b, gate_b)
    fwd_c, bwd_c = mlstm_prep(qc, kc, vc, gc, conv_w, conv_b, gate_b)
    bsz = q.shape[0]
    zero = (jnp.zeros((bsz, C_HEADS, C_HEAD_DIM, C_HEAD_DIM), jnp.float32),
            jnp.zeros((bsz, C_HEADS, C_HEAD_DIM), jnp.float32),
            jnp.zeros((bsz, C_HEADS), jnp.float32))
    h_cf, st_f = mlstm_scan(*fwd_c, zero)
    h_cb, st_b = mlstm_scan(*bwd_c, zero)
    h_f, _ = mlstm_scan(*fwd, st_f)
    h_b, _ = mlstm_scan(*bwd, st_b)

    def readout(h, og):
        h = rms_norm(jnp.transpose(h, (0, 2, 1, 3)), head_g.reshape(C_HEADS, C_HEAD_DIM))
        return jax.nn.sigmoid(og) * h.reshape(h.shape[:2] + (C_WIDTH,))

    y = readout(h_f + jnp.flip(h_b, axis=2), o)
    if not need_ctx:
        return y, None
    return y, readout(h_cf + jnp.flip(h_cb, axis=2), oc)


def merge(y_a, y_b, y_m, gate_logits, w_branch, w_out):
    y = jnp.stack([y_a, y_b, y_m], axis=-2)
    z = jnp.einsum('btnw,nwd->btnd', y, w_branch)
    g = jax.nn.sigmoid(gate_logits.reshape(gate_logits.shape[:-1] + (N_BRANCH, D_MODEL)))
    return jnp.sum(g * z, axis=-2) @ w_out


def token_mixers(h, hc, need_ctx, cos, sin, w_in, attn_sink, qk_norm_g, conv_w, conv_b, gate_b, head_g, w_branch, w_out):
    qa, ka, va, qb, kb, vb, qm, km, vm, om, gm, gl = split_columns(h @ w_in)
    qa_c, ka_c, va_c, qb_c, kb_c, vb_c, qm_c, km_c, vm_c, om_c, gm_c, gl_c = split_columns(hc @ w_in)
    qa = group(apply_rope(heads(qa, A_HEADS), cos, sin), A_KV)
    ka = apply_rope(heads(ka, A_KV), cos, sin)
    va = heads(va, A_KV)
    qa_c, ka_c, va_c = group(heads(qa_c, A_HEADS), A_KV), heads(ka_c, A_KV), heads(va_c, A_KV)
    y_a = windowed_attention(qa, ka, va, ka_c, va_c, attn_sink)
    qn, kn = qk_norm_g[0], qk_norm_g[1]
    qb = group(apply_rope(rms_norm(heads(qb, B_HEADS), qn), cos, sin), B_KV)
    kb = apply_rope(rms_norm(heads(kb, B_KV), kn), cos, sin)
    vb = heads(vb, B_KV)
    qb_c = group(rms_norm(heads(qb_c, B_HEADS), qn), B_KV)
    kb_c, vb_c = rms_norm(heads(kb_c, B_KV), kn), heads(vb_c, B_KV)
    y_b = dense_attention(qb, jnp.concatenate([kb_c, kb], axis=1), jnp.concatenate([vb_c, vb], axis=1))
    y_m, y_m_c = mlstm_mixer((qm, km, vm, om, gm), (qm_c, km_c, vm_c, om_c, gm_c), need_ctx,
                             conv_w, conv_b, gate_b, head_g)
    y = merge(y_a, y_b, y_m, gl, w_branch, w_out)
    if not need_ctx:
        return y, None
    y_a_c = gqa_softmax(qa_c, ka_c, va_c, None, attn_sink)
    y_b_c = gqa_softmax(qb_c, kb_c, vb_c, None, None)
    return y, merge(y_a_c, y_b_c, y_m_c, gl_c, w_branch, w_out)


def setup_inputs(seed: int = 0) -> dict:
    key = jax.random.key(seed)
    ks = jax.random.split(key, 20)
    f32 = jnp.float32

    def nrm(k, shape, scale):
        return jax.random.normal(k, shape, f32) * scale

    L = DEPTH
    i_bias = nrm(ks[15], (L, 2, C_HEADS), 0.1)
    f_bias = jnp.linspace(F_BIAS_LO, F_BIAS_HI, C_HEADS, dtype=f32) + nrm(ks[16], (L, 2, C_HEADS), 0.1)
    return {
        'x': nrm(ks[0], (BATCH, SEQ, D_MODEL), 1.0),
        'c': nrm(ks[1], (BATCH, D_MODEL), 1.0),
        'ctx': nrm(ks[2], (BATCH, CTX_LEN, D_MODEL), 1.0),
        'c_ctx': nrm(ks[3], (D_MODEL,), 1.0),
        'w_ada': nrm(ks[4], (L, D_MODEL, N_MOD * D_MODEL), 0.5 * D_MODEL ** -0.5),
        'b_ada': nrm(ks[5], (L, N_MOD * D_MODEL), 0.01),
        'norm_g': 1.0 + nrm(ks[6], (L, 6, D_MODEL), 0.02),
        'ffn_w_gate': nrm(ks[7], (L, 2, D_MODEL, D_FF), D_MODEL ** -0.5),
        'ffn_w_up': nrm(ks[8], (L, 2, D_MODEL, D_FF), D_MODEL ** -0.5),
        'ffn_w_down': nrm(ks[9], (L, 2, D_FF, D_MODEL), D_FF ** -0.5),
        'w_in': nrm(ks[10], (L, D_MODEL, IN_WIDTH), D_MODEL ** -0.5),
        'attn_sink': nrm(ks[11], (L, A_HEADS), 0.5),
        'qk_norm_g': 1.0 + nrm(ks[12], (L, 2, HEAD_DIM), 0.02),
        'conv_w': nrm(ks[13], (L, C_CONV, 2 * C_WIDTH), C_CONV ** -0.5),
        'conv_b': nrm(ks[14], (L, 2 * C_WIDTH), 0.01),
        'mlstm_gate_b': jnp.stack([i_bias, f_bias], axis=2).reshape(L, 4 * C_HEADS),
        'mlstm_norm_g': 1.0 + nrm(ks[17], (L, C_WIDTH), 0.02),
        'w_branch': nrm(ks[18], (L, N_BRANCH, BRANCH_WIDTH, D_MODEL), BRANCH_WIDTH ** -0.5),
        'w_out': nrm(ks[19], (L, D_MODEL, D_MODEL), D_MODEL ** -0.5),
    }


def reference(x, c, ctx, c_ctx, w_ada, b_ada, norm_g, ffn_w_gate, ffn_w_up, ffn_w_down, w_in, attn_sink,
              qk_norm_g, conv_w, conv_b, mlstm_gate_b, mlstm_norm_g, w_branch, w_out):
    bsz, t_lat = x.shape[:2]
    rows = t_lat // GRID_W
    cos, sin = axial_rope_tables(rows)
    for l in range(DEPTH):
        last = l == DEPTH - 1
        mod = (jax.nn.silu(c) @ w_ada[l] + b_ada[l]).reshape(bsz, 1, N_MOD, D_MODEL)
        mod_c = (jax.nn.silu(c_ctx) @ w_ada[l] + b_ada[l]).reshape(1, 1, N_MOD, D_MODEL)
        ffn1 = (ffn_w_gate[l, 0], ffn_w_up[l, 0], ffn_w_down[l, 0])
        ffn2 = (ffn_w_gate[l, 1], ffn_w_up[l, 1], ffn_w_down[l, 1])
        x = half_ffn(x, mod, 0, norm_g[l], ffn1)
        ctx = half_ffn(ctx, mod_c, 0, norm_g[l], ffn1)
        h = modulated_norm(x, mod, 1, norm_g[l])
        hc = modulated_norm(ctx, mod_c, 1, norm_g[l])
        y, y_c = token_mixers(h, hc, not last, cos, sin, w_in[l], attn_sink[l], qk_norm_g[l], conv_w[l], conv_b[l],
                              mlstm_gate_b[l], mlstm_norm_g[l], w_branch[l], w_out[l])
        x = gated_residual(x, y, mod, 1, norm_g[l], 1.0)
        x = half_ffn(x, mod, 2, norm_g[l], ffn2)
        if not last:
            ctx = gated_residual(ctx, y_c, mod_c, 1, norm_g[l], 1.0)
            ctx = half_ffn(ctx, mod_c, 2, norm_g[l], ffn2)
    return x
```

```python
import contextlib
import numpy as np
import concourse.bass as bass
import concourse.mybir as mybir
from concourse.bass_utils import run_bass_kernel_spmd

F32 = mybir.dt.float32
BF16 = mybir.dt.bfloat16
AF = mybir.ActivationFunctionType
ALU = mybir.AluOpType

D = 1024
KC = 8
FF = 2816
FC = 22
GRID_W = 64
HD = 64
IN_W = 6672
O_QA, O_KA, O_VA, O_QB, O_KB, O_VB = 0, 512, 640, 768, 1280, 1408
O_QM, O_KM, O_VM, O_OM, O_GM, O_GL = 1536, 2048, 2560, 3072, 3584, 3600
EPS = 1e-6
NEG = -30000.0


class Cfg:
    def __init__(self, T=2048, CTX=256, L=4, NB=2, n_cores=8, stop_after=None):
        self.T, self.CTX, self.L, self.NB, self.n_cores = T, CTX, L, NB, n_cores
        self.NT = T + CTX
        self.stop_after = stop_after


class Sem:
    def __init__(self, h):
        self.h = h
        self.count = 0


class H:
    __slots__ = ("name", "w", "r", "excl")

    def __init__(self, name=""):
        self.name = name
        self.excl = False
        self.w = None
        self.r = {}


class V:
    __slots__ = ("ap", "hs")

    def __init__(self, ap, hs):
        self.ap = ap
        self.hs = hs


class Buf:
    def __init__(self, t, name):
        self.t = t
        self.name = name
        self.h = H(name)
        self.regs = {}

    def __getitem__(self, idx):
        return V(self.t[idx], [self.h])

    def r(self, key):
        if key not in self.regs:
            self.regs[key] = H(f"{self.name}.{key}")
        return _Reg(self, [self.regs[key]])

    def rs(self, keys):
        hs = []
        for key in keys:
            if key not in self.regs:
                self.regs[key] = H(f"{self.name}.{key}")
            hs.append(self.regs[key])
        return _Reg(self, hs)


class _Reg:
    def __init__(self, buf, hs):
        self.buf, self.hs = buf, hs

    def __getitem__(self, idx):
        return V(self.buf.t[idx], self.hs)


class Eng:
    def __init__(self, name, raw, sem, self_sync):
        self.name, self.raw, self.sem = name, raw, sem
        self.seen = {}
        self.self_sync = self_sync


class K:
    def __init__(self, nc, es):
        self.nc, self.es, self.es0 = nc, es, es
        self.n_inst = 0
        self.sem_by_name = {}
        mk = lambda n: Sem(es.enter_context(nc.semaphore(n)))
        self.pe = Eng("pe", nc.tensor, mk("s_pe"), False)
        self.act = Eng("act", nc.scalar, mk("s_act"), True)
        self.dve = Eng("dve", nc.vector, mk("s_dve"), True)
        self.pool = Eng("pool", nc.gpsimd, mk("s_pool"), True)
        self.sp = Eng("sp", nc.sync, mk("s_sp"), True)
        self.engs = [self.pe, self.act, self.dve, self.pool, self.sp]
        self.dma_sems = []

    def sbuf(self, name, shape, dt):
        self.n_buf = getattr(self, "n_buf", 0) + 1
        name = f"{name}_{self.n_buf}"
        return Buf(self.es.enter_context(self.nc.sbuf_tensor(name, list(shape), dt)), name)

    def psum(self, name, shape, dt):
        b = Buf(self.es.enter_context(self.nc.psum_tensor(name, list(shape), dt)), name)
        b.h.excl = True
        return b

    def new_dma_sem(self, name):
        if name not in self.sem_by_name:
            s = Sem(self.es0.enter_context(self.nc.semaphore(name)))
            self.dma_sems.append(s)
            self.sem_by_name[name] = s
        return self.sem_by_name[name]

    def _wait(self, eng, deps):
        for sem, val in deps.items():
            if sem is eng.sem and not eng.self_sync:
                continue
            if eng.seen.get(sem, 0) < val:
                eng.raw.wait_ge(sem.h, val)
                eng.seen[sem] = val

    @staticmethod
    def _deps(reads, writes, own=None):
        deps = {}

        def add(ev):
            if ev is not None:
                s, v = ev
                if deps.get(s, 0) < v:
                    deps[s] = v
        for h in reads:
            add(h.w)
            if h.excl:
                for s, v in h.r.items():
                    if s is not own:
                        add((s, v))
        for h in writes:
            add(h.w)
            for s, v in h.r.items():
                add((s, v))
        return deps

    def emit(self, eng, reads, writes, fn):
        deps = self._deps(reads, writes, eng.sem)
        self._wait(eng, deps)
        inst = fn()
        inst.then_inc(eng.sem.h, 1)
        eng.sem.count += 1
        ev = (eng.sem, eng.sem.count)
        for h in reads:
            if h.r.get(ev[0], 0) < ev[1]:
                h.r[ev[0]] = ev[1]
        for h in writes:
            h.w = ev
            h.r = {}
        self.n_inst += 1
        return inst

    def dma(self, q, out, in_, sem, **kw):
        reads, writes = in_.hs, out.hs
        deps = self._deps(reads, writes)
        if sem.count:
            deps[sem] = max(deps.get(sem, 0), sem.count)
        self._wait(q, deps)
        inst = q.raw.dma_start(out=out.ap, in_=in_.ap, **kw)
        inst.then_inc(sem.h, 16)
        sem.count += 16
        ev = (sem, sem.count)
        for h in reads:
            if h.r.get(ev[0], 0) < ev[1]:
                h.r[ev[0]] = ev[1]
        for h in writes:
            h.w = ev
            h.r = {}
        self.n_inst += 1

    def barrier(self):
        deps = {}
        for e in self.engs:
            if e.sem.count:
                deps[e.sem] = e.sem.count
        for s in self.dma_sems:
            if s.count:
                deps[s] = s.count
        for e in self.engs:
            d = {s: v for s, v in deps.items() if s is not e.sem}
            self._wait(e, d)

    def final_wait(self, eng, sems):
        for s in sems:
            eng.raw.wait_ge(s.h, s.count)

    @staticmethod
    def _hs(*vs):
        out = []
        for v in vs:
            if isinstance(v, V):
                out += v.hs
        return out

    @staticmethod
    def _a(v):
        return v.ap if isinstance(v, V) else v

    def mm(self, out, lhsT, rhs, start=True, stop=True, **kw):
        return self.emit(self.pe, lhsT.hs + rhs.hs, out.hs,
                         lambda: self.nc.tensor.matmul(out.ap, lhsT.ap, rhs.ap, start=start, stop=stop, **kw))

    def transpose(self, out, in_, ident):
        return self.emit(self.pe, in_.hs + ident.hs, out.hs,
                         lambda: self.nc.tensor.transpose(out.ap, in_.ap, ident.ap))

    def actf(self, out, in_, func, bias=None, scale=None):
        kw = {}
        if bias is not None:
            kw["bias"] = self._a(bias)
        if scale is not None:
            kw["scale"] = self._a(scale)
        return self.emit(self.act, self._hs(in_, bias, scale), out.hs,
                         lambda: self.nc.scalar.activation(out=out.ap, in_=in_.ap, func=func, **kw))

    def tt(self, eng, out, in0, in1, op):
        return self.emit(eng, in0.hs + in1.hs, out.hs,
                         lambda: eng.raw.tensor_tensor(out=out.ap, in0=in0.ap, in1=in1.ap, op=op))

    def ts(self, eng, out, in0, s1, s2, op0, op1=None):
        kw = {}
        if op1 is not None:
            kw["op1"] = op1
        return self.emit(eng, self._hs(in0, s1, s2), out.hs,
                         lambda: eng.raw.tensor_scalar(out=out.ap, in0=in0.ap, scalar1=self._a(s1),
                                                       scalar2=self._a(s2), op0=op0, **kw))

    def stt(self, out, in0, scalar, in1, op0, op1):
        return self.emit(self.dve, self._hs(in0, scalar, in1), out.hs,
                         lambda: self.nc.vector.scalar_tensor_tensor(out=out.ap, in0=in0.ap, scalar=self._a(scalar),
                                                                     in1=in1.ap, op0=op0, op1=op1))

    def copy(self, eng, out, in_):
        if eng is self.act:
            return self.emit(eng, in_.hs, out.hs, lambda: self.nc.scalar.copy(out=out.ap, in_=in_.ap))
        return self.emit(eng, in_.hs, out.hs, lambda: eng.raw.tensor_copy(out=out.ap, in_=in_.ap))

    def recip(self, out, in_):
        return self.emit(self.dve, in_.hs, out.hs, lambda: self.nc.vector.reciprocal(out=out.ap, in_=in_.ap))

    def memset(self, eng, out, val):
        return self.emit(eng, [], out.hs, lambda: eng.raw.memset(out.ap, val))


class WPool:
    def __init__(self, k, name, shape, n, dt=BF16):
        self.k = k
        self.slots = [k.sbuf(f"{name}{i}", shape, dt) for i in range(n)]
        self.sems = [k.new_dma_sem(f"ws_{name}{i}") for i in range(n)]
        self.i = 0

    def load(self, fn):
        s, sem = self.slots[self.i], self.sems[self.i]
        self.i = (self.i + 1) % len(self.slots)
        fn(s, lambda out, in_, **kw: self.k.dma(self.k.pool, out, in_, sem, **kw))
        return s


def col_groups(a, b, CTX, maxn=512):
    out = []
    c = a
    while c < b:
        lim = CTX if c < CTX else b
        n = min(maxn, lim - c, b - c)
        out.append((c, n, c < CTX))
        c += n
    return out


def build_program(cfg):
    nc = bass.Bass("TRN2", target_bir_lowering=False)
    es = contextlib.ExitStack()
    k = K(nc, es)
    T, CTX, L, NB, NT = cfg.T, cfg.CTX, cfg.L, cfg.NB, cfg.NT
    NCLS = NB + 1
    CLS_CTX = NB

    def din(name, shape, dt=F32):
        t = nc.dram_tensor(name, list(shape), dt, kind="ExternalInput")
        return Buf(t.ap(), name)

    xT = din("xT", [NB, 128, KC, T])
    ctxT = din("ctxT", [NB, 128, KC, CTX])
    c_in = din("c_in", [128, KC, NCLS])
    w_ada = din("w_ada", [L, D, 9 * D])
    b_ada_t = din("b_ada_t", [128, L, 72])
    norm_g_t = din("norm_g_t", [128, L, 6, KC])
    w_gate = din("ffn_w_gate", [L, 2, D, FF])
    w_up = din("ffn_w_up", [L, 2, D, FF])
    w_down = din("ffn_w_down", [L, 2, FF, D])
    w_in = din("w_in", [L, D, IN_W])
    w_branch = din("w_branch", [L, 3, 512, D])
    w_out = din("w_out", [L, D, D])
    consts_bf = din("consts_bf", [128, 8, 128])
    consts_f = din("consts_f", [128, 8, 128])
    winmask_d = din("winmask", [128, 384])
    rope_d = din("rope", [128, 4, 64])
    convw_t = din("convw_t", [128, L, 8, 5])
    convb_t = din("convb_t", [128, L, 8])
    mg_t = din("mg_t", [128, L, 4])
    qkg_t = din("qkg_t", [128, L, 2])
    sink_t = din("sink_t", [128, L, 8])
    gateb_t = din("gateb_t", [128, L, 16])
    outT_t = nc.dram_tensor("outT", [NB, 128, KC, T], F32, kind="ExternalOutput")
    outT = Buf(outT_t.ap(), "outT")

    R = k.sbuf("R", [128, KC, NT], F32)
    NRB = NT // 256

    def Rv(c0, n):
        keys = list(range(c0 // 256, (c0 + n - 1) // 256 + 1))
        return R.rs(keys)

    gsb = k.sbuf("gsb", [128, L, 6, KC], F32)
    tabA = k.sbuf("tabA", [128, L, 3, NCLS, KC], F32)
    tabB = k.sbuf("tabB", [128, L, 3, NCLS, KC], F32)
    tabG = k.sbuf("tabG", [128, L, 3, NCLS, KC], F32)
    cbf = k.sbuf("cbf", [128, 8, 128], BF16)
    ONES = lambda: cbf[:, 0, :]
    epsb = k.sbuf("epsb", [128, 1], F32)
    cf = k.sbuf("cf", [128, 8, 128], F32)
    TRI = {1: (lambda: cf[:, 0, :]), -1: (lambda: cf[:, 1, :])}
    MSK = {1: (lambda: cf[:, 2, :]), -1: (lambda: cf[:, 3, :])}
    winmask = k.sbuf("winmask_sb", [128, 384], BF16)
    ropec = k.sbuf("ropec", [128, 4, 64], F32)
    convw = k.sbuf("convw", [128, L, 8, 5], F32)
    convb = k.sbuf("convb", [128, L, 8], F32)
    mgs = k.sbuf("mgs", [128, L, 4], F32)
    qkg = k.sbuf("qkg", [128, L, 2], F32)
    esink = k.sbuf("esink", [128, L, 8], F32)
    gateb = k.sbuf("gateb", [128, L, 16], F32)
    lnsc = k.sbuf("lnsc", [128, 1], F32)
    sem_in = k.new_dma_sem("sem_in")
    sem_c = k.new_dma_sem("sem_c")
    sem_out = k.new_dma_sem("sem_out")

    PS = [k.psum(f"ps{i}", [128, 512], F32) for i in range(7)]
    PSB = k.psum("psb", [128, 1024], BF16)

    k.dma(k.pool, cbf[:, :, :], consts_bf[:, :, :], sem_c)
    k.dma(k.sp, gsb[:, :, :, :], norm_g_t[:, :, :, :], sem_c)
    k.memset(k.dve, epsb[:, :], EPS)
    k.memset(k.dve, lnsc[:, :], float(np.log(128.0 ** -0.5)))
    k.dma(k.sp, cf[:, :, :], consts_f[:, :, :], sem_c)
    k.dma(k.pool, winmask[:, :], winmask_d[:, :], sem_c)
    k.dma(k.sp, ropec[:, :, :], rope_d[:, :, :], sem_c)
    k.dma(k.sp, convw[:, :, :, :], convw_t[:, :, :, :], sem_c)
    k.dma(k.sp, convb[:, :, :], convb_t[:, :, :], sem_c)
    k.dma(k.sp, mgs[:, :, :], mg_t[:, :, :], sem_c)
    k.dma(k.sp, qkg[:, :, :], qkg_t[:, :, :], sem_c)
    k.dma(k.sp, esink[:, :, :], sink_t[:, :, :], sem_c)
    k.dma(k.sp, gateb[:, :, :], gateb_t[:, :, :], sem_c)
    k.actf(esink[:, :, :], esink[:, :, :], AF.Exp)
    with contextlib.ExitStack() as es2:
        k.es = es2
        modraw = k.sbuf("modraw", [128, L, 72, NCLS], F32)
        csb = k.sbuf("csb", [128, KC, NCLS], F32)
        csig = k.sbuf("csig", [128, KC, NCLS], F32)
        cact = k.sbuf("cact", [128, KC, NCLS], BF16)
        bada = k.sbuf("bada", [128, L, 72], F32)
        wp = WPool(k, "wada", [128, KC, 512], 3)
        k.dma(k.sp, csb[:, :, :], c_in[:, :, :], sem_c)
        k.dma(k.sp, bada[:, :, :], b_ada_t[:, :, :], sem_c)
        k.actf(csig[:, :, :], csb[:, :, :], AF.Sigmoid)
        k.tt(k.dve, cact[:, :, :], csb[:, :, :], csig[:, :, :], ALU.mult)
        for l in range(L):
            for n4 in range(18):
                src = w_ada[l, :, n4 * 512:(n4 + 1) * 512]

                def ld(s, dma, src=src):
                    dma(s[:, :, :], V(src.ap.rearrange("(c p) n -> p c n", p=128), src.hs))
                wt = wp.load(ld)
                ps = PS[n4 % 4]
                for j in range(4):
                    n = n4 * 4 + j
                    for c in range(KC):
                        k.mm(ps[:, j * 8:j * 8 + NCLS], wt[:, c, j * 128:(j + 1) * 128], cact[:, c, :],
                             start=(c == 0), stop=(c == KC - 1))
                for j in range(4):
                    n = n4 * 4 + j
                    k.ts(k.dve, modraw[:, l, n, :], ps[:, j * 8:j * 8 + NCLS], bada[:, l, n:n + 1], None, ALU.add)
        for l in range(L):
            for kk in range(3):
                wgt = 1.0 if kk == 1 else 0.5
                for cl in range(NCLS):
                    sl = lambda m: modraw[:, l, m * 8:(m + 1) * 8, cl]
                    k.copy(k.dve, tabB[:, l, kk, cl, :], sl(3 * kk))
                    k.stt(tabA[:, l, kk, cl, :], sl(3 * kk + 1), 1.0, gsb[:, l, 2 * kk, :], ALU.add, ALU.mult)
                    k.stt(tabG[:, l, kk, cl, :], sl(3 * kk + 2), wgt, gsb[:, l, 2 * kk + 1, :], ALU.mult, ALU.mult)
        k.barrier()
    k.es = es

    def norm_stats(src_fn, n, ps, sqbuf, rstd_out):
        for c in range(KC):
            k.actf(sqbuf[:, c % 2, :n], src_fn(c), AF.Square)
            k.mm(ps[:, :n], ONES(), sqbuf[:, c % 2, :n], start=(c == 0), stop=(c == KC - 1))
        k.actf(rstd_out, ps[:, :n], AF.Sqrt, bias=epsb[:, 0:1], scale=1.0 / D)
        k.recip(rstd_out, rstd_out)

    def ffn_phase(l, kk, b):
        which = 0 if kk == 0 else 1
        TT = 768
        with contextlib.ExitStack() as es2:
            k.es = es2
            hid = k.sbuf("hid", [128, FC, TT], BF16)
            ybuf = k.sbuf("ybuf", [128, KC, TT], F32)
            hn = Buf(ybuf.t[:, :, :].rearrange("p c t -> p (c t)").bitcast(BF16)[:, 0:KC * TT]
                     .rearrange("p (c t) -> p c t", c=KC), "hn")
            hn.h = ybuf.h
            sq = k.sbuf("sq", [128, 2, 512], BF16)
            rstd = k.sbuf("rstd", [128, 512], F32)
            tmp = k.sbuf("tmp", [128, 2, 512], F32)
            sg = k.sbuf("sg", [128, 2, 512], F32)
            wg_p = WPool(k, "wg", [128, KC, 256], 3)
            wu_p = WPool(k, "wu", [128, KC, 256], 3)
            wd_p = WPool(k, "wd", [128, FC, 128], 2)
            for t0 in range(0, NT, TT):
                t1 = min(NT, t0 + TT)
                groups = col_groups(t0, t1, CTX)
                for (c0, n, isc) in groups:
                    cl = CLS_CTX if isc else b
                    o = c0 - t0
                    norm_stats(lambda c: Rv(c0, n)[:, c, c0:c0 + n], n, PS[0], sq, rstd[:, :n])
                    for c in range(KC):
                        tb = tmp[:, c % 2, :n]
                        k.stt(tb, Rv(c0, n)[:, c, c0:c0 + n], tabA[:, l, kk, cl, c:c + 1], rstd[:, :n],
                              ALU.mult, ALU.mult)
                        k.actf(hn[:, c, o:o + n], tb, AF.Identity, bias=tabB[:, l, kk, cl, c:c + 1])
                for j in range(FC):
                    def ldg(s, dma, j=j):
                        src = w_gate[l, which, :, j * 128:(j + 2) * 128]
                        dma(s[:, :, :], V(src.ap.rearrange("(c p) n -> p c n", p=128), src.hs))

                    def ldu(s, dma, j=j):
                        src = w_up[l, which, :, j * 128:(j + 2) * 128]
                        dma(s[:, :, :], V(src.ap.rearrange("(c p) n -> p c n", p=128), src.hs))
                    if j % 2 == 0:
                        wg2, wu2 = wg_p.load(ldg), wu_p.load(ldu)
                    jo = (j % 2) * 128
                    wg = _Reg(wg2, [wg2.h])
                    wu = _Reg(wu2, [wu2.h])
                    for gi, (c0, n, isc) in enumerate(groups):
                        o = c0 - t0
                        pg, pu = PS[1 + 2 * ((j * 2 + gi) % 2)], PS[2 + 2 * ((j * 2 + gi) % 2)]
                        for c in range(KC):
                            k.mm(pg[:, :n], wg[:, c, jo:jo + 128], hn[:, c, o:o + n], start=(c == 0), stop=(c == KC - 1))
                        for c in range(KC):
                            k.mm(pu[:, :n], wu[:, c, jo:jo + 128], hn[:, c, o:o + n], start=(c == 0), stop=(c == KC - 1))
                        sgb = sg[:, (j * 2 + gi) % 2, :n]
                        k.actf(sgb, pg[:, :n], AF.Silu)
                        k.tt(k.dve, hid[:, j, o:o + n], sgb, pu[:, :n], ALU.mult)
                for i in range(KC):
                    def ldd(s, dma, i=i):
                        src = w_down[l, which, :, i * 128:(i + 1) * 128]
                        dma(s[:, :, :], V(src.ap.rearrange("(c p) n -> p c n", p=128), src.hs))
                    wd = wd_p.load(ldd)
                    for gi, (c0, n, isc) in enumerate(groups):
                        o = c0 - t0
                        py = PS[5 + ((i * 2 + gi) % 2)]
                        for j in range(FC):
                            k.mm(py[:, :n], wd[:, j, :], hid[:, j, o:o + n], start=(j == 0), stop=(j == FC - 1))
                        k.copy(k.act, ybuf[:, i, o:o + n], py[:, :n])
                for (c0, n, isc) in groups:
                    cl = CLS_CTX if isc else b
                    o = c0 - t0
                    norm_stats(lambda c: ybuf[:, c, o:o + n], n, PS[0], sq, rstd[:, :n])
                    for c in range(KC):
                        tb = tmp[:, c % 2, :n]
                        k.stt(tb, ybuf[:, c, o:o + n], tabG[:, l, kk, cl, c:c + 1], rstd[:, :n], ALU.mult, ALU.mult)
                        k.tt(k.dve, Rv(c0, n)[:, c, c0:c0 + n], Rv(c0, n)[:, c, c0:c0 + n], tb, ALU.add)
            k.barrier()
        k.es = es


    def wtile(pool, l, col_ranges, width):
        def ld(s_, dma):
            o = 0
            for (c0, n) in col_ranges:
                src = w_in[l, :, c0:c0 + n]
                dma(s_[:, :, o:o + n], V(src.ap.rearrange("(c p) n -> p c n", p=128), src.hs))
                o += n
        return pool.load(ld)

    def mixer_phase(l, b):
        MT = 256
        NKT = NT // 128
        with contextlib.ExitStack() as es2:
            k.es = es2
            Kst = k.sbuf("Kst", [128, 4, NT], BF16)
            Vst = k.sbuf("Vst", [128, 2, NKT, 128], BF16)
            Hf = k.sbuf("Hf", [128, 4, NT], BF16)
            hnb = k.sbuf("hnb", [128, KC, MT + 4], BF16)
            sq = k.sbuf("msq", [128, 2, MT + 8], BF16)
            rstd = k.sbuf("mrstd", [128, MT + 8], F32)
            tmp = k.sbuf("mtmp", [128, 2, MT + 8], F32)
            ropeC = k.sbuf("ropeC", [128, MT], F32)
            ropeS = k.sbuf("ropeS", [128, MT], F32)
            stage = k.sbuf("stage", [128, 2, MT + 4], F32)
            qk = k.sbuf("qk", [128, 8, MT], BF16)
            Vm = k.sbuf("Vm", [128, MT // 128, 4, 129], BF16)
            gsbuf = k.sbuf("gsbuf", [128, MT // 128, 16], F32)
            lfb = k.sbuf("lfb", [128, MT // 128, 4], F32)
            sm = k.sbuf("sm", [128, 32], F32)
            lfrep = k.sbuf("lfrep", [128, 128], F32)
            dlog = k.sbuf("dlog", [128, 128], F32)
            DT = k.sbuf("DT", [128, 128], F32)
            ebt = k.sbuf("ebt", [128, 128], F32)
            qp = k.sbuf("qp", [128, 128], BF16)
            STm = k.sbuf("STm", [128, 128], BF16)
            kw = k.sbuf("kw", [128, 128], BF16)
            dd = k.sbuf("dd", [128, 128], F32)
            St = k.sbuf("St", [128, 4, 129], F32)
            Cbf = k.sbuf("Cbf", [128, 4, 128], BF16)
            nbc = k.sbuf("nbc", [128, 4, 128], BF16)
            carry = k.sbuf("carry", [128, 8, 2], F32)
            rawA = k.sbuf("rawA", [128, KC * MT], F32)
            hsum = Buf(rawA.t[:, 0:4 * MT].rearrange("p (h t) -> p h t", h=4), "hsum")
            qa = Buf(rawA.t[:, 4 * MT:6 * MT].bitcast(BF16).rearrange("p (h t) -> p h t", h=4), "qa")
            qb = Buf(rawA.t[:, 6 * MT:8 * MT].bitcast(BF16).rearrange("p (h t) -> p h t", h=4), "qb")
            ybuf = Buf(rawA.t[:, :].rearrange("p (c t) -> p c t", c=KC), "mybuf")
            for bb in (hsum, qa, qb, ybuf):
                bb.h = rawA.h
            yy = k.sbuf("yy", [128, 3, 4, MT], BF16)
            Pt = k.sbuf("Pt", [128, 3, MT], BF16)
            Pm = k.sbuf("Pm", [128, 2, MT], BF16)
            rden = k.sbuf("rden", [128, 2, MT], F32)
            sig = k.sbuf("sig", [128, 2, MT], F32)
            gz = k.sbuf("gz", [128, 2, MT], F32)
            msum = k.sbuf("msum", [128, KC, MT], BF16)
            wq_p = WPool(k, "wq", [128, KC, 128], 4)
            wv_p = WPool(k, "wv", [128, KC, 256], 2)
            wb_p = WPool(k, "wb", [128, 4, 128], 3)
            wo_p = WPool(k, "wo", [128, KC, 128], 2)
            k.memset(k.dve, Vm[:, :, :, 128:129], 1.0)

            tiles = [(0, CTX, True)] + [(CTX + i * MT, MT, False) for i in range(T // MT)]

            def seqb(isc):
                return (0, CTX) if isc else (CTX, NT)

            def norm_tile(t0, n, isc, lo, hi):
                cl = CLS_CTX if isc else b
                m = hi - lo
                o = lo - (t0 - 2)
                norm_stats(lambda c: Rv(lo, m)[:, c, lo:hi], m, PS[0], sq, rstd[:, :m])
                for c in range(KC):
                    tb = tmp[:, c % 2, :m]
                    k.stt(tb, Rv(lo, m)[:, c, lo:hi], tabA[:, l, 1, cl, c:c + 1], rstd[:, :m], ALU.mult, ALU.mult)
                    k.actf(hnb[:, c, o:o + m], tb, AF.Identity, bias=tabB[:, l, 1, cl, c:c + 1])

            def rope_tile(t0, n, isc):
                if isc:
                    k.memset(k.dve, ropeC[:, :n], 1.0)
                    k.memset(k.dve, ropeS[:, :n], 0.0)
                    return
                r0 = (t0 - CTX) // 64
                nr = n // 64
                for tab, oi in ((ropeC, 0), (ropeS, 2)):
                    rowv = ropec[:, oi, r0:r0 + nr]
                    colv = ropec[:, oi + 1, :]
                    a0 = V(rowv.ap.unsqueeze(2).broadcast_to([128, nr, 64]), rowv.hs)
                    a1 = V(colv.ap.unsqueeze(1).broadcast_to([128, nr, 64]), colv.hs)
                    outv = tab[:, :n]
                    k.tt(k.dve, V(outv.ap.rearrange("p (r c) -> p r c", c=64), outv.hs), a0, a1, ALU.add)

            def proj_fm(wt, wsl, o, n, ps):
                for c in range(KC):
                    k.mm(ps[:, :n], wt[:, c, wsl], hnb[:, c, o:o + n], start=(c == 0), stop=(c == KC - 1))

            def qk_post(ps, n, gcol, dst):
                xn = tmp[:, 0, :n]
                if gcol is not None:
                    k.actf(sq[:, 0, :n], ps[:, :n], AF.Square)
                    k.mm(PS[1][:, :n], cbf[:, 1, :], sq[:, 0, :n])
                    k.actf(rstd[:, :n], PS[1][:, :n], AF.Sqrt, bias=epsb[:, 0:1], scale=1.0 / 64)
                    k.recip(rstd[:, :n], rstd[:, :n])
                    k.stt(xn, ps[:, :n], qkg[:, l, gcol:gcol + 1], rstd[:, :n], ALU.mult, ALU.mult)
                else:
                    k.copy(k.act, xn, ps[:, :n])
                k.copy(k.act, sq[:, 1, :n], xn)
                k.mm(PS[1][:, :n], cbf[:, 2, :], sq[:, 1, :n])
                t2 = tmp[:, 1, :n]
                k.tt(k.dve, t2, PS[1][:, :n], ropeS[:, :n], ALU.mult)
                k.tt(k.dve, xn, xn, ropeC[:, :n], ALU.mult)
                k.tt(k.dve, dst, xn, t2, ALU.add)

            def mlstm_tile(t0, n, isc, dirn, use_carry):
                s0, s1 = seqb(isc)
                lo = max(t0 - 2, s0)
                hi = t0 + n if use_carry else min(t0 + n + 2, s1)
                nblk = n // 128
                for ch in range(8):
                    col = (O_QM if ch < 4 else O_KM) + (ch % 4) * 128
                    wt = wtile(wq_p, l, [(col, 128)], 128)
                    stg = stage[:, ch % 2, :]
                    ps = PS[2 + ch % 2]
                    m = hi - lo
                    o = lo - (t0 - 2)
                    proj_fm(wt, slice(0, 128), o, m, ps)
                    if o > 0:
                        k.memset(k.dve, stage[:, ch % 2, 0:o], 0.0)
                    k.copy(k.act, stage[:, ch % 2, o:o + m], ps[:, :m])
                    if use_carry:
                        if t0 + n >= s1:
                            k.memset(k.dve, stage[:, ch % 2, n + 2:n + 4], 0.0)
                        else:
                            k.copy(k.dve, stage[:, ch % 2, n + 2:n + 4], carry[:, ch, :])
                    elif o + m < n + 4:
                        k.memset(k.dve, stage[:, ch % 2, o + m:n + 4], 0.0)
                    acc = tmp[:, ch % 2, :n]
                    k.ts(k.dve, acc, stage[:, ch % 2, 0:n], convw[:, l, ch, 0:1], convb[:, l, ch:ch + 1], ALU.mult, ALU.add)
                    for j in range(1, 5):
                        k.stt(acc, stage[:, ch % 2, j:j + n], convw[:, l, ch, j:j + 1], acc, ALU.mult, ALU.add)
                    if use_carry:
                        k.copy(k.dve, carry[:, ch, :], stage[:, ch % 2, 2:4])
                    k.actf(qk[:, ch, :n], acc, AF.Silu)
                for half in range(2):
                    wt = wtile(wv_p, l, [(O_VM + half * 256, 256)], 256)
                    for bi in range(nblk):
                        o = 2 + bi * 128
                        for c in range(KC):
                            k.mm(PS[2][:, :256], hnb[:, c, o:o + 128], wt[:, c, :], start=(c == 0), stop=(c == KC - 1))
                        pv = PS[2][:, :256]
                        k.copy(k.act, Vm[:, bi, 2 * half:2 * half + 2, 0:128],
                               V(pv.ap.rearrange("p (h d) -> p h d", h=2), pv.hs))
                wg = wtile(wq_p, l, [(O_GM, 16)], 128)
                for bi in range(nblk):
                    o = 2 + bi * 128
                    for c in range(KC):
                        k.mm(PS[3][:, :16], hnb[:, c, o:o + 128], wg[:, c, 0:16], start=(c == 0), stop=(c == KC - 1))
                    k.tt(k.dve, gsbuf[:, bi, :], PS[3][:, :16], gateb[:, l, :], ALU.add)
                ic = 0 if dirn > 0 else 8
                fc = ic + 4
                k.actf(lfb[:, :nblk, :], gsbuf[:, :nblk, fc:fc + 4], AF.Exp, scale=-1.0)
                k.actf(lfb[:, :nblk, :], lfb[:, :nblk, :], AF.Ln, bias=1.0)
                k.ts(k.dve, lfb[:, :nblk, :], lfb[:, :nblk, :], -1.0, None, ALU.mult)
                blks = range(nblk) if dirn > 0 else range(nblk - 1, -1, -1)
                corder = (0, 1) if dirn > 0 else (1, 0)
                for bi in blks:
                    c0 = t0 + bi * 128
                    bo = bi * 128
                    k.mm(PS[4][:, 0:4], TRI[dirn](), lfb[:, bi, :])
                    k.mm(PS[4][:, 8:12], cf[:, 4, :], lfb[:, bi, :])
                    k.tt(k.dve, sm[:, 0:4], gsbuf[:, bi, ic:ic + 4], PS[4][:, 0:4], ALU.subtract)
                    k.tt(k.dve, sm[:, 4:8], sm[:, 0:4], PS[4][:, 8:12], ALU.add)
                    k.actf(sm[:, 4:8], sm[:, 4:8], AF.Exp)
                    k.ts(k.dve, sm[:, 8:12], sm[:, 0:4], lnsc[:, 0:1], None, ALU.add)
                    for h in range(4):
                        k.ts(k.dve, lfrep[:, :], cf[:, 5, :], lfb[:, bi, h:h + 1], None, ALU.mult)
                        k.mm(PS[5][:, 0:128], lfrep[:, :], TRI[dirn]())
                        k.mm(PS[5][:, 128:130], lfrep[:, :], cf[:, 6, 0:2])
                        k.actf(sm[:, 12:14], PS[5][:, 128:130], AF.Exp)
                        k.tt(k.dve, dlog[:, :], PS[5][:, 0:128], MSK[dirn](), ALU.add)
                        k.actf(DT[:, :], dlog[:, :], AF.Exp, bias=sm[:, 8 + h:9 + h])
                        k.actf(ebt[:, :], PS[5][:, 0:128], AF.Exp, bias=lnsc[:, 0:1])
                        k.tt(k.dve, qp[:, :], qk[:, h, bo:bo + 128], ebt[:, :], ALU.mult)
                        k.mm(PS[6][:, 0:128], qk[:, 4 + h, bo:bo + 128], qk[:, h, bo:bo + 128])
                        k.tt(k.dve, STm[:, :], PS[6][:, 0:128], DT[:, :], ALU.mult)
                        k.transpose(PSB[:, 0:128], qk[:, 4 + h, bo:bo + 128], cbf[:, 3, :])
                        k.ts(k.dve, kw[:, :], PSB[:, 0:128], sm[:, 4 + h:5 + h], None, ALU.mult)
                        k.mm(PS[2][:, 0:128], Vm[:, bi, h, 0:128], STm[:, :], start=True, stop=False)
                        k.mm(PS[3][:, 0:128], cbf[:, 0, :], STm[:, :], start=True, stop=False)
                        for ci, cc in enumerate(corder):
                            cs = slice(cc * 64, cc * 64 + 64)
                            last = ci == 1
                            k.mm(PS[2][:, cs], Cbf[:, h, :], qp[:, cs], start=False, stop=last)
                            k.mm(PS[3][:, cs], nbc[:, h, :], qp[:, cs], start=False, stop=last)
                            k.mm(PS[6][:, 256:385], kw[cs, :], Vm[cs, bi, h, :])
                            k.stt(St[:, h, :], St[:, h, :], sm[:, 12 + cc:13 + cc], PS[6][:, 256:385], ALU.mult, ALU.add)
                            k.copy(k.act, Cbf[:, h, :], St[:, h, 0:128])
                            k.ts(k.dve, nbc[:, h, :], cf[:, 5, :], St[:, h, 128:129], None, ALU.mult)
                        k.actf(dd[:, :], PS[3][:, 0:128], AF.Abs)
                        k.ts(k.dve, dd[:, :], dd[:, :], 1.0, None, ALU.max)
                        k.recip(dd[:, :], dd[:, :])
                        if dirn > 0:
                            k.tt(k.dve, Hf[:, h, c0:c0 + 128], PS[2][:, 0:128], dd[:, :], ALU.mult)
                        else:
                            k.tt(k.dve, dd[:, :], PS[2][:, 0:128], dd[:, :], ALU.mult)
                            k.tt(k.dve, hsum[:, h, bo:bo + 128], dd[:, :], Hf[:, h, c0:c0 + 128], ALU.add)

            def attention(t0, n, isc, mixer):
                qsrc = qa if mixer == 0 else qb
                nct = CTX // 128
                for h in range(8):
                    kv, base, c = h // 4, 64 * (h % 2), h // 2
                    pr = slice(base, base + 64)
                    pn, pd = PS[2 + (h % 2) * 2], PS[3 + (h % 2) * 2]
                    kts = [(j, 0, n, None) for j in range(nct)]
                    if not isc:
                        ql0 = t0 - CTX
                        if mixer == 1:
                            kts += [(nct + j, 0, n, None) for j in range(T // 128)]
                        else:
                            for j in range(ql0 // 128 - 1, (ql0 + n) // 128 + 1):
                                if j < 0 or j >= T // 128:
                                    continue
                                a = max(128 * (j - 1), ql0)
                                e = min(128 * (j + 2), ql0 + n)
                                kts.append((nct + j, a - ql0, e - ql0, a - (128 * j - 128)))
                    for ki, (kt, a, e, mo) in enumerate(kts):
                        m = e - a
                        pss = PS[0 + ki % 2]
                        k.mm(pss[:, :m], Kst[pr, mixer * 2 + kv, kt * 128:(kt + 1) * 128], qsrc[pr, c, a:e])
                        pt = Pt[:, ki % 3, :m]
                        k.actf(pt, pss[:, :m], AF.Exp, scale=0.125)
                        if mo is not None:
                            pm = Pm[:, ki % 2, :m]
                            k.tt(k.dve, pm, pt, winmask[:, mo:mo + m], ALU.mult)
                            pt = pm
                        lastk = ki == len(kts) - 1
                        k.mm(pn[pr, a:e], Vst[:, mixer, kt, kv * 64:(kv + 1) * 64], pt, start=(ki == 0), stop=lastk)
                        k.mm(pd[pr, a:e], cbf[:, 0, 0:64], pt, start=(ki == 0), stop=lastk)
                    rd = rden[pr, h % 2, :n]
                    if mixer == 0:
                        k.ts(k.dve, rd, pd[pr, :n], esink[pr, l, h:h + 1], None, ALU.add)
                        k.recip(rd, rd)
                    else:
                        k.recip(rd, pd[pr, :n])
                    k.tt(k.dve, yy[pr, mixer, c, :n], pn[pr, :n], rd, ALU.mult)

            k.memset(k.dve, St[:, :, :], 0.0)
            k.memset(k.dve, Cbf[:, :, :], 0.0)
            k.memset(k.dve, nbc[:, :, :], 0.0)
            for (t0, n, isc) in tiles:
                s0, s1 = seqb(isc)
                lo, hi = max(t0 - 2, s0), min(t0 + n + 2, s1)
                norm_tile(t0, n, isc, lo, hi)
                rope_tile(t0, n, isc)
                for mixer in range(2):
                    for kv in range(2):
                        col = (O_KA if mixer == 0 else O_KB) + kv * 64
                        wt = wtile(wq_p, l, [(col, 64), (col, 64)], 128)
                        ps = PS[2 + (mixer * 2 + kv) % 2]
                        proj_fm(wt, slice(0, 128), 2, n, ps)
                        qk_post(ps, n, (1 if mixer == 1 else None), Kst[:, mixer * 2 + kv, t0:t0 + n])
                wt = wtile(wv_p, l, [(O_VA, 128), (O_VB, 128)], 256)
                for bi in range(n // 128):
                    o = 2 + bi * 128
                    for c in range(KC):
                        k.mm(PS[4][:, :256], hnb[:, c, o:o + 128], wt[:, c, 0:256], start=(c == 0), stop=(c == KC - 1))
                    pv = PS[4][:, :256]
                    kt = (t0 + bi * 128) // 128
                    k.copy(k.act, Vst[:, :, kt, :], V(pv.ap.rearrange("p (m d) -> p m d", m=2), pv.hs))
                mlstm_tile(t0, n, isc, 1, False)

            k.memset(k.dve, St[:, :, :], 0.0)
            k.memset(k.dve, Cbf[:, :, :], 0.0)
            k.memset(k.dve, nbc[:, :, :], 0.0)
            tiles2 = [tiles[0]] + tiles[:0:-1]
            for (t0, n, isc) in tiles2:
                cl = CLS_CTX if isc else b
                s0, s1 = seqb(isc)
                lo = max(t0 - 2, s0)
                norm_tile(t0, n, isc, lo, t0 + n)
                rope_tile(t0, n, isc)
                mlstm_tile(t0, n, isc, -1, True)
                for h in range(4):
                    k.actf(sq[:, h % 2, :n], hsum[:, h, :n], AF.Square)
                    k.mm(PS[0][:, :n], cbf[:, 0, :], sq[:, h % 2, :n])
                    k.actf(rstd[:, :n], PS[0][:, :n], AF.Sqrt, bias=epsb[:, 0:1], scale=1.0 / 128)
                    k.recip(rstd[:, :n], rstd[:, :n])
                    wt = wtile(wq_p, l, [(O_OM + h * 128, 128)], 128)
                    proj_fm(wt, slice(0, 128), 2, n, PS[1])
                    k.actf(sig[:, h % 2, :n], PS[1][:, :n], AF.Sigmoid)
                    tb = tmp[:, h % 2, :n]
                    k.stt(tb, hsum[:, h, :n], mgs[:, l, h:h + 1], rstd[:, :n], ALU.mult, ALU.mult)
                    k.tt(k.dve, yy[:, 2, h, :n], tb, sig[:, h % 2, :n], ALU.mult)
                for mixer in range(2):
                    for c in range(4):
                        col = (O_QA if mixer == 0 else O_QB) + c * 128
                        wt = wtile(wq_p, l, [(col, 128)], 128)
                        ps = PS[2 + c % 2]
                        proj_fm(wt, slice(0, 128), 2, n, ps)
                        qk_post(ps, n, (0 if mixer == 1 else None), (qa if mixer == 0 else qb)[:, c, :n])
                attention(t0, n, isc, 0)
                attention(t0, n, isc, 1)
                for i in range(KC):
                    for br in range(3):
                        def ldb(s_, dma, br=br, i=i):
                            src = w_branch[l, br, :, i * 128:(i + 1) * 128]
                            dma(s_[:, :, :], V(src.ap.rearrange("(c p) n -> p c n", p=128), src.hs))
                        wb = wb_p.load(ldb)
                        pz = PS[2 + br % 2]
                        for c in range(4):
                            k.mm(pz[:, :n], wb[:, c, :], yy[:, br, c, :n], start=(c == 0), stop=(c == 3))
                        wt = wtile(wq_p, l, [(O_GL + br * D + i * 128, 128)], 128)
                        pg = PS[4 + br % 2]
                        proj_fm(wt, slice(0, 128), 2, n, pg)
                        sg_ = sig[:, br % 2, :n]
                        k.actf(sg_, pg[:, :n], AF.Sigmoid)
                        if br == 0:
                            k.tt(k.dve, gz[:, 0, :n], sg_, pz[:, :n], ALU.mult)
                        else:
                            k.tt(k.dve, gz[:, 1, :n], sg_, pz[:, :n], ALU.mult)
                            dst = msum[:, i, :n] if br == 2 else gz[:, 0, :n]
                            k.tt(k.dve, dst, gz[:, 0, :n], gz[:, 1, :n], ALU.add)
                for o_ in range(KC):
                    def ldo(s_, dma, o_=o_):
                        src = w_out[l, :, o_ * 128:(o_ + 1) * 128]
                        dma(s_[:, :, :], V(src.ap.rearrange("(c p) n -> p c n", p=128), src.hs))
                    wo = wo_p.load(ldo)
                    py = PS[2 + o_ % 2]
                    for i in range(KC):
                        k.mm(py[:, :n], wo[:, i, :], msum[:, i, :n], start=(i == 0), stop=(i == KC - 1))
                    k.copy(k.act, ybuf[:, o_, :n], py[:, :n])
                norm_stats(lambda c: ybuf[:, c, :n], n, PS[0], sq, rstd[:, :n])
                for c in range(KC):
                    tb = tmp[:, c % 2, :n]
                    k.stt(tb, ybuf[:, c, :n], tabG[:, l, 1, cl, c:c + 1], rstd[:, :n], ALU.mult, ALU.mult)
                    k.tt(k.dve, Rv(t0, n)[:, c, t0:t0 + n], Rv(t0, n)[:, c, t0:t0 + n], tb, ALU.add)
            k.barrier()
        k.es = es

    for b in range(NB):
        k.dma(k.sp, R.rs(range(CTX // 256))[:, :, 0:CTX], ctxT[b, :, :, :], sem_in)
        k.dma(k.sp, R.rs(range(CTX // 256, NRB))[:, :, CTX:NT], xT[b, :, :, :], sem_in)
        for l in range(L):
            ffn_phase(l, 0, b)
            if cfg.stop_after == "ffn1":
                break
            mixer_phase(l, b)
            if cfg.stop_after == "mixer":
                break
            ffn_phase(l, 2, b)
        k.dma(k.sp, outT[b, :, :, :], R.rs(range(CTX // 256, NRB))[:, :, CTX:NT], sem_out)
    k.final_wait(k.sp, [sem_out])
    return nc, es, k


def make_consts():
    c = np.zeros((128, 8, 128), np.float32)
    c[:, 0, :] = 1.0
    p = np.arange(128)
    c[:, 1, :] = (p[:, None] // 64 == p[None, :] // 64)
    c[:, 2, :] = (p[:, None] == (p[None, :] ^ 16))
    c[:, 3, :] = np.eye(128)
    return c


def make_consts_f():
    c = np.zeros((128, 8, 128), np.float32)
    p = np.arange(128)
    same = (p[:, None] // 64 == p[None, :] // 64)
    c[:, 0, :] = same & (p[:, None] <= p[None, :])
    c[:, 1, :] = same & (p[:, None] >= p[None, :])
    c[:, 2, :] = np.where(c[:, 0, :] > 0, 0.0, NEG)
    c[:, 3, :] = np.where(c[:, 1, :] > 0, 0.0, NEG)
    c[:, 4, :] = same
    c[:, 5, :] = 1.0
    c[:, 6, 0] = p < 64
    c[:, 6, 1] = p >= 64
    return c


def make_winmask():
    s_ = np.arange(128)[:, None]
    q = np.arange(384)[None, :]
    return (np.abs(q - 128 - s_) <= 128).astype(np.float32)


def make_rope(T):
    rows = T // GRID_W
    freqs = (np.float32(10000.0) ** (-np.arange(16, dtype=np.float32) / np.float32(16))).astype(np.float32)
    r = np.zeros((128, 4, 64), np.float32)
    for p in range(128):
        d = p % 64
        axis, half, pair = d // 32, (d % 32) // 16, d % 16
        sign = -1.0 if half == 0 else 1.0
        if axis == 0:
            ang = (np.arange(rows, dtype=np.float32) * freqs[pair]).astype(np.float32)
            r[p, 0, :rows] = np.cos(ang)
            r[p, 2, :rows] = sign * np.sin(ang)
        else:
            ang = (np.arange(64, dtype=np.float32) * freqs[pair]).astype(np.float32)
            r[p, 1, :] = np.cos(ang)
            r[p, 3, :] = sign * np.sin(ang)
    return r


def to_bf16(a):
    import ml_dtypes
    return a.astype(ml_dtypes.bfloat16)


def fm(a):
    a = np.swapaxes(a, -1, -2)
    sh = a.shape
    a = a.reshape(sh[:-2] + (KC, 128, sh[-1]))
    return np.ascontiguousarray(np.swapaxes(a, -3, -2))


def vec_t(a, nchunk):
    sh = a.shape
    a = a.reshape(sh[:-1] + (nchunk, 128))
    return np.ascontiguousarray(np.moveaxis(a, -1, 0))


def host_inputs(cfg, inp, core):
    NB = cfg.NB
    bs = slice(core * NB, (core + 1) * NB)
    L = cfg.L
    m = {}
    m["xT"] = fm(inp["x"][bs])
    m["ctxT"] = fm(inp["ctx"][bs])
    cc = np.concatenate([inp["c"][bs], inp["c_ctx"][None, :]], axis=0)
    m["c_in"] = np.ascontiguousarray(cc.T.reshape(KC, 128, NB + 1).transpose(1, 0, 2))
    m["w_ada"] = inp["w_ada"][:L]
    m["b_ada_t"] = vec_t(inp["b_ada"][:L], 72)
    m["norm_g_t"] = vec_t(inp["norm_g"][:L], KC)
    m["ffn_w_gate"] = inp["ffn_w_gate"][:L]
    m["ffn_w_up"] = inp["ffn_w_up"][:L]
    m["ffn_w_down"] = inp["ffn_w_down"][:L]
    m["w_in"] = inp["w_in"][:L]
    m["w_branch"] = inp["w_branch"][:L]
    m["w_out"] = inp["w_out"][:L]
    m["consts_bf"] = make_consts()
    m["consts_f"] = make_consts_f()
    m["winmask"] = make_winmask()
    m["rope"] = make_rope(cfg.T)
    cw = inp["conv_w"][:L]
    m["convw_t"] = np.ascontiguousarray(cw.reshape(L, 5, 8, 128).transpose(3, 0, 2, 1))
    m["convb_t"] = vec_t(inp["conv_b"][:L], 8)
    m["mg_t"] = vec_t(inp["mlstm_norm_g"][:L], 4)
    qg = inp["qk_norm_g"][:L]
    m["qkg_t"] = np.ascontiguousarray(np.tile(qg, (1, 1, 2)).transpose(2, 0, 1))
    m["sink_t"] = np.ascontiguousarray(np.broadcast_to(inp["attn_sink"][:L][None], (128, L, 8)))
    m["gateb_t"] = np.ascontiguousarray(np.broadcast_to(inp["mlstm_gate_b"][:L][None], (128, L, 16)))
    return m


_CACHE = {}


def run(cfg, inputs):
    key = (cfg.T, cfg.CTX, cfg.L, cfg.NB, cfg.n_cores, cfg.stop_after)
    if key not in _CACHE:
        _CACHE[key] = build_program(cfg)
    nc, es, kk = _CACHE[key]
    inp = {n: np.asarray(v) for n, v in inputs.items()}
    in_maps = [host_inputs(cfg, inp, core) for core in range(cfg.n_cores)]
    res = run_bass_kernel_spmd(nc, in_maps, core_ids=list(range(cfg.n_cores)))
    outs = []
    for core in range(cfg.n_cores):
        o = res.results[core]["outT"]
        o = np.swapaxes(o, 1, 2).reshape(cfg.NB, D, cfg.T)
        outs.append(np.swapaxes(o, 1, 2))
    return np.ascontiguousarray(np.concatenate(outs, axis=0)).astype(np.float32)


def kernel(**inputs):
    cfg = Cfg()
    return run(cfg, inputs)
```

```python
import contextlib
import numpy as np
import concourse.bass as bass
import concourse.mybir as mybir
from concourse.bass_utils import run_bass_kernel_spmd

F32 = mybir.dt.float32
BF16 = mybir.dt.bfloat16
AF = mybir.ActivationFunctionType
ALU = mybir.AluOpType

D = 1024
KC = 8
FF = 2816
FC = 22
GRID_W = 64
HD = 64
IN_W = 6672
O_QA, O_KA, O_VA, O_QB, O_KB, O_VB = 0, 512, 640, 768, 1280, 1408
O_QM, O_KM, O_VM, O_OM, O_GM, O_GL = 1536, 2048, 2560, 3072, 3584, 3600
EPS = 1e-6
NEG = -30000.0


class Cfg:
    def __init__(self, T=2048, CTX=256, L=4, NB=2, n_cores=8, stop_after=None):
        self.T, self.CTX, self.L, self.NB, self.n_cores = T, CTX, L, NB, n_cores
        self.NT = T + CTX
        self.stop_after = stop_after


class Sem:
    def __init__(self, h):
        self.h = h
        self.count = 0


class H:
    __slots__ = ("name", "w", "r", "excl")

    def __init__(self, name=""):
        self.name = name
        self.excl = False
        self.w = None
        self.r = {}


class V:
    __slots__ = ("ap", "hs")

    def __init__(self, ap, hs):
        self.ap = ap
        self.hs = hs


class Buf:
    def __init__(self, t, name, slots=None):
        self.t = t
        self.name = name
        self.h = H(name)
        self.regs = {}
        self.slots = slots

    def _slot(self, j):
        key = ("slot", j)
        if key not in self.regs:
            self.regs[key] = H(f"{self.name}[{j}]")
        return self.regs[key]

    def __getitem__(self, idx):
        if self.slots:
            i1 = idx[1] if isinstance(idx, tuple) and len(idx) > 1 else slice(None)
            if isinstance(i1, int):
                hs = [self._slot(i1)]
            else:
                hs = [self._slot(j) for j in range(*i1.indices(self.slots))]
            return V(self.t[idx], hs)
        return V(self.t[idx], [self.h])

    def r(self, key):
        if key not in self.regs:
            self.regs[key] = H(f"{self.name}.{key}")
        return _Reg(self, [self.regs[key]])

    def rs(self, keys):
        hs = []
        for key in keys:
            if key not in self.regs:
                self.regs[key] = H(f"{self.name}.{key}")
            hs.append(self.regs[key])
        return _Reg(self, hs)


class _Reg:
    def __init__(self, buf, hs):
        self.buf, self.hs = buf, hs

    def __getitem__(self, idx):
        return V(self.buf.t[idx], self.hs)


class Eng:
    def __init__(self, name, raw, sem, self_sync):
        self.name, self.raw, self.sem = name, raw, sem
        self.seen = {}
        self.self_sync = self_sync


class K:
    def __init__(self, nc, es):
        self.nc, self.es, self.es0 = nc, es, es
        self.n_inst = 0
        self.sem_by_name = {}
        mk = lambda n: Sem(es.enter_context(nc.semaphore(n)))
        self.pe = Eng("pe", nc.tensor, mk("s_pe"), False)
        self.act = Eng("act", nc.scalar, mk("s_act"), True)
        self.dve = Eng("dve", nc.vector, mk("s_dve"), True)
        self.pool = Eng("pool", nc.gpsimd, mk("s_pool"), True)
        self.sp = Eng("sp", nc.sync, mk("s_sp"), True)
        self.engs = [self.pe, self.act, self.dve, self.pool, self.sp]
        self.dma_sems = []

    def sbuf(self, name, shape, dt, slotted=False):
        self.n_buf = getattr(self, "n_buf", 0) + 1
        name = f"{name}_{self.n_buf}"
        return Buf(self.es.enter_context(self.nc.sbuf_tensor(name, list(shape), dt)), name,
                   slots=(shape[1] if slotted else None))

    def psum(self, name, shape, dt):
        b = Buf(self.es.enter_context(self.nc.psum_tensor(name, list(shape), dt)), name)
        b.h.excl = True
        return b

    def new_dma_sem(self, name):
        if name not in self.sem_by_name:
            s = Sem(self.es0.enter_context(self.nc.semaphore(name)))
            self.dma_sems.append(s)
            self.sem_by_name[name] = s
        return self.sem_by_name[name]

    def _wait(self, eng, deps):
        for sem, val in deps.items():
            if sem is eng.sem and not eng.self_sync:
                continue
            if eng.seen.get(sem, 0) < val:
                eng.raw.wait_ge(sem.h, val)
                eng.seen[sem] = val

    @staticmethod
    def _deps(reads, writes, own=None):
        deps = {}

        def add(ev):
            if ev is not None:
                s, v = ev
                if deps.get(s, 0) < v:
                    deps[s] = v
        for h in reads:
            add(h.w)
            if h.excl:
                for s, v in h.r.items():
                    if s is not own:
                        add((s, v))
        for h in writes:
            add(h.w)
            for s, v in h.r.items():
                add((s, v))
        return deps

    def emit(self, eng, reads, writes, fn):
        deps = self._deps(reads, writes, eng.sem)
        self._wait(eng, deps)
        inst = fn()
        inst.then_inc(eng.sem.h, 1)
        eng.sem.count += 1
        ev = (eng.sem, eng.sem.count)
        for h in reads:
            if h.r.get(ev[0], 0) < ev[1]:
                h.r[ev[0]] = ev[1]
        for h in writes:
            h.w = ev
            h.r = {}
        self.n_inst += 1
        return inst

    def dma(self, q, out, in_, sem, **kw):
        reads, writes = in_.hs, out.hs
        deps = self._deps(reads, writes)
        if sem.count:
            deps[sem] = max(deps.get(sem, 0), sem.count)
        self._wait(q, deps)
        inst = q.raw.dma_start(out=out.ap, in_=in_.ap, **kw)
        inst.then_inc(sem.h, 16)
        sem.count += 16
        ev = (sem, sem.count)
        for h in reads:
            if h.r.get(ev[0], 0) < ev[1]:
                h.r[ev[0]] = ev[1]
        for h in writes:
            h.w = ev
            h.r = {}
        self.n_inst += 1

    def barrier(self):
        deps = {}
        for e in self.engs:
            if e.sem.count:
                deps[e.sem] = e.sem.count
        for s in self.dma_sems:
            if s.count:
                deps[s] = s.count
        for e in self.engs:
            d = {s: v for s, v in deps.items() if s is not e.sem}
            self._wait(e, d)

    def final_wait(self, eng, sems):
        for s in sems:
            eng.raw.wait_ge(s.h, s.count)

    @staticmethod
    def _hs(*vs):
        out = []
        for v in vs:
            if isinstance(v, V):
                out += v.hs
        return out

    @staticmethod
    def _a(v):
        return v.ap if isinstance(v, V) else v

    def mm(self, out, lhsT, rhs, start=True, stop=True, **kw):
        return self.emit(self.pe, lhsT.hs + rhs.hs, out.hs,
                         lambda: self.nc.tensor.matmul(out.ap, lhsT.ap, rhs.ap, start=start, stop=stop, **kw))

    def transpose(self, out, in_, ident):
        return self.emit(self.pe, in_.hs + ident.hs, out.hs,
                         lambda: self.nc.tensor.transpose(out.ap, in_.ap, ident.ap))

    def actf(self, out, in_, func, bias=None, scale=None):
        kw = {}
        if bias is not None:
            kw["bias"] = self._a(bias)
        if scale is not None:
            kw["scale"] = self._a(scale)
        return self.emit(self.act, self._hs(in_, bias, scale), out.hs,
                         lambda: self.nc.scalar.activation(out=out.ap, in_=in_.ap, func=func, **kw))

    def tt(self, eng, out, in0, in1, op):
        return self.emit(eng, in0.hs + in1.hs, out.hs,
                         lambda: eng.raw.tensor_tensor(out=out.ap, in0=in0.ap, in1=in1.ap, op=op))

    def ts(self, eng, out, in0, s1, s2, op0, op1=None):
        kw = {}
        if op1 is not None:
            kw["op1"] = op1
        return self.emit(eng, self._hs(in0, s1, s2), out.hs,
                         lambda: eng.raw.tensor_scalar(out=out.ap, in0=in0.ap, scalar1=self._a(s1),
                                                       scalar2=self._a(s2), op0=op0, **kw))

    def stt(self, out, in0, scalar, in1, op0, op1):
        return self.emit(self.dve, self._hs(in0, scalar, in1), out.hs,
                         lambda: self.nc.vector.scalar_tensor_tensor(out=out.ap, in0=in0.ap, scalar=self._a(scalar),
                                                                     in1=in1.ap, op0=op0, op1=op1))

    def copy(self, eng, out, in_):
        if eng is self.act:
            return self.emit(eng, in_.hs, out.hs, lambda: self.nc.scalar.copy(out=out.ap, in_=in_.ap))
        return self.emit(eng, in_.hs, out.hs, lambda: eng.raw.tensor_copy(out=out.ap, in_=in_.ap))

    def recip(self, out, in_):
        return self.emit(self.dve, in_.hs, out.hs, lambda: self.nc.vector.reciprocal(out=out.ap, in_=in_.ap))

    def memset(self, eng, out, val):
        return self.emit(eng, [], out.hs, lambda: eng.raw.memset(out.ap, val))


class WPool:
    def __init__(self, k, name, shape, n, dt=BF16, q=None):
        self.k = k
        self.slots = [k.sbuf(f"{name}{i}", shape, dt) for i in range(n)]
        self.sems = [k.new_dma_sem(f"ws_{name}{i}") for i in range(n)]
        self.i = 0
        self.q = q

    def load(self, fn):
        s, sem = self.slots[self.i], self.sems[self.i]
        self.i = (self.i + 1) % len(self.slots)
        fn(s, lambda out, in_, **kw: self.k.dma(self.q or self.k.sp, out, in_, sem, **kw))
        return s


def col_groups(a, b, CTX, maxn=512):
    out = []
    c = a
    while c < b:
        lim = CTX if c < CTX else b
        n = min(maxn, lim - c, b - c)
        out.append((c, n, c < CTX))
        c += n
    return out


def build_program(cfg):
    nc = bass.Bass("TRN2", target_bir_lowering=False)
    es = contextlib.ExitStack()
    k = K(nc, es)
    T, CTX, L, NB, NT = cfg.T, cfg.CTX, cfg.L, cfg.NB, cfg.NT
    NCLS = NB + 1
    CLS_CTX = NB

    def din(name, shape, dt=F32):
        t = nc.dram_tensor(name, list(shape), dt, kind="ExternalInput")
        return Buf(t.ap(), name)

    xT = din("xT", [NB, 128, KC, T])
    ctxT = din("ctxT", [NB, 128, KC, CTX])
    c_in = din("c_in", [128, KC, NCLS])
    w_ada = din("w_ada", [L, D, 9 * D])
    b_ada_t = din("b_ada_t", [128, L, 72])
    norm_g_t = din("norm_g_t", [128, L, 6, KC])
    w_gate = din("ffn_w_gate", [L, 2, D, FF])
    w_up = din("ffn_w_up", [L, 2, D, FF])
    w_down = din("ffn_w_down", [L, 2, FF, D])
    w_in = din("w_in", [L, D, IN_W])
    w_branch = din("w_branch", [L, 3, 512, D])
    w_out = din("w_out", [L, D, D])
    consts_bf = din("consts_bf", [128, 8, 128])
    consts_f = din("consts_f", [128, 8, 128])
    winmask_d = din("winmask", [128, 384])
    rope_d = din("rope", [128, 4, 64])
    convw_t = din("convw_t", [128, L, 8, 5])
    convb_t = din("convb_t", [128, L, 8])
    mg_t = din("mg_t", [128, L, 4])
    qkg_t = din("qkg_t", [128, L, 2])
    sink_t = din("sink_t", [128, L, 8])
    gateb_t = din("gateb_t", [128, L, 16])
    outT_t = nc.dram_tensor("outT", [NB, 128, KC, T], F32, kind="ExternalOutput")
    outT = Buf(outT_t.ap(), "outT")

    def dint(name, shape):
        t = nc.dram_tensor(name, list(shape), BF16, kind="Internal")
        return Buf(t.ap(), name)

    UNITS = [[(O_KA, 128)], [(O_KB, 128)], [(O_VA, 128)], [(O_VB, 128)]]
    U_KA, U_KB, U_VAB = 0, 1, 2
    U_QA = len(UNITS)
    UNITS += [[(O_QA + 64 * c, 64), (O_QA + 64 * (c + 4), 64)] for c in range(4)]
    U_QB = len(UNITS)
    UNITS += [[(O_QB + 64 * c, 64), (O_QB + 64 * (c + 4), 64)] for c in range(4)]
    U_QM = len(UNITS)
    UNITS += [[(O_QM + 128 * h, 128)] for h in range(4)]
    UNITS += [[(O_KM + 128 * h, 128)] for h in range(4)]
    U_VM = len(UNITS)
    UNITS += [[(O_VM + 128 * h, 128)] for h in range(4)]
    U_OM = len(UNITS)
    UNITS += [[(O_OM + 128 * h, 128)] for h in range(4)]
    U_GM = len(UNITS)
    UNITS += [[(O_GM, 16)]]
    U_GL = len(UNITS)
    UNITS += [[(O_GL + 128 * g, 128)] for g in range(24)]
    NU = len(UNITS)
    wgu_d = dint("wgu_d", [L, 2, 2, FC // 2, 128, KC, 256])
    wdn_d = dint("wdn_d", [L, 2, KC, 128, FC, 128])
    wi_d = dint("wi_d", [L, NU, 128, KC, 128])
    wb_d = dint("wb_d", [L, 3, KC, 128, 4, 128])
    wo_d = dint("wo_d", [L, KC, 128, KC, 128])
    cv_sems = [k.new_dma_sem(f"cv{i}") for i in range(4)]
    cv_n = [0]

    def wreg(buf, l, grp):
        return buf.rs([(l, grp, q) for q in range(4)])

    def cv(buf, l, grp, dst_ap, src_v):
        q = cv_n[0] % 4
        cv_n[0] += 1
        k.dma(k.pool, V(dst_ap, buf.r((l, grp, q)).hs), src_v, cv_sems[q])

    def km(v):
        return V(v.ap.rearrange("(c p) n -> p c n", p=128), v.hs)

    def convert_ffn(l, which):
        for j2 in range(FC // 2):
            for g, wsrc in enumerate((w_gate, w_up)):
                cv(wgu_d, l, which, wgu_d.t[l, which, g, j2, :, :, :], km(wsrc[l, which, :, j2 * 256:(j2 + 1) * 256]))
        for i in range(KC):
            cv(wdn_d, l, which, wdn_d.t[l, which, i, :, :, :], km(w_down[l, which, :, i * 128:(i + 1) * 128]))

    def convert_mixer(l):
        for u, rngs in enumerate(UNITS):
            o = 0
            for (c0, n) in rngs:
                cv(wi_d, l, 0, wi_d.t[l, u, :, :, o:o + n], km(w_in[l, :, c0:c0 + n]))
                o += n
        for br in range(3):
            for c in range(4):
                if br < 2:
                    for hf in range(2):
                        r0 = 64 * (c + 4 * hf)
                        src = w_branch[l, br, r0:r0 + 64, :]
                        cv(wb_d, l, 0, wb_d.t[l, br, :, 64 * hf:64 * hf + 64, c, :].rearrange("i p n -> p i n"),
                           V(src.ap.rearrange("p (i n) -> p i n", n=128), src.hs))
                else:
                    src = w_branch[l, br, 128 * c:128 * c + 128, :]
                    cv(wb_d, l, 0, wb_d.t[l, br, :, :, c, :].rearrange("i p n -> p i n"),
                       V(src.ap.rearrange("p (i n) -> p i n", n=128), src.hs))
        for o_ in range(KC):
            cv(wo_d, l, 0, wo_d.t[l, o_, :, :, :], km(w_out[l, :, o_ * 128:(o_ + 1) * 128]))

    R = k.sbuf("R", [128, KC, NT], F32)
    NRB = NT // 256

    def Rv(c0, n):
        keys = list(range(c0 // 256, (c0 + n - 1) // 256 + 1))
        return R.rs(keys)

    gsb = k.sbuf("gsb", [128, L, 6, KC], F32)
    tabA = k.sbuf("tabA", [128, L, 3, NCLS, KC], F32)
    tabB = k.sbuf("tabB", [128, L, 3, NCLS, KC], F32)
    tabG = k.sbuf("tabG", [128, L, 3, NCLS, KC], F32)
    cbf = k.sbuf("cbf", [128, 8, 128], BF16)
    ONES = lambda: cbf[:, 0, :]
    epsb = k.sbuf("epsb", [128, 1], F32)
    cf = k.sbuf("cf", [128, 8, 128], F32)
    TRI = {1: (lambda: cf[:, 0, :]), -1: (lambda: cf[:, 1, :])}
    MSK = {1: (lambda: cf[:, 2, :]), -1: (lambda: cf[:, 3, :])}
    winmask = k.sbuf("winmask_sb", [128, 384], BF16)
    ropec = k.sbuf("ropec", [128, 4, 64], F32)
    convw = k.sbuf("convw", [128, L, 8, 5], F32)
    convb = k.sbuf("convb", [128, L, 8], F32)
    mgs = k.sbuf("mgs", [128, L, 4], F32)
    qkg = k.sbuf("qkg", [128, L, 2], F32)
    esink = k.sbuf("esink", [128, L, 8], F32)
    gateb = k.sbuf("gateb", [128, L, 16], F32)
    lnsc = k.sbuf("lnsc", [128, 1], F32)
    sem_in = k.new_dma_sem("sem_in")
    sem_c = k.new_dma_sem("sem_c")
    sem_out = k.new_dma_sem("sem_out")

    PS = [k.psum(f"ps{i}", [128, 512], F32) for i in range(7)]
    PSB = k.psum("psb", [128, 1024], BF16)

    k.dma(k.pool, cbf[:, :, :], consts_bf[:, :, :], sem_c)
    k.dma(k.sp, gsb[:, :, :, :], norm_g_t[:, :, :, :], sem_c)
    k.memset(k.dve, epsb[:, :], EPS)
    k.memset(k.dve, lnsc[:, :], float(np.log(128.0 ** -0.5)))
    k.dma(k.sp, cf[:, :, :], consts_f[:, :, :], sem_c)
    k.dma(k.pool, winmask[:, :], winmask_d[:, :], sem_c)
    k.dma(k.sp, ropec[:, :, :], rope_d[:, :, :], sem_c)
    k.dma(k.sp, convw[:, :, :, :], convw_t[:, :, :, :], sem_c)
    k.dma(k.sp, convb[:, :, :], convb_t[:, :, :], sem_c)
    k.dma(k.sp, mgs[:, :, :], mg_t[:, :, :], sem_c)
    k.dma(k.sp, qkg[:, :, :], qkg_t[:, :, :], sem_c)
    k.dma(k.sp, esink[:, :, :], sink_t[:, :, :], sem_c)
    k.dma(k.sp, gateb[:, :, :], gateb_t[:, :, :], sem_c)
    k.actf(esink[:, :, :], esink[:, :, :], AF.Exp)
    convert_ffn(0, 0)
    with contextlib.ExitStack() as es2:
        k.es = es2
        modraw = k.sbuf("modraw", [128, L, 72, NCLS], F32)
        csb = k.sbuf("csb", [128, KC, NCLS], F32)
        csig = k.sbuf("csig", [128, KC, NCLS], F32)
        cact = k.sbuf("cact", [128, KC, NCLS], BF16)
        bada = k.sbuf("bada", [128, L, 72], F32)
        wp = WPool(k, "wada", [128, KC, 512], 3, q=k.pool)
        k.dma(k.sp, csb[:, :, :], c_in[:, :, :], sem_c)
        k.dma(k.sp, bada[:, :, :], b_ada_t[:, :, :], sem_c)
        k.actf(csig[:, :, :], csb[:, :, :], AF.Sigmoid)
        k.tt(k.dve, cact[:, :, :], csb[:, :, :], csig[:, :, :], ALU.mult)
        for l in range(L):
            for n4 in range(18):
                src = w_ada[l, :, n4 * 512:(n4 + 1) * 512]

                def ld(s, dma, src=src):
                    dma(s[:, :, :], V(src.ap.rearrange("(c p) n -> p c n", p=128), src.hs))
                wt = wp.load(ld)
                ps = PS[n4 % 4]
                for j in range(4):
                    n = n4 * 4 + j
                    for c in range(KC):
                        k.mm(ps[:, j * 8:j * 8 + NCLS], wt[:, c, j * 128:(j + 1) * 128], cact[:, c, :],
                             start=(c == 0), stop=(c == KC - 1))
                for j in range(4):
                    n = n4 * 4 + j
                    k.ts(k.dve, modraw[:, l, n, :], ps[:, j * 8:j * 8 + NCLS], bada[:, l, n:n + 1], None, ALU.add)
        for l in range(L):
            for kk in range(3):
                wgt = 1.0 if kk == 1 else 0.5
                for cl in range(NCLS):
                    sl = lambda m: modraw[:, l, m * 8:(m + 1) * 8, cl]
                    k.copy(k.dve, tabB[:, l, kk, cl, :], sl(3 * kk))
                    k.stt(tabA[:, l, kk, cl, :], sl(3 * kk + 1), 1.0, gsb[:, l, 2 * kk, :], ALU.add, ALU.mult)
                    k.stt(tabG[:, l, kk, cl, :], sl(3 * kk + 2), wgt, gsb[:, l, 2 * kk + 1, :], ALU.mult, ALU.mult)
        k.barrier()
    k.es = es
    for l_ in range(L):
        if l_ > 0:
            convert_ffn(l_, 0)
        convert_mixer(l_)
        convert_ffn(l_, 1)

    def norm_stats(src_fn, n, ps, sqbuf, rstd_out):
        for c in range(KC):
            k.actf(sqbuf[:, c % 2, :n], src_fn(c), AF.Square)
            k.mm(ps[:, :n], ONES(), sqbuf[:, c % 2, :n], start=(c == 0), stop=(c == KC - 1))
        k.actf(rstd_out, ps[:, :n], AF.Ln, bias=epsb[:, 0:1], scale=1.0 / D)
        k.actf(rstd_out, rstd_out, AF.Exp, scale=-0.5)

    def ffn_phase(l, kk, b):
        which = 0 if kk == 0 else 1
        TT = 768
        with contextlib.ExitStack() as es2:
            k.es = es2
            hid = k.sbuf("hid", [128, FC, TT], BF16, slotted=True)
            ybuf = k.sbuf("ybuf", [128, KC, TT], F32)
            hn = Buf(ybuf.t[:, :, :].rearrange("p c t -> p (c t)").bitcast(BF16)[:, 0:KC * TT]
                     .rearrange("p (c t) -> p c t", c=KC), "hn")
            hn.h = ybuf.h
            sq = k.sbuf("sq", [128, 2, 512], BF16, slotted=True)
            rstd = k.sbuf("rstd", [128, 512], F32)
            tmp = k.sbuf("tmp", [128, 2, 512], F32, slotted=True)
            sg = k.sbuf("sg", [128, 2, 512], F32, slotted=True)
            wg_p = WPool(k, "wg", [128, KC, 256], 3)
            wu_p = WPool(k, "wu", [128, KC, 256], 3)
            wd_p = WPool(k, "wd", [128, FC, 128], 2)
            for t0 in range(0, NT, TT):
                t1 = min(NT, t0 + TT)
                groups = col_groups(t0, t1, CTX)
                if kk == 2 and l == L - 1:
                    groups = [g for g in groups if not g[2]]
                for (c0, n, isc) in groups:
                    cl = CLS_CTX if isc else b
                    o = c0 - t0
                    norm_stats(lambda c: Rv(c0, n)[:, c, c0:c0 + n], n, PS[0], sq, rstd[:, :n])
                    for c in range(KC):
                        tb = tmp[:, c % 2, :n]
                        k.stt(tb, Rv(c0, n)[:, c, c0:c0 + n], tabA[:, l, kk, cl, c:c + 1], rstd[:, :n],
                              ALU.mult, ALU.mult)
                        k.actf(hn[:, c, o:o + n], tb, AF.Identity, bias=tabB[:, l, kk, cl, c:c + 1])
                for j in range(FC):
                    def ldg(s, dma, j=j):
                        dma(s[:, :, :], wreg(wgu_d, l, which)[l, which, 0, j // 2, :, :, :])

                    def ldu(s, dma, j=j):
                        dma(s[:, :, :], wreg(wgu_d, l, which)[l, which, 1, j // 2, :, :, :])
                    if j % 2 == 0:
                        wg2, wu2 = wg_p.load(ldg), wu_p.load(ldu)
                    jo = (j % 2) * 128
                    wg = _Reg(wg2, [wg2.h])
                    wu = _Reg(wu2, [wu2.h])
                    for gi, (c0, n, isc) in enumerate(groups):
                        o = c0 - t0
                        pg, pu = PS[1 + 2 * ((j * 2 + gi) % 2)], PS[2 + 2 * ((j * 2 + gi) % 2)]
                        for c in range(KC):
                            k.mm(pg[:, :n], wg[:, c, jo:jo + 128], hn[:, c, o:o + n], start=(c == 0), stop=(c == KC - 1))
                        for c in range(KC):
                            k.mm(pu[:, :n], wu[:, c, jo:jo + 128], hn[:, c, o:o + n], start=(c == 0), stop=(c == KC - 1))
                        sgb = sg[:, (j * 2 + gi) % 2, :n]
                        k.actf(sgb, pg[:, :n], AF.Silu)
                        k.tt(k.dve, hid[:, j, o:o + n], sgb, pu[:, :n], ALU.mult)
                for i in range(KC):
                    def ldd(s, dma, i=i):
                        dma(s[:, :, :], wreg(wdn_d, l, which)[l, which, i, :, :, :])
                    wd = wd_p.load(ldd)
                    for gi, (c0, n, isc) in enumerate(groups):
                        o = c0 - t0
                        py = PS[5 + ((i * 2 + gi) % 2)]
                        for j in range(FC):
                            k.mm(py[:, :n], wd[:, j, :], hid[:, j, o:o + n], start=(j == 0), stop=(j == FC - 1))
                        k.copy(k.act, ybuf[:, i, o:o + n], py[:, :n])
                for (c0, n, isc) in groups:
                    cl = CLS_CTX if isc else b
                    o = c0 - t0
                    norm_stats(lambda c: ybuf[:, c, o:o + n], n, PS[0], sq, rstd[:, :n])
                    for c in range(KC):
                        tb = tmp[:, c % 2, :n]
                        k.stt(tb, ybuf[:, c, o:o + n], tabG[:, l, kk, cl, c:c + 1], rstd[:, :n], ALU.mult, ALU.mult)
                        k.tt(k.dve, Rv(c0, n)[:, c, c0:c0 + n], Rv(c0, n)[:, c, c0:c0 + n], tb, ALU.add)
            k.barrier()
        k.es = es


    def wtile(pool, l, u, nu=1):
        def ld(s_, dma):
            if nu == 1:
                dma(s_[:, :, :], wreg(wi_d, l, 0)[l, u, :, :, :])
            else:
                for i in range(nu):
                    dma(s_[:, i, :, :], wreg(wi_d, l, 0)[l, u + i, :, :, :])
        return pool.load(ld)

    def mixer_phase(l, b):
        MT = 256
        NKT = NT // 128
        with contextlib.ExitStack() as es2:
            k.es = es2
            Kst = k.sbuf("Kst", [128, 2, NT], BF16, slotted=True)
            Vst = k.sbuf("Vst", [128, 2, NKT, 128], BF16, slotted=True)
            Hf = k.sbuf("Hf", [128, 4, NT], BF16, slotted=True)
            hnb = k.sbuf("hnb", [128, KC, MT + 4], BF16, slotted=True)
            sq = k.sbuf("msq", [128, 2, MT + 8], BF16, slotted=True)
            rstd = k.sbuf("mrstd", [128, MT + 8], F32)
            tmp = k.sbuf("mtmp", [128, 2, MT + 8], F32, slotted=True)
            ropeC = k.sbuf("ropeC", [128, MT], F32)
            ropeS = k.sbuf("ropeS", [128, MT], F32)
            stage = k.sbuf("stage", [128, 2, MT + 4], F32, slotted=True)
            qk = k.sbuf("qk", [128, 8, MT], BF16, slotted=True)
            Vm = k.sbuf("Vm", [128, MT // 128, 4, 129], BF16, slotted=True)
            gsbuf = k.sbuf("gsbuf", [128, MT // 128, 16], F32)
            lfb = k.sbuf("lfb", [128, MT // 128, 4], F32)
            sm = k.sbuf("sm", [128, 32], F32)
            lfrep = k.sbuf("lfrep", [128, 2, 128], F32, slotted=True)
            DT = k.sbuf("DT", [128, 4, 128], BF16, slotted=True)
            ebt = k.sbuf("ebt", [128, 4, 128], F32, slotted=True)
            qp = k.sbuf("qp", [128, 4, 128], BF16, slotted=True)
            STm = k.sbuf("STm", [128, 4, 128], BF16, slotted=True)
            kw = k.sbuf("kw", [128, 4, 128], BF16, slotted=True)
            dd = k.sbuf("dd", [128, 4, 128], F32, slotted=True)
            St = k.sbuf("St", [128, 4, 129], F32, slotted=True)
            Cbf = k.sbuf("Cbf", [128, 4, 128], BF16, slotted=True)
            nbc = k.sbuf("nbc", [128, 4, 128], BF16, slotted=True)
            carry = k.sbuf("carry", [128, 8, 2], F32)
            rawA = k.sbuf("rawA", [128, KC * MT], F32)
            hsum = Buf(rawA.t[:, 0:4 * MT].rearrange("p (h t) -> p h t", h=4), "hsum")
            qa = Buf(rawA.t[:, 4 * MT:6 * MT].bitcast(BF16).rearrange("p (h t) -> p h t", h=4), "qa")
            qb = Buf(rawA.t[:, 6 * MT:8 * MT].bitcast(BF16).rearrange("p (h t) -> p h t", h=4), "qb")
            ybuf = Buf(rawA.t[:, :].rearrange("p (c t) -> p c t", c=KC), "mybuf")
            for bb in (hsum, qa, qb, ybuf):
                bb.h = rawA.h
            yy = k.sbuf("yy", [128, 3, 4, MT], BF16, slotted=True)
            Pt = k.sbuf("Pt", [128, 4, MT], BF16, slotted=True)
            Pm = k.sbuf("Pm", [128, 3, MT], BF16, slotted=True)
            rden = k.sbuf("rden", [128, 2, MT], F32, slotted=True)
            sig = k.sbuf("sig", [128, 2, MT], F32, slotted=True)
            gz = k.sbuf("gz", [128, 2, MT], F32, slotted=True)
            msum = k.sbuf("msum", [128, KC, MT], BF16, slotted=True)
            wq_p = WPool(k, "wq", [128, KC, 128], 4)
            wv_p = WPool(k, "wv", [128, 2, KC, 128], 2)
            wb_p = WPool(k, "wb", [128, 4, 128], 3)
            wo_p = WPool(k, "wo", [128, KC, 128], 2)
            k.memset(k.dve, Vm[:, :, :, 128:129], 1.0)

            tiles = [(0, CTX, True)] + [(CTX + i * MT, MT, False) for i in range(T // MT)]
            skip_ctx_out = (l == L - 1)

            def seqb(isc):
                return (0, CTX) if isc else (CTX, NT)

            def bc(v, shape, axis):
                return V(v.ap.unsqueeze(axis).broadcast_to(shape), v.hs)

            def norm_tile(t0, n, isc, lo, hi):
                cl = CLS_CTX if isc else b
                m = hi - lo
                o = lo - (t0 - 2)
                norm_stats(lambda c: Rv(lo, m)[:, c, lo:hi], m, PS[0], sq, rstd[:, :m])
                for c in range(KC):
                    tb = tmp[:, c % 2, :m]
                    k.stt(tb, Rv(lo, m)[:, c, lo:hi], tabA[:, l, 1, cl, c:c + 1], rstd[:, :m], ALU.mult, ALU.mult)
                    k.actf(hnb[:, c, o:o + m], tb, AF.Identity, bias=tabB[:, l, 1, cl, c:c + 1])

            def rope_tile(t0, n, isc):
                if isc:
                    k.memset(k.dve, ropeC[:, :n], 1.0)
                    k.memset(k.dve, ropeS[:, :n], 0.0)
                    return
                r0 = (t0 - CTX) // 64
                nr = n // 64
                for tab, oi in ((ropeC, 0), (ropeS, 2)):
                    a0 = bc(ropec[:, oi, r0:r0 + nr], [128, nr, 64], 2)
                    a1 = bc(ropec[:, oi + 1, :], [128, nr, 64], 1)
                    outv = tab[:, :n]
                    k.tt(k.dve, V(outv.ap.rearrange("p (r c) -> p r c", c=64), outv.hs), a0, a1, ALU.add)

            def proj_fm(wt, o, n, ps):
                for c in range(KC):
                    k.mm(ps[:, :n], wt[:, c, :], hnb[:, c, o:o + n], start=(c == 0), stop=(c == KC - 1))

            def qk_post(ps, n, gcol, dst):
                xn = tmp[:, 0, :n]
                if gcol is not None:
                    k.actf(sq[:, 0, :n], ps[:, :n], AF.Square)
                    k.mm(PS[1][:, :n], cbf[:, 1, :], sq[:, 0, :n])
                    k.actf(rstd[:, :n], PS[1][:, :n], AF.Ln, bias=epsb[:, 0:1], scale=1.0 / 64)
                    k.actf(rstd[:, :n], rstd[:, :n], AF.Exp, scale=-0.5)
                    k.stt(xn, ps[:, :n], qkg[:, l, gcol:gcol + 1], rstd[:, :n], ALU.mult, ALU.mult)
                else:
                    k.copy(k.act, xn, ps[:, :n])
                k.copy(k.act, sq[:, 1, :n], xn)
                k.mm(PS[1][:, :n], cbf[:, 2, :], sq[:, 1, :n])
                t2 = tmp[:, 1, :n]
                k.tt(k.dve, t2, PS[1][:, :n], ropeS[:, :n], ALU.mult)
                k.tt(k.dve, xn, xn, ropeC[:, :n], ALU.mult)
                k.tt(k.dve, dst, xn, t2, ALU.add)

            def mlstm_tile(t0, n, isc, dirn, use_carry):
                s0, s1 = seqb(isc)
                lo = max(t0 - 2, s0)
                hi = t0 + n if use_carry else min(t0 + n + 2, s1)
                nblk = n // 128
                for ch in range(8):
                    wt = wtile(wq_p, l, U_QM + ch)
                    ps = PS[2 + ch % 2]
                    m = hi - lo
                    o = lo - (t0 - 2)
                    proj_fm(wt, o, m, ps)
                    if o > 0:
                        k.memset(k.dve, stage[:, ch % 2, 0:o], 0.0)
                    k.copy(k.act, stage[:, ch % 2, o:o + m], ps[:, :m])
                    if use_carry:
                        if t0 + n >= s1:
                            k.memset(k.dve, stage[:, ch % 2, n + 2:n + 4], 0.0)
                        else:
                            k.copy(k.dve, stage[:, ch % 2, n + 2:n + 4], carry[:, ch, :])
                    elif o + m < n + 4:
                        k.memset(k.dve, stage[:, ch % 2, o + m:n + 4], 0.0)
                    acc = tmp[:, ch % 2, :n]
                    k.ts(k.dve, acc, stage[:, ch % 2, 0:n], convw[:, l, ch, 0:1], convb[:, l, ch:ch + 1], ALU.mult, ALU.add)
                    for j in range(1, 5):
                        k.stt(acc, stage[:, ch % 2, j:j + n], convw[:, l, ch, j:j + 1], acc, ALU.mult, ALU.add)
                    if use_carry:
                        k.copy(k.dve, carry[:, ch, :], stage[:, ch % 2, 2:4])
                    k.actf(qk[:, ch, :n], acc, AF.Silu)
                for half in range(2):
                    wt = wtile(wv_p, l, U_VM + 2 * half, 2)
                    for bi in range(nblk):
                        o = 2 + bi * 128
                        for c in range(KC):
                            k.mm(PS[2][:, :256], hnb[:, c, o:o + 128], wt[:, :, c, :], start=(c == 0), stop=(c == KC - 1))
                        pv = PS[2][:, :256]
                        k.copy(k.act, Vm[:, bi, 2 * half:2 * half + 2, 0:128],
                               V(pv.ap.rearrange("p (h d) -> p h d", h=2), pv.hs))
                wg = wtile(wq_p, l, U_GM)
                for bi in range(nblk):
                    o = 2 + bi * 128
                    for c in range(KC):
                        k.mm(PS[3][:, :16], hnb[:, c, o:o + 128], wg[:, c, 0:16], start=(c == 0), stop=(c == KC - 1))
                    k.tt(k.dve, gsbuf[:, bi, :], PS[3][:, :16], gateb[:, l, :], ALU.add)
                ic = 0 if dirn > 0 else 8
                fc = ic + 4
                k.actf(lfb[:, :nblk, :], gsbuf[:, :nblk, fc:fc + 4], AF.Exp, scale=-1.0)
                k.actf(lfb[:, :nblk, :], lfb[:, :nblk, :], AF.Ln, bias=1.0)
                k.ts(k.dve, lfb[:, :nblk, :], lfb[:, :nblk, :], -1.0, None, ALU.mult)
                blks = range(nblk) if dirn > 0 else range(nblk - 1, -1, -1)
                corder = (0, 1) if dirn > 0 else (1, 0)
                PSN = [PS[2], PS[3], PS[5], PS[6]]
                mskb = cbf[:, 4, :] if dirn > 0 else cbf[:, 5, :]
                for bi in blks:
                    c0 = t0 + bi * 128
                    bo = bi * 128
                    k.mm(PS[4][:, 0:4], TRI[dirn](), lfb[:, bi, :])
                    k.mm(PS[4][:, 8:12], cf[:, 4, :], lfb[:, bi, :])
                    k.tt(k.dve, sm[:, 0:4], gsbuf[:, bi, ic:ic + 4], PS[4][:, 0:4], ALU.subtract)
                    k.tt(k.dve, sm[:, 4:8], sm[:, 0:4], PS[4][:, 8:12], ALU.add)
                    k.actf(sm[:, 4:8], sm[:, 4:8], AF.Exp)
                    k.ts(k.dve, sm[:, 8:12], sm[:, 0:4], lnsc[:, 0:1], None, ALU.add)
                    for h in range(4):
                        lfr = lfrep[:, h % 2, :]
                        k.ts(k.dve, lfr, cf[:, 5, :], lfb[:, bi, h:h + 1], None, ALU.mult)
                        k.mm(PS[5][:, 128 * h:128 * h + 128], lfr, TRI[dirn]())
                        k.mm(PS[4][:, 16 + 2 * h:18 + 2 * h], lfr, cf[:, 6, 0:2])
                    k.actf(sm[:, 16:24], PS[4][:, 16:24], AF.Exp)
                    for h in range(4):
                        k.actf(DT[:, h, :], PS[5][:, 128 * h:128 * h + 128], AF.Exp, bias=sm[:, 8 + h:9 + h])
                    pb = PS[5][:, 0:512]
                    k.actf(ebt[:, :, :], V(pb.ap.rearrange("p (h t) -> p h t", h=4), pb.hs), AF.Exp, bias=lnsc[:, 0:1])
                    k.tt(k.dve, qp[:, :, :], qk[:, 0:4, bo:bo + 128], ebt[:, :, :], ALU.mult)
                    for h in range(4):
                        k.mm(PS[6][:, 128 * h:128 * h + 128], qk[:, 4 + h, bo:bo + 128], qk[:, h, bo:bo + 128])
                    pst = PS[6][:, 0:512]
                    k.tt(k.dve, STm[:, :, :], V(pst.ap.rearrange("p (h t) -> p h t", h=4), pst.hs), DT[:, :, :], ALU.mult)
                    k.tt(k.dve, STm[:, :, :], STm[:, :, :], bc(mskb, [128, 4, 128], 1), ALU.mult)
                    for h in range(4):
                        k.transpose(PSB[:, 128 * h:128 * h + 128], qk[:, 4 + h, bo:bo + 128], cbf[:, 3, :])
                    ptr = PSB[:, 0:512]
                    k.tt(k.dve, kw[:, :, :], V(ptr.ap.rearrange("p (h t) -> p h t", h=4), ptr.hs),
                         bc(sm[:, 4:8], [128, 4, 128], 2), ALU.mult)
                    sfl = STm[:, :, :]
                    k.mm(PS[0][:, 0:512], cbf[:, 0, :], V(sfl.ap.rearrange("p h t -> p (h t)"), sfl.hs), start=True, stop=False)
                    for h in range(4):
                        k.mm(PSN[h][:, 0:128], Vm[:, bi, h, 0:128], STm[:, h, :], start=True, stop=False)
                    for ci, cc in enumerate(corder):
                        cs = slice(cc * 64, cc * 64 + 64)
                        for h in range(4):
                            last = ci == 1
                            k.mm(PSN[h][:, cs], Cbf[:, h, :], qp[:, h, cs], start=False, stop=last)
                            k.mm(PS[0][:, 128 * h + cc * 64:128 * h + cc * 64 + 64], nbc[:, h, :], qp[:, h, cs],
                                 start=False, stop=(last and h == 3))
                            k.mm(PS[1][:, 0:129], kw[cs, h, :], Vm[cs, bi, h, :])
                            k.stt(St[:, h, :], St[:, h, :], sm[:, 16 + 2 * h + cc:17 + 2 * h + cc], PS[1][:, 0:129],
                                  ALU.mult, ALU.add)
                            k.copy(k.act, Cbf[:, h, :], St[:, h, 0:128])
                            k.ts(k.dve, nbc[:, h, :], cf[:, 5, :], St[:, h, 128:129], None, ALU.mult)
                    pdn = PS[0][:, 0:512]
                    k.actf(dd[:, :, :], V(pdn.ap.rearrange("p (h t) -> p h t", h=4), pdn.hs), AF.Abs)
                    k.ts(k.dve, dd[:, :, :], dd[:, :, :], 1.0, None, ALU.max)
                    k.actf(dd[:, :, :], dd[:, :, :], AF.Ln)
                    k.actf(dd[:, :, :], dd[:, :, :], AF.Exp, scale=-1.0)
                    for h in range(4):
                        if dirn > 0:
                            k.tt(k.dve, Hf[:, h, c0:c0 + 128], PSN[h][:, 0:128], dd[:, h, :], ALU.mult)
                        else:
                            k.tt(k.dve, hsum[:, h, bo:bo + 128], PSN[h][:, 0:128], dd[:, h, :], ALU.mult)
                    if dirn < 0:
                        k.tt(k.dve, hsum[:, :, bo:bo + 128], hsum[:, :, bo:bo + 128], Hf[:, :, c0:c0 + 128], ALU.add)

            def attention(t0, n, isc, mixer):
                qsrc = qa if mixer == 0 else qb
                nct = CTX // 128
                items = []
                for c in range(4):
                    for hf in range(2):
                        kts = [(j, 0, n, None) for j in range(nct)]
                        if not isc:
                            ql0 = t0 - CTX
                            if mixer == 1:
                                kts += [(nct + j, 0, n, None) for j in range(T // 128)]
                            else:
                                for j in range(ql0 // 128 - 1, (ql0 + n) // 128 + 1):
                                    if j < 0 or j >= T // 128:
                                        continue
                                    a = max(128 * (j - 1), ql0)
                                    e = min(128 * (j + 2), ql0 + n)
                                    kts.append((nct + j, a - ql0, e - ql0, a - (128 * j - 128)))
                        for ki, (kt, a, e, mo) in enumerate(kts):
                            items.append((c, hf, ki, len(kts), kt, a, e, mo))
                PSS = [PS[0], PS[1], PS[6]]

                def stage_a(idx):
                    c, hf, ki, nk, kt, a, e, mo = items[idx]
                    pr = slice(64 * hf, 64 * hf + 64)
                    m = e - a
                    pss = PSS[idx % 3]
                    k.mm(pss[:, :m], Kst[pr, mixer, kt * 128:(kt + 1) * 128], qsrc[pr, c, a:e])
                    pt = Pt[:, idx % 4, :m]
                    k.actf(pt, pss[:, :m], AF.Exp, scale=0.125)
                    if mo is not None:
                        pm = Pm[:, idx % 3, :m]
                        k.tt(k.dve, pm, pt, winmask[:, mo:mo + m], ALU.mult)
                        pt = pm
                    return pt

                def stage_b(idx, pt):
                    c, hf, ki, nk, kt, a, e, mo = items[idx]
                    h = c + 4 * hf
                    pr = slice(64 * hf, 64 * hf + 64)
                    pn, pd = PS[2 + hf * 2], PS[3 + hf * 2]
                    lastk = ki == nk - 1
                    k.mm(pn[pr, a:e], Vst[:, mixer, kt, hf * 64:(hf + 1) * 64], pt, start=(ki == 0), stop=lastk)
                    k.mm(pd[pr, a:e], cbf[:, 0, 0:64], pt, start=(ki == 0), stop=lastk)
                    if lastk:
                        rd = rden[pr, hf, :n]
                        if mixer == 0:
                            k.actf(rd, pd[pr, :n], AF.Ln, bias=esink[pr, l, h:h + 1])
                        else:
                            k.actf(rd, pd[pr, :n], AF.Ln)
                        k.actf(rd, rd, AF.Exp, scale=-1.0)
                        k.tt(k.dve, yy[pr, mixer, c, :n], pn[pr, :n], rd, ALU.mult)

                LA = 2
                pts = {}
                for idx in range(len(items) + LA):
                    if idx < len(items):
                        pts[idx] = stage_a(idx)
                    if idx >= LA:
                        stage_b(idx - LA, pts.pop(idx - LA))

            k.memset(k.dve, St[:, :, :], 0.0)
            k.memset(k.dve, Cbf[:, :, :], 0.0)
            k.memset(k.dve, nbc[:, :, :], 0.0)
            for (t0, n, isc) in tiles:
                s0, s1 = seqb(isc)
                lo, hi = max(t0 - 2, s0), min(t0 + n + 2, s1)
                norm_tile(t0, n, isc, lo, hi)
                rope_tile(t0, n, isc)
                for mixer in range(2):
                    wt = wtile(wq_p, l, U_KA if mixer == 0 else U_KB)
                    ps = PS[2 + mixer]
                    proj_fm(wt, 2, n, ps)
                    qk_post(ps, n, (1 if mixer == 1 else None), Kst[:, mixer, t0:t0 + n])
                wt = wtile(wv_p, l, U_VAB, 2)
                for bi in range(n // 128):
                    o = 2 + bi * 128
                    for c in range(KC):
                        k.mm(PS[4][:, :256], hnb[:, c, o:o + 128], wt[:, :, c, :], start=(c == 0), stop=(c == KC - 1))
                    pv = PS[4][:, :256]
                    kt = (t0 + bi * 128) // 128
                    k.copy(k.act, Vst[:, :, kt, :], V(pv.ap.rearrange("p (m d) -> p m d", m=2), pv.hs))
                mlstm_tile(t0, n, isc, 1, False)

            k.memset(k.dve, St[:, :, :], 0.0)
            k.memset(k.dve, Cbf[:, :, :], 0.0)
            k.memset(k.dve, nbc[:, :, :], 0.0)
            tiles2 = [tiles[0]] + tiles[:0:-1]
            for (t0, n, isc) in tiles2:
                cl = CLS_CTX if isc else b
                s0, s1 = seqb(isc)
                lo = max(t0 - 2, s0)
                norm_tile(t0, n, isc, lo, t0 + n)
                mlstm_tile(t0, n, isc, -1, True)
                if isc and skip_ctx_out:
                    continue
                rope_tile(t0, n, isc)
                for h in range(4):
                    k.actf(sq[:, h % 2, :n], hsum[:, h, :n], AF.Square)
                    k.mm(PS[0][:, :n], cbf[:, 0, :], sq[:, h % 2, :n])
                    k.actf(rstd[:, :n], PS[0][:, :n], AF.Ln, bias=epsb[:, 0:1], scale=1.0 / 128)
                    k.actf(rstd[:, :n], rstd[:, :n], AF.Exp, scale=-0.5)
                    wt = wtile(wq_p, l, U_OM + h)
                    proj_fm(wt, 2, n, PS[1])
                    k.actf(sig[:, h % 2, :n], PS[1][:, :n], AF.Sigmoid)
                    tb = tmp[:, h % 2, :n]
                    k.stt(tb, hsum[:, h, :n], mgs[:, l, h:h + 1], rstd[:, :n], ALU.mult, ALU.mult)
                    k.tt(k.dve, yy[:, 2, h, :n], tb, sig[:, h % 2, :n], ALU.mult)
                for mixer in range(2):
                    for c in range(4):
                        wt = wtile(wq_p, l, (U_QA if mixer == 0 else U_QB) + c)
                        ps = PS[2 + c % 2]
                        proj_fm(wt, 2, n, ps)
                        qk_post(ps, n, (0 if mixer == 1 else None), (qa if mixer == 0 else qb)[:, c, :n])
                attention(t0, n, isc, 0)
                attention(t0, n, isc, 1)
                for i in range(KC):
                    for br in range(3):
                        def ldb(s_, dma, br=br, i=i):
                            dma(s_[:, :, :], wreg(wb_d, l, 0)[l, br, i, :, :, :])
                        wb = wb_p.load(ldb)
                        pz = PS[2 + br % 2]
                        for c in range(4):
                            k.mm(pz[:, :n], wb[:, c, :], yy[:, br, c, :n], start=(c == 0), stop=(c == 3))
                        wt = wtile(wq_p, l, U_GL + br * 8 + i)
                        pg = PS[4 + br % 2]
                        proj_fm(wt, 2, n, pg)
                        sg_ = sig[:, br % 2, :n]
                        k.actf(sg_, pg[:, :n], AF.Sigmoid)
                        if br == 0:
                            k.tt(k.dve, gz[:, 0, :n], sg_, pz[:, :n], ALU.mult)
                        else:
                            k.tt(k.dve, gz[:, 1, :n], sg_, pz[:, :n], ALU.mult)
                            dst = msum[:, i, :n] if br == 2 else gz[:, 0, :n]
                            k.tt(k.dve, dst, gz[:, 0, :n], gz[:, 1, :n], ALU.add)
                for o_ in range(KC):
                    def ldo(s_, dma, o_=o_):
                        dma(s_[:, :, :], wreg(wo_d, l, 0)[l, o_, :, :, :])
                    wo = wo_p.load(ldo)
                    py = PS[2 + o_ % 2]
                    for i in range(KC):
                        k.mm(py[:, :n], wo[:, i, :], msum[:, i, :n], start=(i == 0), stop=(i == KC - 1))
                    k.copy(k.act, ybuf[:, o_, :n], py[:, :n])
                norm_stats(lambda c: ybuf[:, c, :n], n, PS[0], sq, rstd[:, :n])
                for c in range(KC):
                    tb = tmp[:, c % 2, :n]
                    k.stt(tb, ybuf[:, c, :n], tabG[:, l, 1, cl, c:c + 1], rstd[:, :n], ALU.mult, ALU.mult)
                    k.tt(k.dve, Rv(t0, n)[:, c, t0:t0 + n], Rv(t0, n)[:, c, t0:t0 + n], tb, ALU.add)
            k.barrier()
        k.es = es

    for b in range(NB):
        k.dma(k.sp, R.rs(range(CTX // 256))[:, :, 0:CTX], ctxT[b, :, :, :], sem_in)
        k.dma(k.sp, R.rs(range(CTX // 256, NRB))[:, :, CTX:NT], xT[b, :, :, :], sem_in)
        for l in range(L):
            ffn_phase(l, 0, b)
            if cfg.stop_after == "ffn1":
                break
            mixer_phase(l, b)
            if cfg.stop_after == "mixer":
                break
            ffn_phase(l, 2, b)
        k.dma(k.sp, outT[b, :, :, :], R.rs(range(CTX // 256, NRB))[:, :, CTX:NT], sem_out)
    k.final_wait(k.sp, [sem_out])
    return nc, es, k


def make_consts():
    c = np.zeros((128, 8, 128), np.float32)
    c[:, 0, :] = 1.0
    p = np.arange(128)
    c[:, 1, :] = (p[:, None] // 64 == p[None, :] // 64)
    c[:, 2, :] = (p[:, None] == (p[None, :] ^ 16))
    c[:, 3, :] = np.eye(128)
    same = (p[:, None] // 64 == p[None, :] // 64)
    c[:, 4, :] = same & (p[:, None] <= p[None, :])
    c[:, 5, :] = same & (p[:, None] >= p[None, :])
    return c


def make_consts_f():
    c = np.zeros((128, 8, 128), np.float32)
    p = np.arange(128)
    same = (p[:, None] // 64 == p[None, :] // 64)
    c[:, 0, :] = same & (p[:, None] <= p[None, :])
    c[:, 1, :] = same & (p[:, None] >= p[None, :])
    c[:, 2, :] = np.where(c[:, 0, :] > 0, 0.0, NEG)
    c[:, 3, :] = np.where(c[:, 1, :] > 0, 0.0, NEG)
    c[:, 4, :] = same
    c[:, 5, :] = 1.0
    c[:, 6, 0] = p < 64
    c[:, 6, 1] = p >= 64
    return c


def make_winmask():
    s_ = np.arange(128)[:, None]
    q = np.arange(384)[None, :]
    return (np.abs(q - 128 - s_) <= 128).astype(np.float32)


def make_rope(T):
    rows = T // GRID_W
    freqs = (np.float32(10000.0) ** (-np.arange(16, dtype=np.float32) / np.float32(16))).astype(np.float32)
    r = np.zeros((128, 4, 64), np.float32)
    for p in range(128):
        d = p % 64
        axis, half, pair = d // 32, (d % 32) // 16, d % 16
        sign = -1.0 if half == 0 else 1.0
        if axis == 0:
            ang = (np.arange(rows, dtype=np.float32) * freqs[pair]).astype(np.float32)
            r[p, 0, :rows] = np.cos(ang)
            r[p, 2, :rows] = sign * np.sin(ang)
        else:
            ang = (np.arange(64, dtype=np.float32) * freqs[pair]).astype(np.float32)
            r[p, 1, :] = np.cos(ang)
            r[p, 3, :] = sign * np.sin(ang)
    return r


def to_bf16(a):
    import ml_dtypes
    return a.astype(ml_dtypes.bfloat16)


def fm(a):
    a = np.swapaxes(a, -1, -2)
    sh = a.shape
    a = a.reshape(sh[:-2] + (KC, 128, sh[-1]))
    return np.ascontiguousarray(np.swapaxes(a, -3, -2))


def vec_t(a, nchunk):
    sh = a.shape
    a = a.reshape(sh[:-1] + (nchunk, 128))
    return np.ascontiguousarray(np.moveaxis(a, -1, 0))


def host_inputs(cfg, inp, core):
    NB = cfg.NB
    bs = slice(core * NB, (core + 1) * NB)
    L = cfg.L
    m = {}
    m["xT"] = fm(inp["x"][bs])
    m["ctxT"] = fm(inp["ctx"][bs])
    cc = np.concatenate([inp["c"][bs], inp["c_ctx"][None, :]], axis=0)
    m["c_in"] = np.ascontiguousarray(cc.T.reshape(KC, 128, NB + 1).transpose(1, 0, 2))
    m["w_ada"] = inp["w_ada"][:L]
    m["b_ada_t"] = vec_t(inp["b_ada"][:L], 72)
    m["norm_g_t"] = vec_t(inp["norm_g"][:L], KC)
    m["ffn_w_gate"] = inp["ffn_w_gate"][:L]
    m["ffn_w_up"] = inp["ffn_w_up"][:L]
    m["ffn_w_down"] = inp["ffn_w_down"][:L]
    m["w_in"] = inp["w_in"][:L]
    m["w_branch"] = inp["w_branch"][:L]
    m["w_out"] = inp["w_out"][:L]
    m["consts_bf"] = make_consts()
    m["consts_f"] = make_consts_f()
    m["winmask"] = make_winmask()
    m["rope"] = make_rope(cfg.T)
    cw = inp["conv_w"][:L]
    m["convw_t"] = np.ascontiguousarray(cw.reshape(L, 5, 8, 128).transpose(3, 0, 2, 1))
    m["convb_t"] = vec_t(inp["conv_b"][:L], 8)
    m["mg_t"] = vec_t(inp["mlstm_norm_g"][:L], 4)
    qg = inp["qk_norm_g"][:L]
    m["qkg_t"] = np.ascontiguousarray(np.tile(qg, (1, 1, 2)).transpose(2, 0, 1))
    m["sink_t"] = np.ascontiguousarray(np.broadcast_to(inp["attn_sink"][:L][None], (128, L, 8)))
    m["gateb_t"] = np.ascontiguousarray(np.broadcast_to(inp["mlstm_gate_b"][:L][None], (128, L, 16)))
    return m


_CACHE = {}


def run(cfg, inputs):
    key = (cfg.T, cfg.CTX, cfg.L, cfg.NB, cfg.n_cores, cfg.stop_after)
    if key not in _CACHE:
        _CACHE[key] = build_program(cfg)
    nc, es, kk = _CACHE[key]
    inp = {n: np.asarray(v) for n, v in inputs.items()}
    in_maps = [host_inputs(cfg, inp, core) for core in range(cfg.n_cores)]
    res = run_bass_kernel_spmd(nc, in_maps, core_ids=list(range(cfg.n_cores)))
    outs = []
    for core in range(cfg.n_cores):
        o = res.results[core]["outT"]
        o = np.swapaxes(o, 1, 2).reshape(cfg.NB, D, cfg.T)
        outs.append(np.swapaxes(o, 1, 2))
    return np.ascontiguousarray(np.concatenate(outs, axis=0)).astype(np.float32)


def kernel(**inputs):
    cfg = Cfg()
    return run(cfg, inputs)
```

```python
import contextlib
import numpy as np
import concourse.bass as bass
import concourse.mybir as mybir
from concourse.bass_utils import run_bass_kernel_spmd

F32 = mybir.dt.float32
BF16 = mybir.dt.bfloat16
AF = mybir.ActivationFunctionType
ALU = mybir.AluOpType

D = 1024
KC = 8
FF = 2816
FC = 22
GRID_W = 64
HD = 64
IN_W = 6672
O_QA, O_KA, O_VA, O_QB, O_KB, O_VB = 0, 512, 640, 768, 1280, 1408
O_QM, O_KM, O_VM, O_OM, O_GM, O_GL = 1536, 2048, 2560, 3072, 3584, 3600
EPS = 1e-6
NEG = -30000.0


class Cfg:
    def __init__(self, T=2048, CTX=256, L=4, NB=2, n_cores=8, stop_after=None):
        self.T, self.CTX, self.L, self.NB, self.n_cores = T, CTX, L, NB, n_cores
        self.NT = T + CTX
        self.stop_after = stop_after


class Sem:
    def __init__(self, h):
        self.h = h
        self.count = 0


class H:
    __slots__ = ("name", "w", "r", "excl")

    def __init__(self, name=""):
        self.name = name
        self.excl = False
        self.w = None
        self.r = {}


class V:
    __slots__ = ("ap", "hs")

    def __init__(self, ap, hs):
        self.ap = ap
        self.hs = hs


class Buf:
    def __init__(self, t, name, slots=None):
        self.t = t
        self.name = name
        self.h = H(name)
        self.regs = {}
        self.slots = slots

    def _slot(self, j):
        key = ("slot", j)
        if key not in self.regs:
            self.regs[key] = H(f"{self.name}[{j}]")
        return self.regs[key]

    def __getitem__(self, idx):
        if self.slots:
            i1 = idx[1] if isinstance(idx, tuple) and len(idx) > 1 else slice(None)
            if isinstance(i1, int):
                hs = [self._slot(i1)]
            else:
                hs = [self._slot(j) for j in range(*i1.indices(self.slots))]
            return V(self.t[idx], hs)
        return V(self.t[idx], [self.h])

    def r(self, key):
        if key not in self.regs:
            self.regs[key] = H(f"{self.name}.{key}")
        return _Reg(self, [self.regs[key]])

    def rs(self, keys):
        hs = []
        for key in keys:
            if key not in self.regs:
                self.regs[key] = H(f"{self.name}.{key}")
            hs.append(self.regs[key])
        return _Reg(self, hs)


class _Reg:
    def __init__(self, buf, hs):
        self.buf, self.hs = buf, hs

    def __getitem__(self, idx):
        return V(self.buf.t[idx], self.hs)


class Eng:
    def __init__(self, name, raw, sem, self_sync):
        self.name, self.raw, self.sem = name, raw, sem
        self.seen = {}
        self.self_sync = self_sync


class K:
    def __init__(self, nc, es):
        self.nc, self.es, self.es0 = nc, es, es
        self.n_inst = 0
        self.sem_by_name = {}
        mk = lambda n: Sem(es.enter_context(nc.semaphore(n)))
        self.pe = Eng("pe", nc.tensor, mk("s_pe"), False)
        self.act = Eng("act", nc.scalar, mk("s_act"), True)
        self.dve = Eng("dve", nc.vector, mk("s_dve"), True)
        self.pool = Eng("pool", nc.gpsimd, mk("s_pool"), True)
        self.sp = Eng("sp", nc.sync, mk("s_sp"), True)
        self.engs = [self.pe, self.act, self.dve, self.pool, self.sp]
        self.dma_sems = []

    def sbuf(self, name, shape, dt, slotted=False):
        self.n_buf = getattr(self, "n_buf", 0) + 1
        name = f"{name}_{self.n_buf}"
        return Buf(self.es.enter_context(self.nc.sbuf_tensor(name, list(shape), dt)), name,
                   slots=(shape[1] if slotted else None))

    def psum(self, name, shape, dt):
        b = Buf(self.es.enter_context(self.nc.psum_tensor(name, list(shape), dt)), name)
        b.h.excl = True
        return b

    def new_dma_sem(self, name):
        if name not in self.sem_by_name:
            s = Sem(self.es0.enter_context(self.nc.semaphore(name)))
            self.dma_sems.append(s)
            self.sem_by_name[name] = s
        return self.sem_by_name[name]

    def _wait(self, eng, deps):
        for sem, val in deps.items():
            if sem is eng.sem and not eng.self_sync:
                continue
            if eng.seen.get(sem, 0) < val:
                eng.raw.wait_ge(sem.h, val)
                eng.seen[sem] = val

    @staticmethod
    def _deps(reads, writes, own=None):
        deps = {}

        def add(ev):
            if ev is not None:
                s, v = ev
                if deps.get(s, 0) < v:
                    deps[s] = v
        for h in reads:
            add(h.w)
            if h.excl:
                for s, v in h.r.items():
                    if s is not own:
                        add((s, v))
        for h in writes:
            add(h.w)
            for s, v in h.r.items():
                add((s, v))
        return deps

    def emit(self, eng, reads, writes, fn):
        deps = self._deps(reads, writes, eng.sem)
        self._wait(eng, deps)
        inst = fn()
        inst.then_inc(eng.sem.h, 1)
        eng.sem.count += 1
        ev = (eng.sem, eng.sem.count)
        for h in reads:
            if h.r.get(ev[0], 0) < ev[1]:
                h.r[ev[0]] = ev[1]
        for h in writes:
            h.w = ev
            h.r = {}
        self.n_inst += 1
        return inst

    def dma(self, q, out, in_, sem, **kw):
        reads, writes = in_.hs, out.hs
        deps = self._deps(reads, writes)
        if sem.count:
            deps[sem] = max(deps.get(sem, 0), sem.count)
        self._wait(q, deps)
        inst = q.raw.dma_start(out=out.ap, in_=in_.ap, **kw)
        inst.then_inc(sem.h, 16)
        sem.count += 16
        ev = (sem, sem.count)
        for h in reads:
            if h.r.get(ev[0], 0) < ev[1]:
                h.r[ev[0]] = ev[1]
        for h in writes:
            h.w = ev
            h.r = {}
        self.n_inst += 1

    def barrier(self):
        deps = {}
        for e in self.engs:
            if e.sem.count:
                deps[e.sem] = e.sem.count
        for s in self.dma_sems:
            if s.count:
                deps[s] = s.count
        for e in self.engs:
            d = {s: v for s, v in deps.items() if s is not e.sem}
            self._wait(e, d)

    def final_wait(self, eng, sems):
        for s in sems:
            eng.raw.wait_ge(s.h, s.count)

    @staticmethod
    def _hs(*vs):
        out = []
        for v in vs:
            if isinstance(v, V):
                out += v.hs
        return out

    @staticmethod
    def _a(v):
        return v.ap if isinstance(v, V) else v

    def mm(self, out, lhsT, rhs, start=True, stop=True, **kw):
        return self.emit(self.pe, lhsT.hs + rhs.hs, out.hs,
                         lambda: self.nc.tensor.matmul(out.ap, lhsT.ap, rhs.ap, start=start, stop=stop, **kw))

    def transpose(self, out, in_, ident):
        return self.emit(self.pe, in_.hs + ident.hs, out.hs,
                         lambda: self.nc.tensor.transpose(out.ap, in_.ap, ident.ap))

    def actf(self, out, in_, func, bias=None, scale=None):
        kw = {}
        if bias is not None:
            kw["bias"] = self._a(bias)
        if scale is not None:
            kw["scale"] = self._a(scale)
        return self.emit(self.act, self._hs(in_, bias, scale), out.hs,
                         lambda: self.nc.scalar.activation(out=out.ap, in_=in_.ap, func=func, **kw))

    def tt(self, eng, out, in0, in1, op):
        return self.emit(eng, in0.hs + in1.hs, out.hs,
                         lambda: eng.raw.tensor_tensor(out=out.ap, in0=in0.ap, in1=in1.ap, op=op))

    def ts(self, eng, out, in0, s1, s2, op0, op1=None):
        kw = {}
        if op1 is not None:
            kw["op1"] = op1
        return self.emit(eng, self._hs(in0, s1, s2), out.hs,
                         lambda: eng.raw.tensor_scalar(out=out.ap, in0=in0.ap, scalar1=self._a(s1),
                                                       scalar2=self._a(s2), op0=op0, **kw))

    def stt(self, out, in0, scalar, in1, op0, op1):
        return self.emit(self.dve, self._hs(in0, scalar, in1), out.hs,
                         lambda: self.nc.vector.scalar_tensor_tensor(out=out.ap, in0=in0.ap, scalar=self._a(scalar),
                                                                     in1=in1.ap, op0=op0, op1=op1))

    def copy(self, eng, out, in_):
        if eng is self.act:
            return self.emit(eng, in_.hs, out.hs, lambda: self.nc.scalar.copy(out=out.ap, in_=in_.ap))
        return self.emit(eng, in_.hs, out.hs, lambda: eng.raw.tensor_copy(out=out.ap, in_=in_.ap))

    def recip(self, out, in_):
        return self.emit(self.dve, in_.hs, out.hs, lambda: self.nc.vector.reciprocal(out=out.ap, in_=in_.ap))

    def memset(self, eng, out, val):
        return self.emit(eng, [], out.hs, lambda: eng.raw.memset(out.ap, val))


class WPool:
    def __init__(self, k, name, shape, n, dt=BF16, q=None):
        self.k = k
        self.slots = [k.sbuf(f"{name}{i}", shape, dt) for i in range(n)]
        self.sems = [k.new_dma_sem(f"ws_{name}{i}") for i in range(n)]
        self.i = 0
        self.q = q

    def load(self, fn):
        s, sem = self.slots[self.i], self.sems[self.i]
        self.i = (self.i + 1) % len(self.slots)
        fn(s, lambda out, in_, **kw: self.k.dma(self.q or self.k.sp, out, in_, sem, **kw))
        return s


def col_groups(a, b, CTX, maxn=512):
    out = []
    c = a
    while c < b:
        lim = CTX if c < CTX else b
        n = min(maxn, lim - c, b - c)
        out.append((c, n, c < CTX))
        c += n
    return out


def build_program(cfg):
    nc = bass.Bass("TRN2", target_bir_lowering=False)
    es = contextlib.ExitStack()
    k = K(nc, es)
    T, CTX, L, NB, NT = cfg.T, cfg.CTX, cfg.L, cfg.NB, cfg.NT
    NCLS = NB + 1
    CLS_CTX = NB

    def din(name, shape, dt=F32):
        t = nc.dram_tensor(name, list(shape), dt, kind="ExternalInput")
        return Buf(t.ap(), name)

    xT = din("xT", [NB, 128, KC, T])
    ctxT = din("ctxT", [NB, 128, KC, CTX])
    c_in = din("c_in", [128, KC, NCLS])
    w_ada = din("w_ada", [L, D, 9 * D])
    b_ada_t = din("b_ada_t", [128, L, 72])
    norm_g_t = din("norm_g_t", [128, L, 6, KC])
    w_gate = din("ffn_w_gate", [L, 2, D, FF])
    w_up = din("ffn_w_up", [L, 2, D, FF])
    w_down = din("ffn_w_down", [L, 2, FF, D])
    w_in = din("w_in", [L, D, IN_W])
    w_branch = din("w_branch", [L, 3, 512, D])
    w_out = din("w_out", [L, D, D])
    consts_bf = din("consts_bf", [128, 8, 128])
    consts_f = din("consts_f", [128, 8, 128])
    winmask_d = din("winmask", [128, 384])
    rope_d = din("rope", [128, 4, 64])
    convw_t = din("convw_t", [128, L, 8, 5])
    convb_t = din("convb_t", [128, L, 8])
    mg_t = din("mg_t", [128, L, 4])
    qkg_t = din("qkg_t", [128, L, 2])
    sink_t = din("sink_t", [128, L, 8])
    gateb_t = din("gateb_t", [128, L, 16])
    outT_t = nc.dram_tensor("outT", [NB, 128, KC, T], F32, kind="ExternalOutput")
    outT = Buf(outT_t.ap(), "outT")

    def dint(name, shape):
        t = nc.dram_tensor(name, list(shape), BF16, kind="Internal")
        return Buf(t.ap(), name)

    UNITS = [[(O_KA, 128)], [(O_KB, 128)], [(O_VA, 128)], [(O_VB, 128)]]
    U_KA, U_KB, U_VAB = 0, 1, 2
    U_QA = len(UNITS)
    UNITS += [[(O_QA + 64 * c, 64), (O_QA + 64 * (c + 4), 64)] for c in range(4)]
    U_QB = len(UNITS)
    UNITS += [[(O_QB + 64 * c, 64), (O_QB + 64 * (c + 4), 64)] for c in range(4)]
    U_QM = len(UNITS)
    UNITS += [[(O_QM + 128 * h, 128)] for h in range(4)]
    UNITS += [[(O_KM + 128 * h, 128)] for h in range(4)]
    U_VM = len(UNITS)
    UNITS += [[(O_VM + 128 * h, 128)] for h in range(4)]
    U_OM = len(UNITS)
    UNITS += [[(O_OM + 128 * h, 128)] for h in range(4)]
    U_GM = len(UNITS)
    UNITS += [[(O_GM, 16)]]
    U_GL = len(UNITS)
    UNITS += [[(O_GL + 128 * g, 128)] for g in range(24)]
    NU = len(UNITS)
    wgu_d = dint("wgu_d", [L, 2, 2, FC // 2, 128, KC, 256])
    wdn_d = dint("wdn_d", [L, 2, KC, 128, FC, 128])
    wi_d = dint("wi_d", [L, NU, 128, KC, 128])
    wb_d = dint("wb_d", [L, 3, KC, 128, 4, 128])
    wo_d = dint("wo_d", [L, KC, 128, KC, 128])
    cv_sems = [k.new_dma_sem(f"cv{i}") for i in range(4)]
    cv_n = [0]

    def wreg(buf, l, grp):
        return buf.rs([(l, grp, q) for q in range(4)])

    def cv(buf, l, grp, dst_ap, src_v):
        q = cv_n[0] % 4
        cv_n[0] += 1
        k.dma(k.pool, V(dst_ap, buf.r((l, grp, q)).hs), src_v, cv_sems[q])

    def km(v):
        return V(v.ap.rearrange("(c p) n -> p c n", p=128), v.hs)

    def convert_ffn(l, which):
        for j2 in range(FC // 2):
            for g, wsrc in enumerate((w_gate, w_up)):
                cv(wgu_d, l, which, wgu_d.t[l, which, g, j2, :, :, :], km(wsrc[l, which, :, j2 * 256:(j2 + 1) * 256]))
        for i in range(KC):
            cv(wdn_d, l, which, wdn_d.t[l, which, i, :, :, :], km(w_down[l, which, :, i * 128:(i + 1) * 128]))

    def convert_mixer(l):
        for u, rngs in enumerate(UNITS):
            o = 0
            for (c0, n) in rngs:
                cv(wi_d, l, 0, wi_d.t[l, u, :, :, o:o + n], km(w_in[l, :, c0:c0 + n]))
                o += n
        for br in range(3):
            for c in range(4):
                if br < 2:
                    for hf in range(2):
                        r0 = 64 * (c + 4 * hf)
                        src = w_branch[l, br, r0:r0 + 64, :]
                        cv(wb_d, l, 0, wb_d.t[l, br, :, 64 * hf:64 * hf + 64, c, :].rearrange("i p n -> p i n"),
                           V(src.ap.rearrange("p (i n) -> p i n", n=128), src.hs))
                else:
                    src = w_branch[l, br, 128 * c:128 * c + 128, :]
                    cv(wb_d, l, 0, wb_d.t[l, br, :, :, c, :].rearrange("i p n -> p i n"),
                       V(src.ap.rearrange("p (i n) -> p i n", n=128), src.hs))
        for o_ in range(KC):
            cv(wo_d, l, 0, wo_d.t[l, o_, :, :, :], km(w_out[l, :, o_ * 128:(o_ + 1) * 128]))

    R = k.sbuf("R", [128, KC, NT], F32)
    NRB = NT // 256

    def Rv(c0, n):
        keys = list(range(c0 // 256, (c0 + n - 1) // 256 + 1))
        return R.rs(keys)

    gsb = k.sbuf("gsb", [128, L, 6, KC], F32)
    tabA = k.sbuf("tabA", [128, L, 3, NCLS, KC], F32)
    tabB = k.sbuf("tabB", [128, L, 3, NCLS, KC], F32)
    tabG = k.sbuf("tabG", [128, L, 3, NCLS, KC], F32)
    cbf = k.sbuf("cbf", [128, 8, 128], BF16)
    ONES = lambda: cbf[:, 0, :]
    epsb = k.sbuf("epsb", [128, 1], F32)
    cf = k.sbuf("cf", [128, 8, 128], F32)
    TRI = {1: (lambda: cf[:, 0, :]), -1: (lambda: cf[:, 1, :])}
    MSK = {1: (lambda: cf[:, 2, :]), -1: (lambda: cf[:, 3, :])}
    winmask = k.sbuf("winmask_sb", [128, 384], BF16)
    ropec = k.sbuf("ropec", [128, 4, 64], F32)
    convw = k.sbuf("convw", [128, L, 8, 5], F32)
    convb = k.sbuf("convb", [128, L, 8], F32)
    mgs = k.sbuf("mgs", [128, L, 4], F32)
    qkg = k.sbuf("qkg", [128, L, 2], F32)
    esink = k.sbuf("esink", [128, L, 8], F32)
    gateb = k.sbuf("gateb", [128, L, 16], F32)
    lnsc = k.sbuf("lnsc", [128, 1], F32)
    sem_in = k.new_dma_sem("sem_in")
    sem_c = k.new_dma_sem("sem_c")
    sem_cp = k.new_dma_sem("sem_cp")
    sem_out = k.new_dma_sem("sem_out")

    PS = [k.psum(f"ps{i}", [128, 512], F32) for i in range(7)]
    PSB = k.psum("psb", [128, 1024], BF16)

    k.dma(k.pool, cbf[:, :, :], consts_bf[:, :, :], sem_cp)
    k.dma(k.sp, gsb[:, :, :, :], norm_g_t[:, :, :, :], sem_c)
    k.memset(k.dve, epsb[:, :], EPS)
    k.memset(k.dve, lnsc[:, :], float(np.log(128.0 ** -0.5)))
    k.dma(k.sp, cf[:, :, :], consts_f[:, :, :], sem_c)
    k.dma(k.pool, winmask[:, :], winmask_d[:, :], sem_cp)
    k.dma(k.sp, ropec[:, :, :], rope_d[:, :, :], sem_c)
    k.dma(k.sp, convw[:, :, :, :], convw_t[:, :, :, :], sem_c)
    k.dma(k.sp, convb[:, :, :], convb_t[:, :, :], sem_c)
    k.dma(k.sp, mgs[:, :, :], mg_t[:, :, :], sem_c)
    k.dma(k.sp, qkg[:, :, :], qkg_t[:, :, :], sem_c)
    k.dma(k.sp, esink[:, :, :], sink_t[:, :, :], sem_c)
    k.dma(k.sp, gateb[:, :, :], gateb_t[:, :, :], sem_c)
    k.actf(esink[:, :, :], esink[:, :, :], AF.Exp)
    convert_ffn(0, 0)
    with contextlib.ExitStack() as es2:
        k.es = es2
        modraw = k.sbuf("modraw", [128, L, 72, NCLS], F32)
        csb = k.sbuf("csb", [128, KC, NCLS], F32)
        csig = k.sbuf("csig", [128, KC, NCLS], F32)
        cact = k.sbuf("cact", [128, KC, NCLS], BF16)
        bada = k.sbuf("bada", [128, L, 72], F32)
        wp = WPool(k, "wada", [128, KC, 512], 3, q=k.pool)
        k.dma(k.sp, csb[:, :, :], c_in[:, :, :], sem_c)
        k.dma(k.sp, bada[:, :, :], b_ada_t[:, :, :], sem_c)
        k.actf(csig[:, :, :], csb[:, :, :], AF.Sigmoid)
        k.tt(k.dve, cact[:, :, :], csb[:, :, :], csig[:, :, :], ALU.mult)
        for l in range(L):
            for n4 in range(18):
                src = w_ada[l, :, n4 * 512:(n4 + 1) * 512]

                def ld(s, dma, src=src):
                    dma(s[:, :, :], V(src.ap.rearrange("(c p) n -> p c n", p=128), src.hs))
                wt = wp.load(ld)
                ps = PS[n4 % 4]
                for j in range(4):
                    n = n4 * 4 + j
                    for c in range(KC):
                        k.mm(ps[:, j * 8:j * 8 + NCLS], wt[:, c, j * 128:(j + 1) * 128], cact[:, c, :],
                             start=(c == 0), stop=(c == KC - 1))
                for j in range(4):
                    n = n4 * 4 + j
                    k.ts(k.dve, modraw[:, l, n, :], ps[:, j * 8:j * 8 + NCLS], bada[:, l, n:n + 1], None, ALU.add)
        for l in range(L):
            for kk in range(3):
                wgt = 1.0 if kk == 1 else 0.5
                for cl in range(NCLS):
                    sl = lambda m: modraw[:, l, m * 8:(m + 1) * 8, cl]
                    k.copy(k.dve, tabB[:, l, kk, cl, :], sl(3 * kk))
                    k.stt(tabA[:, l, kk, cl, :], sl(3 * kk + 1), 1.0, gsb[:, l, 2 * kk, :], ALU.add, ALU.mult)
                    k.stt(tabG[:, l, kk, cl, :], sl(3 * kk + 2), wgt, gsb[:, l, 2 * kk + 1, :], ALU.mult, ALU.mult)
        k.barrier()
    k.es = es
    for l_ in range(L):
        if l_ > 0:
            convert_ffn(l_, 0)
        convert_mixer(l_)
        convert_ffn(l_, 1)

    def norm_stats(src_fn, n, ps, sqbuf, rstd_out):
        for c in range(KC):
            k.actf(sqbuf[:, c % 2, :n], src_fn(c), AF.Square)
            k.mm(ps[:, :n], ONES(), sqbuf[:, c % 2, :n], start=(c == 0), stop=(c == KC - 1))
        k.actf(rstd_out, ps[:, :n], AF.Ln, bias=epsb[:, 0:1], scale=1.0 / D)
        k.actf(rstd_out, rstd_out, AF.Exp, scale=-0.5)

    def ffn_phase(l, kk, b):
        which = 0 if kk == 0 else 1
        TT = 768
        with contextlib.ExitStack() as es2:
            k.es = es2
            hid = k.sbuf("hid", [128, FC, TT], BF16, slotted=True)
            ybuf = k.sbuf("ybuf", [128, KC, TT], F32)
            hn = Buf(ybuf.t[:, :, :].rearrange("p c t -> p (c t)").bitcast(BF16)[:, 0:KC * TT]
                     .rearrange("p (c t) -> p c t", c=KC), "hn")
            hn.h = ybuf.h
            sq = k.sbuf("sq", [128, 2, 512], BF16, slotted=True)
            rstd = k.sbuf("rstd", [128, 512], F32)
            tmp = k.sbuf("tmp", [128, 2, 512], F32, slotted=True)
            sg = k.sbuf("sg", [128, 2, 512], F32, slotted=True)
            wg_p = WPool(k, "wg", [128, KC, 256], 3)
            wu_p = WPool(k, "wu", [128, KC, 256], 3)
            wd_p = WPool(k, "wd", [128, FC, 128], 2)
            for t0 in range(0, NT, TT):
                t1 = min(NT, t0 + TT)
                groups = col_groups(t0, t1, CTX)
                if kk == 2 and l == L - 1:
                    groups = [g for g in groups if not g[2]]
                for (c0, n, isc) in groups:
                    cl = CLS_CTX if isc else b
                    o = c0 - t0
                    norm_stats(lambda c: Rv(c0, n)[:, c, c0:c0 + n], n, PS[0], sq, rstd[:, :n])
                    for c in range(KC):
                        tb = tmp[:, c % 2, :n]
                        k.stt(tb, Rv(c0, n)[:, c, c0:c0 + n], tabA[:, l, kk, cl, c:c + 1], rstd[:, :n],
                              ALU.mult, ALU.mult)
                        k.actf(hn[:, c, o:o + n], tb, AF.Identity, bias=tabB[:, l, kk, cl, c:c + 1])
                for j in range(FC):
                    def ldg(s, dma, j=j):
                        dma(s[:, :, :], wreg(wgu_d, l, which)[l, which, 0, j // 2, :, :, :])

                    def ldu(s, dma, j=j):
                        dma(s[:, :, :], wreg(wgu_d, l, which)[l, which, 1, j // 2, :, :, :])
                    if j % 2 == 0:
                        wg2, wu2 = wg_p.load(ldg), wu_p.load(ldu)
                    jo = (j % 2) * 128
                    wg = _Reg(wg2, [wg2.h])
                    wu = _Reg(wu2, [wu2.h])
                    for gi, (c0, n, isc) in enumerate(groups):
                        o = c0 - t0
                        pg, pu = PS[1 + 2 * ((j * 2 + gi) % 2)], PS[2 + 2 * ((j * 2 + gi) % 2)]
                        for c in range(KC):
                            k.mm(pg[:, :n], wg[:, c, jo:jo + 128], hn[:, c, o:o + n], start=(c == 0), stop=(c == KC - 1))
                        for c in range(KC):
                            k.mm(pu[:, :n], wu[:, c, jo:jo + 128], hn[:, c, o:o + n], start=(c == 0), stop=(c == KC - 1))
                        sgb = sg[:, (j * 2 + gi) % 2, :n]
                        k.actf(sgb, pg[:, :n], AF.Silu)
                        k.tt(k.dve, hid[:, j, o:o + n], sgb, pu[:, :n], ALU.mult)
                for i in range(KC):
                    def ldd(s, dma, i=i):
                        dma(s[:, :, :], wreg(wdn_d, l, which)[l, which, i, :, :, :])
                    wd = wd_p.load(ldd)
                    for gi, (c0, n, isc) in enumerate(groups):
                        o = c0 - t0
                        py = PS[5 + ((i * 2 + gi) % 2)]
                        for j in range(FC):
                            k.mm(py[:, :n], wd[:, j, :], hid[:, j, o:o + n], start=(j == 0), stop=(j == FC - 1))
                        k.copy(k.act, ybuf[:, i, o:o + n], py[:, :n])
                for (c0, n, isc) in groups:
                    cl = CLS_CTX if isc else b
                    o = c0 - t0
                    norm_stats(lambda c: ybuf[:, c, o:o + n], n, PS[0], sq, rstd[:, :n])
                    for c in range(KC):
                        tb = tmp[:, c % 2, :n]
                        k.stt(tb, ybuf[:, c, o:o + n], tabG[:, l, kk, cl, c:c + 1], rstd[:, :n], ALU.mult, ALU.mult)
                        k.tt(k.dve, Rv(c0, n)[:, c, c0:c0 + n], Rv(c0, n)[:, c, c0:c0 + n], tb, ALU.add)
            k.barrier()
        k.es = es


    def wtile(pool, l, u, nu=1):
        def ld(s_, dma):
            if nu == 1:
                dma(s_[:, :, :], wreg(wi_d, l, 0)[l, u, :, :, :])
            else:
                for i in range(nu):
                    dma(s_[:, i, :, :], wreg(wi_d, l, 0)[l, u + i, :, :, :])
        return pool.load(ld)

    def mixer_phase(l, b):
        MT = 256
        NKT = NT // 128
        with contextlib.ExitStack() as es2:
            k.es = es2
            Kst = k.sbuf("Kst", [128, 2, NT], BF16, slotted=True)
            Vst = k.sbuf("Vst", [128, 2, NKT, 128], BF16, slotted=True)
            Hf = k.sbuf("Hf", [128, 4, NT], BF16, slotted=True)
            hnb = k.sbuf("hnb", [128, KC, MT + 4], BF16, slotted=True)
            sq = k.sbuf("msq", [128, 2, MT + 8], BF16, slotted=True)
            rstd = k.sbuf("mrstd", [128, MT + 8], F32)
            tmp = k.sbuf("mtmp", [128, 2, MT + 8], F32, slotted=True)
            ropeC = k.sbuf("ropeC", [128, MT], F32)
            ropeS = k.sbuf("ropeS", [128, MT], F32)
            stage = k.sbuf("stage", [128, 2, MT + 4], F32, slotted=True)
            qk = k.sbuf("qk", [128, 8, MT], BF16, slotted=True)
            Vm = k.sbuf("Vm", [128, MT // 128, 4, 129], BF16, slotted=True)
            gsbuf = k.sbuf("gsbuf", [128, MT // 128, 16], F32)
            lfb = k.sbuf("lfb", [128, MT // 128, 4], F32)
            sm = k.sbuf("sm", [128, 32], F32)
            lfrep = k.sbuf("lfrep", [128, 2, 128], F32, slotted=True)
            DT = k.sbuf("DT", [128, 4, 128], BF16, slotted=True)
            ebt = k.sbuf("ebt", [128, 4, 128], F32, slotted=True)
            qp = k.sbuf("qp", [128, 4, 128], BF16, slotted=True)
            STm = k.sbuf("STm", [128, 4, 128], BF16, slotted=True)
            kw = k.sbuf("kw", [128, 4, 128], BF16, slotted=True)
            dd = k.sbuf("dd", [128, 4, 128], F32, slotted=True)
            St = k.sbuf("St", [128, 4, 129], F32, slotted=True)
            Cbf = k.sbuf("Cbf", [128, 4, 128], BF16, slotted=True)
            nbc = k.sbuf("nbc", [128, 4, 128], BF16, slotted=True)
            carry = k.sbuf("carry", [128, 8, 2], F32)
            rawA = k.sbuf("rawA", [128, KC * MT], F32)
            hsum = Buf(rawA.t[:, 0:4 * MT].rearrange("p (h t) -> p h t", h=4), "hsum")
            qa = Buf(rawA.t[:, 4 * MT:6 * MT].bitcast(BF16).rearrange("p (h t) -> p h t", h=4), "qa")
            qb = Buf(rawA.t[:, 6 * MT:8 * MT].bitcast(BF16).rearrange("p (h t) -> p h t", h=4), "qb")
            ybuf = Buf(rawA.t[:, :].rearrange("p (c t) -> p c t", c=KC), "mybuf")
            for bb in (hsum, qa, qb, ybuf):
                bb.h = rawA.h
            yy = k.sbuf("yy", [128, 3, 4, MT], BF16, slotted=True)
            Pt = k.sbuf("Pt", [128, 4, MT], BF16, slotted=True)
            Pm = k.sbuf("Pm", [128, 4, MT], BF16, slotted=True)
            rden = k.sbuf("rden", [128, 2, MT], F32, slotted=True)
            sig = k.sbuf("sig", [128, 2, MT], F32, slotted=True)
            gz = k.sbuf("gz", [128, 2, MT], F32, slotted=True)
            msum = k.sbuf("msum", [128, KC, MT], BF16, slotted=True)
            wq_p = WPool(k, "wq", [128, KC, 128], 4)
            wv_p = WPool(k, "wv", [128, 2, KC, 128], 2)
            wb_p = WPool(k, "wb", [128, 4, 128], 3)
            wo_p = WPool(k, "wo", [128, KC, 128], 2)
            k.memset(k.dve, Vm[:, :, :, 128:129], 1.0)

            tiles = [(0, CTX, True)] + [(CTX + i * MT, MT, False) for i in range(T // MT)]
            skip_ctx_out = (l == L - 1)

            def seqb(isc):
                return (0, CTX) if isc else (CTX, NT)

            def bc(v, shape, axis):
                return V(v.ap.unsqueeze(axis).broadcast_to(shape), v.hs)

            def norm_tile(t0, n, isc, lo, hi):
                cl = CLS_CTX if isc else b
                m = hi - lo
                o = lo - (t0 - 2)
                norm_stats(lambda c: Rv(lo, m)[:, c, lo:hi], m, PS[0], sq, rstd[:, :m])
                for c in range(KC):
                    tb = tmp[:, c % 2, :m]
                    k.stt(tb, Rv(lo, m)[:, c, lo:hi], tabA[:, l, 1, cl, c:c + 1], rstd[:, :m], ALU.mult, ALU.mult)
                    k.actf(hnb[:, c, o:o + m], tb, AF.Identity, bias=tabB[:, l, 1, cl, c:c + 1])

            def rope_tile(t0, n, isc):
                if isc:
                    k.memset(k.dve, ropeC[:, :n], 1.0)
                    k.memset(k.dve, ropeS[:, :n], 0.0)
                    return
                r0 = (t0 - CTX) // 64
                nr = n // 64
                for tab, oi in ((ropeC, 0), (ropeS, 2)):
                    a0 = bc(ropec[:, oi, r0:r0 + nr], [128, nr, 64], 2)
                    a1 = bc(ropec[:, oi + 1, :], [128, nr, 64], 1)
                    outv = tab[:, :n]
                    k.tt(k.dve, V(outv.ap.rearrange("p (r c) -> p r c", c=64), outv.hs), a0, a1, ALU.add)

            def proj_fm(wt, o, n, ps):
                for c in range(KC):
                    k.mm(ps[:, :n], wt[:, c, :], hnb[:, c, o:o + n], start=(c == 0), stop=(c == KC - 1))

            def qk_post(ps, n, gcol, dst):
                xn = tmp[:, 0, :n]
                if gcol is not None:
                    k.actf(sq[:, 0, :n], ps[:, :n], AF.Square)
                    k.mm(PS[1][:, :n], cbf[:, 1, :], sq[:, 0, :n])
                    k.actf(rstd[:, :n], PS[1][:, :n], AF.Ln, bias=epsb[:, 0:1], scale=1.0 / 64)
                    k.actf(rstd[:, :n], rstd[:, :n], AF.Exp, scale=-0.5)
                    k.stt(xn, ps[:, :n], qkg[:, l, gcol:gcol + 1], rstd[:, :n], ALU.mult, ALU.mult)
                else:
                    k.copy(k.act, xn, ps[:, :n])
                k.copy(k.act, sq[:, 1, :n], xn)
                k.mm(PS[1][:, :n], cbf[:, 2, :], sq[:, 1, :n])
                t2 = tmp[:, 1, :n]
                k.tt(k.dve, t2, PS[1][:, :n], ropeS[:, :n], ALU.mult)
                k.tt(k.dve, xn, xn, ropeC[:, :n], ALU.mult)
                k.tt(k.dve, dst, xn, t2, ALU.add)

            def mlstm_tile(t0, n, isc, dirn, use_carry):
                s0, s1 = seqb(isc)
                lo = max(t0 - 2, s0)
                hi = t0 + n if use_carry else min(t0 + n + 2, s1)
                nblk = n // 128
                for ch in range(8):
                    wt = wtile(wq_p, l, U_QM + ch)
                    ps = PS[2 + ch % 2]
                    m = hi - lo
                    o = lo - (t0 - 2)
                    proj_fm(wt, o, m, ps)
                    if o > 0:
                        k.memset(k.dve, stage[:, ch % 2, 0:o], 0.0)
                    k.copy(k.act, stage[:, ch % 2, o:o + m], ps[:, :m])
                    if use_carry:
                        if t0 + n >= s1:
                            k.memset(k.dve, stage[:, ch % 2, n + 2:n + 4], 0.0)
                        else:
                            k.copy(k.dve, stage[:, ch % 2, n + 2:n + 4], carry[:, ch, :])
                    elif o + m < n + 4:
                        k.memset(k.dve, stage[:, ch % 2, o + m:n + 4], 0.0)
                    acc = tmp[:, ch % 2, :n]
                    k.ts(k.dve, acc, stage[:, ch % 2, 0:n], convw[:, l, ch, 0:1], convb[:, l, ch:ch + 1], ALU.mult, ALU.add)
                    for j in range(1, 5):
                        k.stt(acc, stage[:, ch % 2, j:j + n], convw[:, l, ch, j:j + 1], acc, ALU.mult, ALU.add)
                    if use_carry:
                        k.copy(k.dve, carry[:, ch, :], stage[:, ch % 2, 2:4])
                    k.actf(qk[:, ch, :n], acc, AF.Silu)
                for half in range(2):
                    wt = wtile(wv_p, l, U_VM + 2 * half, 2)
                    for bi in range(nblk):
                        o = 2 + bi * 128
                        for c in range(KC):
                            k.mm(PS[2][:, :256], hnb[:, c, o:o + 128], wt[:, :, c, :], start=(c == 0), stop=(c == KC - 1))
                        pv = PS[2][:, :256]
                        k.copy(k.act, Vm[:, bi, 2 * half:2 * half + 2, 0:128],
                               V(pv.ap.rearrange("p (h d) -> p h d", h=2), pv.hs))
                wg = wtile(wq_p, l, U_GM)
                for bi in range(nblk):
                    o = 2 + bi * 128
                    for c in range(KC):
                        k.mm(PS[3][:, :16], hnb[:, c, o:o + 128], wg[:, c, 0:16], start=(c == 0), stop=(c == KC - 1))
                    k.tt(k.dve, gsbuf[:, bi, :], PS[3][:, :16], gateb[:, l, :], ALU.add)
                ic = 0 if dirn > 0 else 8
                fc = ic + 4
                k.actf(lfb[:, :nblk, :], gsbuf[:, :nblk, fc:fc + 4], AF.Exp, scale=-1.0)
                k.actf(lfb[:, :nblk, :], lfb[:, :nblk, :], AF.Ln, bias=1.0)
                k.ts(k.dve, lfb[:, :nblk, :], lfb[:, :nblk, :], -1.0, None, ALU.mult)
                blks = range(nblk) if dirn > 0 else range(nblk - 1, -1, -1)
                corder = (0, 1) if dirn > 0 else (1, 0)
                PSN = [PS[2], PS[3], PS[5], PS[6]]
                mskb = cbf[:, 4, :] if dirn > 0 else cbf[:, 5, :]
                for bi in blks:
                    c0 = t0 + bi * 128
                    bo = bi * 128
                    k.mm(PS[4][:, 0:4], TRI[dirn](), lfb[:, bi, :])
                    k.mm(PS[4][:, 8:12], cf[:, 4, :], lfb[:, bi, :])
                    k.tt(k.dve, sm[:, 0:4], gsbuf[:, bi, ic:ic + 4], PS[4][:, 0:4], ALU.subtract)
                    k.tt(k.dve, sm[:, 4:8], sm[:, 0:4], PS[4][:, 8:12], ALU.add)
                    k.actf(sm[:, 4:8], sm[:, 4:8], AF.Exp)
                    k.ts(k.dve, sm[:, 8:12], sm[:, 0:4], lnsc[:, 0:1], None, ALU.add)
                    for h in range(4):
                        lfr = lfrep[:, h % 2, :]
                        k.ts(k.dve, lfr, cf[:, 5, :], lfb[:, bi, h:h + 1], None, ALU.mult)
                        k.mm(PS[5][:, 128 * h:128 * h + 128], lfr, TRI[dirn]())
                        k.mm(PS[4][:, 16 + 2 * h:18 + 2 * h], lfr, cf[:, 6, 0:2])
                    k.actf(sm[:, 16:24], PS[4][:, 16:24], AF.Exp)
                    for h in range(4):
                        k.actf(DT[:, h, :], PS[5][:, 128 * h:128 * h + 128], AF.Exp, bias=sm[:, 8 + h:9 + h])
                    pb = PS[5][:, 0:512]
                    k.actf(ebt[:, :, :], V(pb.ap.rearrange("p (h t) -> p h t", h=4), pb.hs), AF.Exp, bias=lnsc[:, 0:1])
                    k.tt(k.dve, qp[:, :, :], qk[:, 0:4, bo:bo + 128], ebt[:, :, :], ALU.mult)
                    for h in range(4):
                        k.mm(PS[6][:, 128 * h:128 * h + 128], qk[:, 4 + h, bo:bo + 128], qk[:, h, bo:bo + 128])
                    pst = PS[6][:, 0:512]
                    k.tt(k.dve, STm[:, :, :], V(pst.ap.rearrange("p (h t) -> p h t", h=4), pst.hs), DT[:, :, :], ALU.mult)
                    k.tt(k.dve, STm[:, :, :], STm[:, :, :], bc(mskb, [128, 4, 128], 1), ALU.mult)
                    for h in range(4):
                        k.transpose(PSB[:, 128 * h:128 * h + 128], qk[:, 4 + h, bo:bo + 128], cbf[:, 3, :])
                    ptr = PSB[:, 0:512]
                    k.tt(k.dve, kw[:, :, :], V(ptr.ap.rearrange("p (h t) -> p h t", h=4), ptr.hs),
                         bc(sm[:, 4:8], [128, 4, 128], 2), ALU.mult)
                    sfl = STm[:, :, :]
                    k.mm(PS[0][:, 0:512], cbf[:, 0, :], V(sfl.ap.rearrange("p h t -> p (h t)"), sfl.hs), start=True, stop=False)
                    for h in range(4):
                        k.mm(PSN[h][:, 0:128], Vm[:, bi, h, 0:128], STm[:, h, :], start=True, stop=False)
                    for ci, cc in enumerate(corder):
                        cs = slice(cc * 64, cc * 64 + 64)
                        for h in range(4):
                            last = ci == 1
                            k.mm(PSN[h][:, cs], Cbf[:, h, :], qp[:, h, cs], start=False, stop=last)
                            k.mm(PS[0][:, 128 * h + cc * 64:128 * h + cc * 64 + 64], nbc[:, h, :], qp[:, h, cs],
                                 start=False, stop=(last and h == 3))
                            pu_ = PS[1] if h % 2 == 0 else PS[4]
                            k.mm(pu_[:, 0:129], kw[cs, h, :], Vm[cs, bi, h, :])
                            k.stt(St[:, h, :], St[:, h, :], sm[:, 16 + 2 * h + cc:17 + 2 * h + cc], pu_[:, 0:129],
                                  ALU.mult, ALU.add)
                            k.copy(k.act, Cbf[:, h, :], St[:, h, 0:128])
                            k.ts(k.dve, nbc[:, h, :], cf[:, 5, :], St[:, h, 128:129], None, ALU.mult)
                    pdn = PS[0][:, 0:512]
                    k.actf(dd[:, :, :], V(pdn.ap.rearrange("p (h t) -> p h t", h=4), pdn.hs), AF.Abs)
                    k.ts(k.dve, dd[:, :, :], dd[:, :, :], 1.0, None, ALU.max)
                    k.actf(dd[:, :, :], dd[:, :, :], AF.Ln)
                    k.actf(dd[:, :, :], dd[:, :, :], AF.Exp, scale=-1.0)
                    for h in range(4):
                        if dirn > 0:
                            k.tt(k.dve, Hf[:, h, c0:c0 + 128], PSN[h][:, 0:128], dd[:, h, :], ALU.mult)
                        else:
                            k.tt(k.dve, hsum[:, h, bo:bo + 128], PSN[h][:, 0:128], dd[:, h, :], ALU.mult)
                    if dirn < 0:
                        k.tt(k.dve, hsum[:, :, bo:bo + 128], hsum[:, :, bo:bo + 128], Hf[:, :, c0:c0 + 128], ALU.add)

            def attention(t0, n, isc, mixer):
                qsrc = qa if mixer == 0 else qb
                nct = CTX // 128
                kts = [(j, 0, n, None) for j in range(nct)]
                if not isc:
                    ql0 = t0 - CTX
                    if mixer == 1:
                        kts += [(nct + j, 0, n, None) for j in range(T // 128)]
                    else:
                        for j in range(ql0 // 128 - 1, (ql0 + n) // 128 + 1):
                            if j < 0 or j >= T // 128:
                                continue
                            a = max(128 * (j - 1), ql0)
                            e = min(128 * (j + 2), ql0 + n)
                            kts.append((nct + j, a - ql0, e - ql0, a - (128 * j - 128)))
                pairs = [(c, ki) for c in range(4) for ki in range(len(kts))]
                nk = len(kts)
                PSS = [PS[0], PS[1], PS[6]]

                def stage_a(pi):
                    c, ki = pairs[pi]
                    kt, a, e, mo = kts[ki]
                    m = e - a
                    out = []
                    for hf in range(2):
                        idx = 2 * pi + hf
                        pr = slice(64 * hf, 64 * hf + 64)
                        pss = PSS[idx % 3]
                        k.mm(pss[:, :m], Kst[pr, mixer, kt * 128:(kt + 1) * 128], qsrc[pr, c, a:e])
                    for hf in range(2):
                        idx = 2 * pi + hf
                        pss = PSS[idx % 3]
                        pt = Pt[:, idx % 4, :m]
                        k.actf(pt, pss[:, :m], AF.Exp, scale=0.125)
                        if mo is not None:
                            pm = Pm[:, idx % 4, :m]
                            k.tt(k.dve, pm, pt, winmask[:, mo:mo + m], ALU.mult)
                            pt = pm
                        out.append(pt)
                    return out

                def stage_b(pi, pts_):
                    c, ki = pairs[pi]
                    kt, a, e, mo = kts[ki]
                    pn, pd = PS[2 + 2 * (c % 2)], PS[3 + 2 * (c % 2)]
                    lastk = ki == nk - 1
                    for hf in range(2):
                        pr = slice(64 * hf, 64 * hf + 64)
                        k.mm(pn[pr, a:e], Vst[:, mixer, kt, hf * 64:(hf + 1) * 64], pts_[hf], start=(ki == 0), stop=lastk)
                    for hf in range(2):
                        pr = slice(64 * hf, 64 * hf + 64)
                        k.mm(pd[pr, a:e], cbf[:, 0, 0:64], pts_[hf], start=(ki == 0), stop=lastk)
                    if lastk:
                        for hf in range(2):
                            h = c + 4 * hf
                            pr = slice(64 * hf, 64 * hf + 64)
                            rd = rden[pr, hf, :n]
                            if mixer == 0:
                                k.actf(rd, pd[pr, :n], AF.Ln, bias=esink[pr, l, h:h + 1])
                            else:
                                k.actf(rd, pd[pr, :n], AF.Ln)
                            k.actf(rd, rd, AF.Exp, scale=-1.0)
                            k.tt(k.dve, yy[pr, mixer, c, :n], pn[pr, :n], rd, ALU.mult)

                held = {}
                for pi in range(len(pairs) + 1):
                    if pi < len(pairs):
                        held[pi] = stage_a(pi)
                    if pi >= 1:
                        stage_b(pi - 1, held.pop(pi - 1))

            k.memset(k.dve, St[:, :, :], 0.0)
            k.memset(k.dve, Cbf[:, :, :], 0.0)
            k.memset(k.dve, nbc[:, :, :], 0.0)
            for (t0, n, isc) in tiles:
                s0, s1 = seqb(isc)
                lo, hi = max(t0 - 2, s0), min(t0 + n + 2, s1)
                norm_tile(t0, n, isc, lo, hi)
                rope_tile(t0, n, isc)
                for mixer in range(2):
                    wt = wtile(wq_p, l, U_KA if mixer == 0 else U_KB)
                    ps = PS[2 + mixer]
                    proj_fm(wt, 2, n, ps)
                    qk_post(ps, n, (1 if mixer == 1 else None), Kst[:, mixer, t0:t0 + n])
                wt = wtile(wv_p, l, U_VAB, 2)
                for bi in range(n // 128):
                    o = 2 + bi * 128
                    for c in range(KC):
                        k.mm(PS[4][:, :256], hnb[:, c, o:o + 128], wt[:, :, c, :], start=(c == 0), stop=(c == KC - 1))
                    pv = PS[4][:, :256]
                    kt = (t0 + bi * 128) // 128
                    k.copy(k.act, Vst[:, :, kt, :], V(pv.ap.rearrange("p (m d) -> p m d", m=2), pv.hs))
                mlstm_tile(t0, n, isc, 1, False)

            k.memset(k.dve, St[:, :, :], 0.0)
            k.memset(k.dve, Cbf[:, :, :], 0.0)
            k.memset(k.dve, nbc[:, :, :], 0.0)
            tiles2 = [tiles[0]] + tiles[:0:-1]
            for (t0, n, isc) in tiles2:
                cl = CLS_CTX if isc else b
                s0, s1 = seqb(isc)
                lo = max(t0 - 2, s0)
                norm_tile(t0, n, isc, lo, t0 + n)
                mlstm_tile(t0, n, isc, -1, True)
                if isc and skip_ctx_out:
                    continue
                rope_tile(t0, n, isc)
                for h in range(4):
                    k.actf(sq[:, h % 2, :n], hsum[:, h, :n], AF.Square)
                    k.mm(PS[0][:, :n], cbf[:, 0, :], sq[:, h % 2, :n])
                    k.actf(rstd[:, :n], PS[0][:, :n], AF.Ln, bias=epsb[:, 0:1], scale=1.0 / 128)
                    k.actf(rstd[:, :n], rstd[:, :n], AF.Exp, scale=-0.5)
                    wt = wtile(wq_p, l, U_OM + h)
                    proj_fm(wt, 2, n, PS[1])
                    k.actf(sig[:, h % 2, :n], PS[1][:, :n], AF.Sigmoid)
                    tb = tmp[:, h % 2, :n]
                    k.stt(tb, hsum[:, h, :n], mgs[:, l, h:h + 1], rstd[:, :n], ALU.mult, ALU.mult)
                    k.tt(k.dve, yy[:, 2, h, :n], tb, sig[:, h % 2, :n], ALU.mult)
                for mixer in range(2):
                    for c in range(4):
                        wt = wtile(wq_p, l, (U_QA if mixer == 0 else U_QB) + c)
                        ps = PS[2 + c % 2]
                        proj_fm(wt, 2, n, ps)
                        qk_post(ps, n, (0 if mixer == 1 else None), (qa if mixer == 0 else qb)[:, c, :n])
                attention(t0, n, isc, 0)
                attention(t0, n, isc, 1)
                for i in range(KC):
                    for br in range(3):
                        def ldb(s_, dma, br=br, i=i):
                            dma(s_[:, :, :], wreg(wb_d, l, 0)[l, br, i, :, :, :])
                        wb = wb_p.load(ldb)
                        pz = PS[2 + br % 2]
                        for c in range(4):
                            k.mm(pz[:, :n], wb[:, c, :], yy[:, br, c, :n], start=(c == 0), stop=(c == 3))
                        wt = wtile(wq_p, l, U_GL + br * 8 + i)
                        pg = PS[4 + br % 2]
                        proj_fm(wt, 2, n, pg)
                        sg_ = sig[:, br % 2, :n]
                        k.actf(sg_, pg[:, :n], AF.Sigmoid)
                        if br == 0:
                            k.tt(k.dve, gz[:, 0, :n], sg_, pz[:, :n], ALU.mult)
                        else:
                            k.tt(k.dve, gz[:, 1, :n], sg_, pz[:, :n], ALU.mult)
                            dst = msum[:, i, :n] if br == 2 else gz[:, 0, :n]
                            k.tt(k.dve, dst, gz[:, 0, :n], gz[:, 1, :n], ALU.add)
                for o_ in range(KC):
                    def ldo(s_, dma, o_=o_):
                        dma(s_[:, :, :], wreg(wo_d, l, 0)[l, o_, :, :, :])
                    wo = wo_p.load(ldo)
                    py = PS[2 + o_ % 2]
                    for i in range(KC):
                        k.mm(py[:, :n], wo[:, i, :], msum[:, i, :n], start=(i == 0), stop=(i == KC - 1))
                    k.copy(k.act, ybuf[:, o_, :n], py[:, :n])
                norm_stats(lambda c: ybuf[:, c, :n], n, PS[0], sq, rstd[:, :n])
                for c in range(KC):
                    tb = tmp[:, c % 2, :n]
                    k.stt(tb, ybuf[:, c, :n], tabG[:, l, 1, cl, c:c + 1], rstd[:, :n], ALU.mult, ALU.mult)
                    k.tt(k.dve, Rv(t0, n)[:, c, t0:t0 + n], Rv(t0, n)[:, c, t0:t0 + n], tb, ALU.add)
            k.barrier()
        k.es = es

    for b in range(NB):
        k.dma(k.sp, R.rs(range(CTX // 256))[:, :, 0:CTX], ctxT[b, :, :, :], sem_in)
        k.dma(k.sp, R.rs(range(CTX // 256, NRB))[:, :, CTX:NT], xT[b, :, :, :], sem_in)
        for l in range(L):
            ffn_phase(l, 0, b)
            if cfg.stop_after == "ffn1":
                break
            mixer_phase(l, b)
            if cfg.stop_after == "mixer":
                break
            ffn_phase(l, 2, b)
        k.dma(k.sp, outT[b, :, :, :], R.rs(range(CTX // 256, NRB))[:, :, CTX:NT], sem_out)
    k.final_wait(k.sp, [sem_out])
    return nc, es, k


def make_consts():
    c = np.zeros((128, 8, 128), np.float32)
    c[:, 0, :] = 1.0
    p = np.arange(128)
    c[:, 1, :] = (p[:, None] // 64 == p[None, :] // 64)
    c[:, 2, :] = (p[:, None] == (p[None, :] ^ 16))
    c[:, 3, :] = np.eye(128)
    same = (p[:, None] // 64 == p[None, :] // 64)
    c[:, 4, :] = same & (p[:, None] <= p[None, :])
    c[:, 5, :] = same & (p[:, None] >= p[None, :])
    return c


def make_consts_f():
    c = np.zeros((128, 8, 128), np.float32)
    p = np.arange(128)
    same = (p[:, None] // 64 == p[None, :] // 64)
    c[:, 0, :] = same & (p[:, None] <= p[None, :])
    c[:, 1, :] = same & (p[:, None] >= p[None, :])
    c[:, 2, :] = np.where(c[:, 0, :] > 0, 0.0, NEG)
    c[:, 3, :] = np.where(c[:, 1, :] > 0, 0.0, NEG)
    c[:, 4, :] = same
    c[:, 5, :] = 1.0
    c[:, 6, 0] = p < 64
    c[:, 6, 1] = p >= 64
    return c


def make_winmask():
    s_ = np.arange(128)[:, None]
    q = np.arange(384)[None, :]
    return (np.abs(q - 128 - s_) <= 128).astype(np.float32)


def make_rope(T):
    rows = T // GRID_W
    freqs = (np.float32(10000.0) ** (-np.arange(16, dtype=np.float32) / np.float32(16))).astype(np.float32)
    r = np.zeros((128, 4, 64), np.float32)
    for p in range(128):
        d = p % 64
        axis, half, pair = d // 32, (d % 32) // 16, d % 16
        sign = -1.0 if half == 0 else 1.0
        if axis == 0:
            ang = (np.arange(rows, dtype=np.float32) * freqs[pair]).astype(np.float32)
            r[p, 0, :rows] = np.cos(ang)
            r[p, 2, :rows] = sign * np.sin(ang)
        else:
            ang = (np.arange(64, dtype=np.float32) * freqs[pair]).astype(np.float32)
            r[p, 1, :] = np.cos(ang)
            r[p, 3, :] = sign * np.sin(ang)
    return r


def to_bf16(a):
    import ml_dtypes
    return a.astype(ml_dtypes.bfloat16)


def fm(a):
    a = np.swapaxes(a, -1, -2)
    sh = a.shape
    a = a.reshape(sh[:-2] + (KC, 128, sh[-1]))
    return np.ascontiguousarray(np.swapaxes(a, -3, -2))


def vec_t(a, nchunk):
    sh = a.shape
    a = a.reshape(sh[:-1] + (nchunk, 128))
    return np.ascontiguousarray(np.moveaxis(a, -1, 0))


def host_inputs(cfg, inp, core):
    NB = cfg.NB
    bs = slice(core * NB, (core + 1) * NB)
    L = cfg.L
    m = {}
    m["xT"] = fm(inp["x"][bs])
    m["ctxT"] = fm(inp["ctx"][bs])
    cc = np.concatenate([inp["c"][bs], inp["c_ctx"][None, :]], axis=0)
    m["c_in"] = np.ascontiguousarray(cc.T.reshape(KC, 128, NB + 1).transpose(1, 0, 2))
    m["w_ada"] = inp["w_ada"][:L]
    m["b_ada_t"] = vec_t(inp["b_ada"][:L], 72)
    m["norm_g_t"] = vec_t(inp["norm_g"][:L], KC)
    m["ffn_w_gate"] = inp["ffn_w_gate"][:L]
    m["ffn_w_up"] = inp["ffn_w_up"][:L]
    m["ffn_w_down"] = inp["ffn_w_down"][:L]
    m["w_in"] = inp["w_in"][:L]
    m["w_branch"] = inp["w_branch"][:L]
    m["w_out"] = inp["w_out"][:L]
    m["consts_bf"] = make_consts()
    m["consts_f"] = make_consts_f()
    m["winmask"] = make_winmask()
    m["rope"] = make_rope(cfg.T)
    cw = inp["conv_w"][:L]
    m["convw_t"] = np.ascontiguousarray(cw.reshape(L, 5, 8, 128).transpose(3, 0, 2, 1))
    m["convb_t"] = vec_t(inp["conv_b"][:L], 8)
    m["mg_t"] = vec_t(inp["mlstm_norm_g"][:L], 4)
    qg = inp["qk_norm_g"][:L]
    m["qkg_t"] = np.ascontiguousarray(np.tile(qg, (1, 1, 2)).transpose(2, 0, 1))
    m["sink_t"] = np.ascontiguousarray(np.broadcast_to(inp["attn_sink"][:L][None], (128, L, 8)))
    m["gateb_t"] = np.ascontiguousarray(np.broadcast_to(inp["mlstm_gate_b"][:L][None], (128, L, 16)))
    return m


_CACHE = {}


def run(cfg, inputs):
    key = (cfg.T, cfg.CTX, cfg.L, cfg.NB, cfg.n_cores, cfg.stop_after)
    if key not in _CACHE:
        _CACHE[key] = build_program(cfg)
    nc, es, kk = _CACHE[key]
    inp = {n: np.asarray(v) for n, v in inputs.items()}
    in_maps = [host_inputs(cfg, inp, core) for core in range(cfg.n_cores)]
    res = run_bass_kernel_spmd(nc, in_maps, core_ids=list(range(cfg.n_cores)))
    outs = []
    for core in range(cfg.n_cores):
        o = res.results[core]["outT"]
        o = np.swapaxes(o, 1, 2).reshape(cfg.NB, D, cfg.T)
        outs.append(np.swapaxes(o, 1, 2))
    return np.ascontiguousarray(np.concatenate(outs, axis=0)).astype(np.float32)


def kernel(**inputs):
    cfg = Cfg()
    return run(cfg, inputs)
```

```python
import contextlib
import numpy as np
import concourse.bass as bass
import concourse.mybir as mybir
from concourse.bass_utils import run_bass_kernel_spmd

F32 = mybir.dt.float32
BF16 = mybir.dt.bfloat16
AF = mybir.ActivationFunctionType
ALU = mybir.AluOpType

D = 1024
KC = 8
FF = 2816
FC = 22
GRID_W = 64
HD = 64
IN_W = 6672
O_QA, O_KA, O_VA, O_QB, O_KB, O_VB = 0, 512, 640, 768, 1280, 1408
O_QM, O_KM, O_VM, O_OM, O_GM, O_GL = 1536, 2048, 2560, 3072, 3584, 3600
EPS = 1e-6
NEG = -30000.0


class Cfg:
    def __init__(self, T=2048, CTX=256, L=4, NB=2, n_cores=8, stop_after=None):
        self.T, self.CTX, self.L, self.NB, self.n_cores = T, CTX, L, NB, n_cores
        self.NT = T + CTX
        self.stop_after = stop_after


class Sem:
    def __init__(self, h):
        self.h = h
        self.count = 0


class H:
    __slots__ = ("name", "w", "r", "excl")

    def __init__(self, name=""):
        self.name = name
        self.excl = False
        self.w = None
        self.r = {}


class V:
    __slots__ = ("ap", "hs")

    def __init__(self, ap, hs):
        self.ap = ap
        self.hs = hs


class Buf:
    def __init__(self, t, name, slots=None):
        self.t = t
        self.name = name
        self.h = H(name)
        self.regs = {}
        self.slots = slots

    def _slot(self, j):
        key = ("slot", j)
        if key not in self.regs:
            self.regs[key] = H(f"{self.name}[{j}]")
        return self.regs[key]

    def __getitem__(self, idx):
        if self.slots:
            i1 = idx[1] if isinstance(idx, tuple) and len(idx) > 1 else slice(None)
            if isinstance(i1, int):
                hs = [self._slot(i1)]
            else:
                hs = [self._slot(j) for j in range(*i1.indices(self.slots))]
            return V(self.t[idx], hs)
        return V(self.t[idx], [self.h])

    def r(self, key):
        if key not in self.regs:
            self.regs[key] = H(f"{self.name}.{key}")
        return _Reg(self, [self.regs[key]])

    def rs(self, keys):
        hs = []
        for key in keys:
            if key not in self.regs:
                self.regs[key] = H(f"{self.name}.{key}")
            hs.append(self.regs[key])
        return _Reg(self, hs)


class _Reg:
    def __init__(self, buf, hs):
        self.buf, self.hs = buf, hs

    def __getitem__(self, idx):
        return V(self.buf.t[idx], self.hs)


class Eng:
    def __init__(self, name, raw, sem, self_sync):
        self.name, self.raw, self.sem = name, raw, sem
        self.seen = {}
        self.self_sync = self_sync


class K:
    def __init__(self, nc, es):
        self.nc, self.es, self.es0 = nc, es, es
        self.n_inst = 0
        self.sem_by_name = {}
        mk = lambda n: Sem(es.enter_context(nc.semaphore(n)))
        self.pe = Eng("pe", nc.tensor, mk("s_pe"), False)
        self.act = Eng("act", nc.scalar, mk("s_act"), True)
        self.dve = Eng("dve", nc.vector, mk("s_dve"), True)
        self.pool = Eng("pool", nc.gpsimd, mk("s_pool"), True)
        self.sp = Eng("sp", nc.sync, mk("s_sp"), True)
        self.engs = [self.pe, self.act, self.dve, self.pool, self.sp]
        self.dma_sems = []

    def sbuf(self, name, shape, dt, slotted=False):
        self.n_buf = getattr(self, "n_buf", 0) + 1
        name = f"{name}_{self.n_buf}"
        return Buf(self.es.enter_context(self.nc.sbuf_tensor(name, list(shape), dt)), name,
                   slots=(shape[1] if slotted else None))

    def psum(self, name, shape, dt):
        b = Buf(self.es.enter_context(self.nc.psum_tensor(name, list(shape), dt)), name)
        b.h.excl = True
        return b

    def new_dma_sem(self, name):
        if name not in self.sem_by_name:
            s = Sem(self.es0.enter_context(self.nc.semaphore(name)))
            self.dma_sems.append(s)
            self.sem_by_name[name] = s
        return self.sem_by_name[name]

    def _wait(self, eng, deps):
        for sem, val in deps.items():
            if sem is eng.sem and not eng.self_sync:
                continue
            if eng.seen.get(sem, 0) < val:
                eng.raw.wait_ge(sem.h, val)
                eng.seen[sem] = val

    @staticmethod
    def _deps(reads, writes, own=None):
        deps = {}

        def add(ev):
            if ev is not None:
                s, v = ev
                if deps.get(s, 0) < v:
                    deps[s] = v
        for h in reads:
            add(h.w)
            if h.excl:
                for s, v in h.r.items():
                    if s is not own:
                        add((s, v))
        for h in writes:
            add(h.w)
            for s, v in h.r.items():
                add((s, v))
        return deps

    def emit(self, eng, reads, writes, fn):
        deps = self._deps(reads, writes, eng.sem)
        self._wait(eng, deps)
        inst = fn()
        inst.then_inc(eng.sem.h, 1)
        eng.sem.count += 1
        ev = (eng.sem, eng.sem.count)
        for h in reads:
            if h.r.get(ev[0], 0) < ev[1]:
                h.r[ev[0]] = ev[1]
        for h in writes:
            h.w = ev
            h.r = {}
        self.n_inst += 1
        return inst

    def dma(self, q, out, in_, sem, **kw):
        reads, writes = in_.hs, out.hs
        deps = self._deps(reads, writes)
        if sem.count:
            deps[sem] = max(deps.get(sem, 0), sem.count)
        self._wait(q, deps)
        inst = q.raw.dma_start(out=out.ap, in_=in_.ap, **kw)
        inst.then_inc(sem.h, 16)
        sem.count += 16
        ev = (sem, sem.count)
        for h in reads:
            if h.r.get(ev[0], 0) < ev[1]:
                h.r[ev[0]] = ev[1]
        for h in writes:
            h.w = ev
            h.r = {}
        self.n_inst += 1

    def barrier(self):
        deps = {}
        for e in self.engs:
            if e.sem.count:
                deps[e.sem] = e.sem.count
        for s in self.dma_sems:
            if s.count:
                deps[s] = s.count
        for e in self.engs:
            d = {s: v for s, v in deps.items() if s is not e.sem}
            self._wait(e, d)

    def final_wait(self, eng, sems):
        for s in sems:
            eng.raw.wait_ge(s.h, s.count)

    @staticmethod
    def _hs(*vs):
        out = []
        for v in vs:
            if isinstance(v, V):
                out += v.hs
        return out

    @staticmethod
    def _a(v):
        return v.ap if isinstance(v, V) else v

    def mm(self, out, lhsT, rhs, start=True, stop=True, **kw):
        return self.emit(self.pe, lhsT.hs + rhs.hs, out.hs,
                         lambda: self.nc.tensor.matmul(out.ap, lhsT.ap, rhs.ap, start=start, stop=stop, **kw))

    def transpose(self, out, in_, ident):
        return self.emit(self.pe, in_.hs + ident.hs, out.hs,
                         lambda: self.nc.tensor.transpose(out.ap, in_.ap, ident.ap))

    def actf(self, out, in_, func, bias=None, scale=None):
        kw = {}
        if bias is not None:
            kw["bias"] = self._a(bias)
        if scale is not None:
            kw["scale"] = self._a(scale)
        return self.emit(self.act, self._hs(in_, bias, scale), out.hs,
                         lambda: self.nc.scalar.activation(out=out.ap, in_=in_.ap, func=func, **kw))

    def tt(self, eng, out, in0, in1, op):
        return self.emit(eng, in0.hs + in1.hs, out.hs,
                         lambda: eng.raw.tensor_tensor(out=out.ap, in0=in0.ap, in1=in1.ap, op=op))

    def ts(self, eng, out, in0, s1, s2, op0, op1=None):
        kw = {}
        if op1 is not None:
            kw["op1"] = op1
        return self.emit(eng, self._hs(in0, s1, s2), out.hs,
                         lambda: eng.raw.tensor_scalar(out=out.ap, in0=in0.ap, scalar1=self._a(s1),
                                                       scalar2=self._a(s2), op0=op0, **kw))

    def stt(self, out, in0, scalar, in1, op0, op1):
        return self.emit(self.dve, self._hs(in0, scalar, in1), out.hs,
                         lambda: self.nc.vector.scalar_tensor_tensor(out=out.ap, in0=in0.ap, scalar=self._a(scalar),
                                                                     in1=in1.ap, op0=op0, op1=op1))

    def copy(self, eng, out, in_):
        if eng is self.act:
            return self.emit(eng, in_.hs, out.hs, lambda: self.nc.scalar.copy(out=out.ap, in_=in_.ap))
        return self.emit(eng, in_.hs, out.hs, lambda: eng.raw.tensor_copy(out=out.ap, in_=in_.ap))

    def recip(self, out, in_):
        return self.emit(self.dve, in_.hs, out.hs, lambda: self.nc.vector.reciprocal(out=out.ap, in_=in_.ap))

    def memset(self, eng, out, val):
        return self.emit(eng, [], out.hs, lambda: eng.raw.memset(out.ap, val))


class WPool:
    def __init__(self, k, name, shape, n, dt=BF16, q=None):
        self.k = k
        self.slots = [k.sbuf(f"{name}{i}", shape, dt) for i in range(n)]
        self.sems = [k.new_dma_sem(f"ws_{name}{i}") for i in range(n)]
        self.i = 0
        self.q = q

    def load(self, fn):
        s, sem = self.slots[self.i], self.sems[self.i]
        self.i = (self.i + 1) % len(self.slots)
        fn(s, lambda out, in_, **kw: self.k.dma(self.q or self.k.sp, out, in_, sem, **kw))
        return s


def col_groups(a, b, CTX, maxn=512):
    out = []
    c = a
    while c < b:
        lim = CTX if c < CTX else b
        n = min(maxn, lim - c, b - c)
        out.append((c, n, c < CTX))
        c += n
    return out


def build_program(cfg):
    nc = bass.Bass("TRN2", target_bir_lowering=False)
    es = contextlib.ExitStack()
    k = K(nc, es)
    T, CTX, L, NB, NT = cfg.T, cfg.CTX, cfg.L, cfg.NB, cfg.NT
    NCLS = NB + 1
    CLS_CTX = NB

    def din(name, shape, dt=F32):
        t = nc.dram_tensor(name, list(shape), dt, kind="ExternalInput")
        return Buf(t.ap(), name)

    xT = din("xT", [NB, 128, KC, T])
    ctxT = din("ctxT", [NB, 128, KC, CTX])
    c_in = din("c_in", [128, KC, NCLS])
    w_ada = din("w_ada", [L, D, 9 * D])
    b_ada_t = din("b_ada_t", [128, L, 72])
    norm_g_t = din("norm_g_t", [128, L, 6, KC])
    w_gate = din("ffn_w_gate", [L, 2, D, FF])
    w_up = din("ffn_w_up", [L, 2, D, FF])
    w_down = din("ffn_w_down", [L, 2, FF, D])
    w_in = din("w_in", [L, D, IN_W])
    w_branch = din("w_branch", [L, 3, 512, D])
    w_out = din("w_out", [L, D, D])
    consts_bf = din("consts_bf", [128, 8, 128])
    consts_f = din("consts_f", [128, 8, 128])
    winmask_d = din("winmask", [128, 384])
    rope_d = din("rope", [128, 4, 64])
    convw_t = din("convw_t", [128, L, 8, 5])
    convb_t = din("convb_t", [128, L, 8])
    mg_t = din("mg_t", [128, L, 4])
    qkg_t = din("qkg_t", [128, L, 2])
    sink_t = din("sink_t", [128, L, 8])
    gateb_t = din("gateb_t", [128, L, 16])
    outT_t = nc.dram_tensor("outT", [NB, 128, KC, T], F32, kind="ExternalOutput")
    outT = Buf(outT_t.ap(), "outT")

    def dint(name, shape):
        t = nc.dram_tensor(name, list(shape), BF16, kind="Internal")
        return Buf(t.ap(), name)

    UNITS = [[(O_KA, 128)], [(O_KB, 128)], [(O_VA, 128)], [(O_VB, 128)]]
    U_KA, U_KB, U_VAB = 0, 1, 2
    U_QA = len(UNITS)
    UNITS += [[(O_QA + 64 * c, 64), (O_QA + 64 * (c + 4), 64)] for c in range(4)]
    U_QB = len(UNITS)
    UNITS += [[(O_QB + 64 * c, 64), (O_QB + 64 * (c + 4), 64)] for c in range(4)]
    U_QM = len(UNITS)
    UNITS += [[(O_QM + 128 * h, 128)] for h in range(4)]
    UNITS += [[(O_KM + 128 * h, 128)] for h in range(4)]
    U_VM = len(UNITS)
    UNITS += [[(O_VM + 128 * h, 128)] for h in range(4)]
    U_OM = len(UNITS)
    UNITS += [[(O_OM + 128 * h, 128)] for h in range(4)]
    U_GM = len(UNITS)
    UNITS += [[(O_GM, 16)]]
    U_GL = len(UNITS)
    UNITS += [[(O_GL + 128 * g, 128)] for g in range(24)]
    NU = len(UNITS)
    wgu_d = dint("wgu_d", [L, 2, 2, FC // 2, 128, KC, 256])
    wdn_d = dint("wdn_d", [L, 2, KC, 128, FC, 128])
    wi_d = dint("wi_d", [L, NU, 128, KC, 128])
    wb_d = dint("wb_d", [L, 3, KC, 128, 4, 128])
    wo_d = dint("wo_d", [L, KC, 128, KC, 128])
    NTILE = 1 + T // 256
    qk_d = dint("qk_d", [NTILE, 128, 8, 256])
    vm_d = dint("vm_d", [NTILE, 128, 2, 4, 129])
    gs_t = nc.dram_tensor("gs_d", [NTILE, 128, 2, 16], F32, kind="Internal")
    gs_d = Buf(gs_t.ap(), "gs_d")
    sp_sems = {n_: k.new_dma_sem(n_) for n_ in ("st_qk", "st_vm", "st_gs", "ld_qk", "ld_vm", "ld_gs")}
    cv_sems = [k.new_dma_sem(f"cv{i}") for i in range(4)]
    cv_n = [0]

    def wreg(buf, l, grp):
        return buf.rs([(l, grp, q) for q in range(4)])

    def cv(buf, l, grp, dst_ap, src_v):
        q = cv_n[0] % 4
        cv_n[0] += 1
        k.dma(k.pool, V(dst_ap, buf.r((l, grp, q)).hs), src_v, cv_sems[q])

    def km(v):
        return V(v.ap.rearrange("(c p) n -> p c n", p=128), v.hs)

    def convert_ffn(l, which):
        for j2 in range(FC // 2):
            for g, wsrc in enumerate((w_gate, w_up)):
                cv(wgu_d, l, which, wgu_d.t[l, which, g, j2, :, :, :], km(wsrc[l, which, :, j2 * 256:(j2 + 1) * 256]))
        for i in range(KC):
            cv(wdn_d, l, which, wdn_d.t[l, which, i, :, :, :], km(w_down[l, which, :, i * 128:(i + 1) * 128]))

    def convert_mixer(l):
        for u, rngs in enumerate(UNITS):
            o = 0
            for (c0, n) in rngs:
                cv(wi_d, l, 0, wi_d.t[l, u, :, :, o:o + n], km(w_in[l, :, c0:c0 + n]))
                o += n
        for br in range(3):
            for c in range(4):
                if br < 2:
                    for hf in range(2):
                        r0 = 64 * (c + 4 * hf)
                        src = w_branch[l, br, r0:r0 + 64, :]
                        cv(wb_d, l, 0, wb_d.t[l, br, :, 64 * hf:64 * hf + 64, c, :].rearrange("i p n -> p i n"),
                           V(src.ap.rearrange("p (i n) -> p i n", n=128), src.hs))
                else:
                    src = w_branch[l, br, 128 * c:128 * c + 128, :]
                    cv(wb_d, l, 0, wb_d.t[l, br, :, :, c, :].rearrange("i p n -> p i n"),
                       V(src.ap.rearrange("p (i n) -> p i n", n=128), src.hs))
        for o_ in range(KC):
            cv(wo_d, l, 0, wo_d.t[l, o_, :, :, :], km(w_out[l, :, o_ * 128:(o_ + 1) * 128]))

    R = k.sbuf("R", [128, KC, NT], F32)
    NRB = NT // 256

    def Rv(c0, n):
        keys = list(range(c0 // 256, (c0 + n - 1) // 256 + 1))
        return R.rs(keys)

    gsb = k.sbuf("gsb", [128, L, 6, KC], F32)
    tabA = k.sbuf("tabA", [128, L, 3, NCLS, KC], F32)
    tabB = k.sbuf("tabB", [128, L, 3, NCLS, KC], F32)
    tabG = k.sbuf("tabG", [128, L, 3, NCLS, KC], F32)
    cbf = k.sbuf("cbf", [128, 8, 128], BF16)
    ONES = lambda: cbf[:, 0, :]
    epsb = k.sbuf("epsb", [128, 1], F32)
    cf = k.sbuf("cf", [128, 8, 128], F32)
    TRI = {1: (lambda: cf[:, 0, :]), -1: (lambda: cf[:, 1, :])}
    MSK = {1: (lambda: cf[:, 2, :]), -1: (lambda: cf[:, 3, :])}
    winmask = k.sbuf("winmask_sb", [128, 384], BF16)
    ropec = k.sbuf("ropec", [128, 4, 64], F32)
    convw = k.sbuf("convw", [128, L, 8, 5], F32)
    convb = k.sbuf("convb", [128, L, 8], F32)
    mgs = k.sbuf("mgs", [128, L, 4], F32)
    qkg = k.sbuf("qkg", [128, L, 2], F32)
    esink = k.sbuf("esink", [128, L, 8], F32)
    gateb = k.sbuf("gateb", [128, L, 16], F32)
    lnsc = k.sbuf("lnsc", [128, 1], F32)
    sem_in = k.new_dma_sem("sem_in")
    sem_c = k.new_dma_sem("sem_c")
    sem_cp = k.new_dma_sem("sem_cp")
    sem_out = k.new_dma_sem("sem_out")

    PS = [k.psum(f"ps{i}", [128, 512], F32) for i in range(7)]
    PSB = k.psum("psb", [128, 1024], BF16)

    k.dma(k.pool, cbf[:, :, :], consts_bf[:, :, :], sem_cp)
    k.dma(k.sp, gsb[:, :, :, :], norm_g_t[:, :, :, :], sem_c)
    k.memset(k.dve, epsb[:, :], EPS)
    k.memset(k.dve, lnsc[:, :], float(np.log(128.0 ** -0.5)))
    k.dma(k.sp, cf[:, :, :], consts_f[:, :, :], sem_c)
    k.dma(k.pool, winmask[:, :], winmask_d[:, :], sem_cp)
    k.dma(k.sp, ropec[:, :, :], rope_d[:, :, :], sem_c)
    k.dma(k.sp, convw[:, :, :, :], convw_t[:, :, :, :], sem_c)
    k.dma(k.sp, convb[:, :, :], convb_t[:, :, :], sem_c)
    k.dma(k.sp, mgs[:, :, :], mg_t[:, :, :], sem_c)
    k.dma(k.sp, qkg[:, :, :], qkg_t[:, :, :], sem_c)
    k.dma(k.sp, esink[:, :, :], sink_t[:, :, :], sem_c)
    k.dma(k.sp, gateb[:, :, :], gateb_t[:, :, :], sem_c)
    k.actf(esink[:, :, :], esink[:, :, :], AF.Exp)
    convert_ffn(0, 0)
    with contextlib.ExitStack() as es2:
        k.es = es2
        modraw = k.sbuf("modraw", [128, L, 72, NCLS], F32)
        csb = k.sbuf("csb", [128, KC, NCLS], F32)
        csig = k.sbuf("csig", [128, KC, NCLS], F32)
        cact = k.sbuf("cact", [128, KC, NCLS], BF16)
        bada = k.sbuf("bada", [128, L, 72], F32)
        wp = WPool(k, "wada", [128, KC, 512], 3, q=k.pool)
        k.dma(k.sp, csb[:, :, :], c_in[:, :, :], sem_c)
        k.dma(k.sp, bada[:, :, :], b_ada_t[:, :, :], sem_c)
        k.actf(csig[:, :, :], csb[:, :, :], AF.Sigmoid)
        k.tt(k.dve, cact[:, :, :], csb[:, :, :], csig[:, :, :], ALU.mult)
        for l in range(L):
            for n4 in range(18):
                src = w_ada[l, :, n4 * 512:(n4 + 1) * 512]

                def ld(s, dma, src=src):
                    dma(s[:, :, :], V(src.ap.rearrange("(c p) n -> p c n", p=128), src.hs))
                wt = wp.load(ld)
                ps = PS[n4 % 4]
                for j in range(4):
                    n = n4 * 4 + j
                    for c in range(KC):
                        k.mm(ps[:, j * 8:j * 8 + NCLS], wt[:, c, j * 128:(j + 1) * 128], cact[:, c, :],
                             start=(c == 0), stop=(c == KC - 1))
                for j in range(4):
                    n = n4 * 4 + j
                    k.ts(k.dve, modraw[:, l, n, :], ps[:, j * 8:j * 8 + NCLS], bada[:, l, n:n + 1], None, ALU.add)
        for l in range(L):
            for kk in range(3):
                wgt = 1.0 if kk == 1 else 0.5
                for cl in range(NCLS):
                    sl = lambda m: modraw[:, l, m * 8:(m + 1) * 8, cl]
                    k.copy(k.dve, tabB[:, l, kk, cl, :], sl(3 * kk))
                    k.stt(tabA[:, l, kk, cl, :], sl(3 * kk + 1), 1.0, gsb[:, l, 2 * kk, :], ALU.add, ALU.mult)
                    k.stt(tabG[:, l, kk, cl, :], sl(3 * kk + 2), wgt, gsb[:, l, 2 * kk + 1, :], ALU.mult, ALU.mult)
        k.barrier()
    k.es = es
    for l_ in range(L):
        if l_ > 0:
            convert_ffn(l_, 0)
        convert_mixer(l_)
        convert_ffn(l_, 1)

    def norm_stats(src_fn, n, ps, sqbuf, rstd_out):
        for c in range(KC):
            k.actf(sqbuf[:, c % 2, :n], src_fn(c), AF.Square)
            k.mm(ps[:, :n], ONES(), sqbuf[:, c % 2, :n], start=(c == 0), stop=(c == KC - 1))
        k.actf(rstd_out, ps[:, :n], AF.Ln, bias=epsb[:, 0:1], scale=1.0 / D)
        k.actf(rstd_out, rstd_out, AF.Exp, scale=-0.5)

    def ffn_phase(l, kk, b):
        which = 0 if kk == 0 else 1
        TT = 768
        with contextlib.ExitStack() as es2:
            k.es = es2
            hid = k.sbuf("hid", [128, FC, TT], BF16, slotted=True)
            ybuf = k.sbuf("ybuf", [128, KC, TT], F32)
            hn = Buf(ybuf.t[:, :, :].rearrange("p c t -> p (c t)").bitcast(BF16)[:, 0:KC * TT]
                     .rearrange("p (c t) -> p c t", c=KC), "hn")
            hn.h = ybuf.h
            sq = k.sbuf("sq", [128, 2, 512], BF16, slotted=True)
            rstd = k.sbuf("rstd", [128, 512], F32)
            tmp = k.sbuf("tmp", [128, 2, 512], F32, slotted=True)
            sg = k.sbuf("sg", [128, 2, 512], F32, slotted=True)
            wg_p = WPool(k, "wg", [128, KC, 256], 3)
            wu_p = WPool(k, "wu", [128, KC, 256], 3)
            wd_p = WPool(k, "wd", [128, FC, 128], 2)
            for t0 in range(0, NT, TT):
                t1 = min(NT, t0 + TT)
                groups = col_groups(t0, t1, CTX)
                if kk == 2 and l == L - 1:
                    groups = [g for g in groups if not g[2]]
                for (c0, n, isc) in groups:
                    cl = CLS_CTX if isc else b
                    o = c0 - t0
                    norm_stats(lambda c: Rv(c0, n)[:, c, c0:c0 + n], n, PS[0], sq, rstd[:, :n])
                    for c in range(KC):
                        tb = tmp[:, c % 2, :n]
                        k.stt(tb, Rv(c0, n)[:, c, c0:c0 + n], tabA[:, l, kk, cl, c:c + 1], rstd[:, :n],
                              ALU.mult, ALU.mult)
                        k.actf(hn[:, c, o:o + n], tb, AF.Identity, bias=tabB[:, l, kk, cl, c:c + 1])
                for j in range(FC):
                    def ldg(s, dma, j=j):
                        dma(s[:, :, :], wreg(wgu_d, l, which)[l, which, 0, j // 2, :, :, :])

                    def ldu(s, dma, j=j):
                        dma(s[:, :, :], wreg(wgu_d, l, which)[l, which, 1, j // 2, :, :, :])
                    if j % 2 == 0:
                        wg2, wu2 = wg_p.load(ldg), wu_p.load(ldu)
                    jo = (j % 2) * 128
                    wg = _Reg(wg2, [wg2.h])
                    wu = _Reg(wu2, [wu2.h])
                    for gi, (c0, n, isc) in enumerate(groups):
                        o = c0 - t0
                        pg, pu = PS[1 + 2 * ((j * 2 + gi) % 2)], PS[2 + 2 * ((j * 2 + gi) % 2)]
                        for c in range(KC):
                            k.mm(pg[:, :n], wg[:, c, jo:jo + 128], hn[:, c, o:o + n], start=(c == 0), stop=(c == KC - 1))
                        for c in range(KC):
                            k.mm(pu[:, :n], wu[:, c, jo:jo + 128], hn[:, c, o:o + n], start=(c == 0), stop=(c == KC - 1))
                        sgb = sg[:, (j * 2 + gi) % 2, :n]
                        k.actf(sgb, pg[:, :n], AF.Silu)
                        k.tt(k.dve, hid[:, j, o:o + n], sgb, pu[:, :n], ALU.mult)
                for i in range(KC):
                    def ldd(s, dma, i=i):
                        dma(s[:, :, :], wreg(wdn_d, l, which)[l, which, i, :, :, :])
                    wd = wd_p.load(ldd)
                    for gi, (c0, n, isc) in enumerate(groups):
                        o = c0 - t0
                        py = PS[5 + ((i * 2 + gi) % 2)]
                        for j in range(FC):
                            k.mm(py[:, :n], wd[:, j, :], hid[:, j, o:o + n], start=(j == 0), stop=(j == FC - 1))
                        k.copy(k.act, ybuf[:, i, o:o + n], py[:, :n])
                for (c0, n, isc) in groups:
                    cl = CLS_CTX if isc else b
                    o = c0 - t0
                    norm_stats(lambda c: ybuf[:, c, o:o + n], n, PS[0], sq, rstd[:, :n])
                    for c in range(KC):
                        tb = tmp[:, c % 2, :n]
                        k.stt(tb, ybuf[:, c, o:o + n], tabG[:, l, kk, cl, c:c + 1], rstd[:, :n], ALU.mult, ALU.mult)
                        k.tt(k.dve, Rv(c0, n)[:, c, c0:c0 + n], Rv(c0, n)[:, c, c0:c0 + n], tb, ALU.add)
            k.barrier()
        k.es = es


    def wtile(pool, l, u, nu=1):
        def ld(s_, dma):
            if nu == 1:
                dma(s_[:, :, :], wreg(wi_d, l, 0)[l, u, :, :, :])
            else:
                for i in range(nu):
                    dma(s_[:, i, :, :], wreg(wi_d, l, 0)[l, u + i, :, :, :])
        return pool.load(ld)

    def mixer_phase(l, b):
        MT = 256
        NKT = NT // 128
        with contextlib.ExitStack() as es2:
            k.es = es2
            Kst = k.sbuf("Kst", [128, 2, NT], BF16, slotted=True)
            Vst = k.sbuf("Vst", [128, 2, NKT, 128], BF16, slotted=True)
            Hf = k.sbuf("Hf", [128, 4, NT], BF16, slotted=True)
            hnb = k.sbuf("hnb", [128, KC, MT + 4], BF16, slotted=True)
            sq = k.sbuf("msq", [128, 2, MT + 8], BF16, slotted=True)
            rstd = k.sbuf("mrstd", [128, MT + 8], F32)
            tmp = k.sbuf("mtmp", [128, 2, MT + 8], F32, slotted=True)
            ropeC = k.sbuf("ropeC", [128, MT], F32)
            ropeS = k.sbuf("ropeS", [128, MT], F32)
            stage = k.sbuf("stage", [128, 2, MT + 4], F32, slotted=True)
            qk = k.sbuf("qk", [128, 8, MT], BF16, slotted=True)
            Vm = k.sbuf("Vm", [128, MT // 128, 4, 129], BF16, slotted=True)
            gsbuf = k.sbuf("gsbuf", [128, MT // 128, 16], F32)
            lfb = k.sbuf("lfb", [128, MT // 128, 4], F32)
            sm = k.sbuf("sm", [128, 32], F32)
            lfrep = k.sbuf("lfrep", [128, 2, 128], F32, slotted=True)
            DT = k.sbuf("DT", [128, 4, 128], BF16, slotted=True)
            ebt = k.sbuf("ebt", [128, 4, 128], F32, slotted=True)
            qp = k.sbuf("qp", [128, 4, 128], BF16, slotted=True)
            STm = k.sbuf("STm", [128, 4, 128], BF16, slotted=True)
            kw = k.sbuf("kw", [128, 4, 128], BF16, slotted=True)
            dd = k.sbuf("dd", [128, 4, 128], F32, slotted=True)
            St = k.sbuf("St", [128, 4, 129], F32, slotted=True)
            Cbf = k.sbuf("Cbf", [128, 4, 128], BF16, slotted=True)
            nbc = k.sbuf("nbc", [128, 4, 128], BF16, slotted=True)
            carry = k.sbuf("carry", [128, 8, 2], F32)
            rawA = k.sbuf("rawA", [128, KC * MT], F32)
            hsum = Buf(rawA.t[:, 0:4 * MT].rearrange("p (h t) -> p h t", h=4), "hsum")
            qa = Buf(rawA.t[:, 4 * MT:6 * MT].bitcast(BF16).rearrange("p (h t) -> p h t", h=4), "qa")
            qb = Buf(rawA.t[:, 6 * MT:8 * MT].bitcast(BF16).rearrange("p (h t) -> p h t", h=4), "qb")
            ybuf = Buf(rawA.t[:, :].rearrange("p (c t) -> p c t", c=KC), "mybuf")
            for bb in (hsum, qa, qb, ybuf):
                bb.h = rawA.h
            yy = k.sbuf("yy", [128, 3, 4, MT], BF16, slotted=True)
            Pt = k.sbuf("Pt", [128, 4, MT], BF16, slotted=True)
            Pm = k.sbuf("Pm", [128, 4, MT], BF16, slotted=True)
            rden = k.sbuf("rden", [128, 2, MT], F32, slotted=True)
            sig = k.sbuf("sig", [128, 2, MT], F32, slotted=True)
            gz = k.sbuf("gz", [128, 2, MT], F32, slotted=True)
            msum = k.sbuf("msum", [128, KC, MT], BF16, slotted=True)
            wq_p = WPool(k, "wq", [128, KC, 128], 4)
            wv_p = WPool(k, "wv", [128, 2, KC, 128], 2)
            wb_p = WPool(k, "wb", [128, 4, 128], 3)
            wo_p = WPool(k, "wo", [128, KC, 128], 2)
            k.memset(k.dve, Vm[:, :, :, 128:129], 1.0)

            tiles = [(0, CTX, True)] + [(CTX + i * MT, MT, False) for i in range(T // MT)]
            skip_ctx_out = (l == L - 1)

            def seqb(isc):
                return (0, CTX) if isc else (CTX, NT)

            def bc(v, shape, axis):
                return V(v.ap.unsqueeze(axis).broadcast_to(shape), v.hs)

            def norm_tile(t0, n, isc, lo, hi):
                cl = CLS_CTX if isc else b
                m = hi - lo
                o = lo - (t0 - 2)
                norm_stats(lambda c: Rv(lo, m)[:, c, lo:hi], m, PS[0], sq, rstd[:, :m])
                for c in range(KC):
                    tb = tmp[:, c % 2, :m]
                    k.stt(tb, Rv(lo, m)[:, c, lo:hi], tabA[:, l, 1, cl, c:c + 1], rstd[:, :m], ALU.mult, ALU.mult)
                    k.actf(hnb[:, c, o:o + m], tb, AF.Identity, bias=tabB[:, l, 1, cl, c:c + 1])

            def rope_tile(t0, n, isc):
                if isc:
                    k.memset(k.dve, ropeC[:, :n], 1.0)
                    k.memset(k.dve, ropeS[:, :n], 0.0)
                    return
                r0 = (t0 - CTX) // 64
                nr = n // 64
                for tab, oi in ((ropeC, 0), (ropeS, 2)):
                    a0 = bc(ropec[:, oi, r0:r0 + nr], [128, nr, 64], 2)
                    a1 = bc(ropec[:, oi + 1, :], [128, nr, 64], 1)
                    outv = tab[:, :n]
                    k.tt(k.dve, V(outv.ap.rearrange("p (r c) -> p r c", c=64), outv.hs), a0, a1, ALU.add)

            def proj_fm(wt, o, n, ps):
                for c in range(KC):
                    k.mm(ps[:, :n], wt[:, c, :], hnb[:, c, o:o + n], start=(c == 0), stop=(c == KC - 1))

            def qk_post(ps, n, gcol, dst):
                xn = tmp[:, 0, :n]
                if gcol is not None:
                    k.actf(sq[:, 0, :n], ps[:, :n], AF.Square)
                    k.mm(PS[1][:, :n], cbf[:, 1, :], sq[:, 0, :n])
                    k.actf(rstd[:, :n], PS[1][:, :n], AF.Ln, bias=epsb[:, 0:1], scale=1.0 / 64)
                    k.actf(rstd[:, :n], rstd[:, :n], AF.Exp, scale=-0.5)
                    k.stt(xn, ps[:, :n], qkg[:, l, gcol:gcol + 1], rstd[:, :n], ALU.mult, ALU.mult)
                else:
                    k.copy(k.act, xn, ps[:, :n])
                k.copy(k.act, sq[:, 1, :n], xn)
                k.mm(PS[1][:, :n], cbf[:, 2, :], sq[:, 1, :n])
                t2 = tmp[:, 1, :n]
                k.tt(k.dve, t2, PS[1][:, :n], ropeS[:, :n], ALU.mult)
                k.tt(k.dve, xn, xn, ropeC[:, :n], ALU.mult)
                k.tt(k.dve, dst, xn, t2, ALU.add)

            def mlstm_tile(t0, n, isc, dirn, use_carry):
                s0, s1 = seqb(isc)
                lo = max(t0 - 2, s0)
                hi = t0 + n if use_carry else min(t0 + n + 2, s1)
                nblk = n // 128
                ti = t0 // 256
                if dirn < 0:
                    k.dma(k.sp, qk[:, :, :], qk_d.r(ti)[ti, :, :, :], sp_sems["ld_qk"])
                    k.dma(k.sp, Vm[:, :, :, :], vm_d.r(ti)[ti, :, :, :, :], sp_sems["ld_vm"])
                    k.dma(k.sp, gsbuf[:, :, :], gs_d.r(ti)[ti, :, :, :], sp_sems["ld_gs"])
                for ch in (range(8) if dirn > 0 else ()):
                    wt = wtile(wq_p, l, U_QM + ch)
                    ps = PS[2 + ch % 2]
                    m = hi - lo
                    o = lo - (t0 - 2)
                    proj_fm(wt, o, m, ps)
                    if o > 0:
                        k.memset(k.dve, stage[:, ch % 2, 0:o], 0.0)
                    k.copy(k.act, stage[:, ch % 2, o:o + m], ps[:, :m])
                    if use_carry:
                        if t0 + n >= s1:
                            k.memset(k.dve, stage[:, ch % 2, n + 2:n + 4], 0.0)
                        else:
                            k.copy(k.dve, stage[:, ch % 2, n + 2:n + 4], carry[:, ch, :])
                    elif o + m < n + 4:
                        k.memset(k.dve, stage[:, ch % 2, o + m:n + 4], 0.0)
                    acc = tmp[:, ch % 2, :n]
                    k.ts(k.dve, acc, stage[:, ch % 2, 0:n], convw[:, l, ch, 0:1], convb[:, l, ch:ch + 1], ALU.mult, ALU.add)
                    for j in range(1, 5):
                        k.stt(acc, stage[:, ch % 2, j:j + n], convw[:, l, ch, j:j + 1], acc, ALU.mult, ALU.add)
                    if use_carry:
                        k.copy(k.dve, carry[:, ch, :], stage[:, ch % 2, 2:4])
                    k.actf(qk[:, ch, :n], acc, AF.Silu)
                for half in (range(2) if dirn > 0 else ()):
                    wt = wtile(wv_p, l, U_VM + 2 * half, 2)
                    for bi in range(nblk):
                        o = 2 + bi * 128
                        for c in range(KC):
                            k.mm(PS[2][:, :256], hnb[:, c, o:o + 128], wt[:, :, c, :], start=(c == 0), stop=(c == KC - 1))
                        pv = PS[2][:, :256]
                        k.copy(k.act, Vm[:, bi, 2 * half:2 * half + 2, 0:128],
                               V(pv.ap.rearrange("p (h d) -> p h d", h=2), pv.hs))
                if dirn > 0:
                    wg = wtile(wq_p, l, U_GM)
                    for bi in range(nblk):
                        o = 2 + bi * 128
                        for c in range(KC):
                            k.mm(PS[3][:, :16], hnb[:, c, o:o + 128], wg[:, c, 0:16], start=(c == 0), stop=(c == KC - 1))
                        k.tt(k.dve, gsbuf[:, bi, :], PS[3][:, :16], gateb[:, l, :], ALU.add)
                    k.dma(k.pool, qk_d.r(ti)[ti, :, :, :], qk[:, :, :], sp_sems["st_qk"])
                    k.dma(k.pool, vm_d.r(ti)[ti, :, :, :, :], Vm[:, :, :, :], sp_sems["st_vm"])
                    k.dma(k.pool, gs_d.r(ti)[ti, :, :, :], gsbuf[:, :, :], sp_sems["st_gs"])
                ic = 0 if dirn > 0 else 8
                fc = ic + 4
                k.actf(lfb[:, :nblk, :], gsbuf[:, :nblk, fc:fc + 4], AF.Exp, scale=-1.0)
                k.actf(lfb[:, :nblk, :], lfb[:, :nblk, :], AF.Ln, bias=1.0)
                k.ts(k.dve, lfb[:, :nblk, :], lfb[:, :nblk, :], -1.0, None, ALU.mult)
                blks = range(nblk) if dirn > 0 else range(nblk - 1, -1, -1)
                corder = (0, 1) if dirn > 0 else (1, 0)
                PSN = [PS[2], PS[3], PS[5], PS[6]]
                mskb = cbf[:, 4, :] if dirn > 0 else cbf[:, 5, :]
                for bi in blks:
                    c0 = t0 + bi * 128
                    bo = bi * 128
                    k.mm(PS[4][:, 0:4], TRI[dirn](), lfb[:, bi, :])
                    k.mm(PS[4][:, 8:12], cf[:, 4, :], lfb[:, bi, :])
                    k.tt(k.dve, sm[:, 0:4], gsbuf[:, bi, ic:ic + 4], PS[4][:, 0:4], ALU.subtract)
                    k.tt(k.dve, sm[:, 4:8], sm[:, 0:4], PS[4][:, 8:12], ALU.add)
                    k.actf(sm[:, 4:8], sm[:, 4:8], AF.Exp)
                    k.ts(k.dve, sm[:, 8:12], sm[:, 0:4], lnsc[:, 0:1], None, ALU.add)
                    for h in range(4):
                        lfr = lfrep[:, h % 2, :]
                        k.ts(k.dve, lfr, cf[:, 5, :], lfb[:, bi, h:h + 1], None, ALU.mult)
                        k.mm(PS[5][:, 128 * h:128 * h + 128], lfr, TRI[dirn]())
                        k.mm(PS[4][:, 16 + 2 * h:18 + 2 * h], lfr, cf[:, 6, 0:2])
                    k.actf(sm[:, 16:24], PS[4][:, 16:24], AF.Exp)
                    for h in range(4):
                        k.actf(DT[:, h, :], PS[5][:, 128 * h:128 * h + 128], AF.Exp, bias=sm[:, 8 + h:9 + h])
                    pb = PS[5][:, 0:512]
                    k.actf(ebt[:, :, :], V(pb.ap.rearrange("p (h t) -> p h t", h=4), pb.hs), AF.Exp, bias=lnsc[:, 0:1])
                    k.tt(k.dve, qp[:, :, :], qk[:, 0:4, bo:bo + 128], ebt[:, :, :], ALU.mult)
                    for h in range(4):
                        k.mm(PS[6][:, 128 * h:128 * h + 128], qk[:, 4 + h, bo:bo + 128], qk[:, h, bo:bo + 128])
                    pst = PS[6][:, 0:512]
                    k.tt(k.dve, STm[:, :, :], V(pst.ap.rearrange("p (h t) -> p h t", h=4), pst.hs), DT[:, :, :], ALU.mult)
                    k.tt(k.dve, STm[:, :, :], STm[:, :, :], bc(mskb, [128, 4, 128], 1), ALU.mult)
                    for h in range(4):
                        k.transpose(PSB[:, 128 * h:128 * h + 128], qk[:, 4 + h, bo:bo + 128], cbf[:, 3, :])
                    ptr = PSB[:, 0:512]
                    k.tt(k.dve, kw[:, :, :], V(ptr.ap.rearrange("p (h t) -> p h t", h=4), ptr.hs),
                         bc(sm[:, 4:8], [128, 4, 128], 2), ALU.mult)
                    sfl = STm[:, :, :]
                    k.mm(PS[0][:, 0:512], cbf[:, 0, :], V(sfl.ap.rearrange("p h t -> p (h t)"), sfl.hs), start=True, stop=False)
                    for h in range(4):
                        k.mm(PSN[h][:, 0:128], Vm[:, bi, h, 0:128], STm[:, h, :], start=True, stop=False)
                    for ci, cc in enumerate(corder):
                        cs = slice(cc * 64, cc * 64 + 64)
                        for h in range(4):
                            last = ci == 1
                            k.mm(PSN[h][:, cs], Cbf[:, h, :], qp[:, h, cs], start=False, stop=last)
                            k.mm(PS[0][:, 128 * h + cc * 64:128 * h + cc * 64 + 64], nbc[:, h, :], qp[:, h, cs],
                                 start=False, stop=(last and h == 3))
                            pu_ = PS[1] if h % 2 == 0 else PS[4]
                            k.mm(pu_[:, 0:129], kw[cs, h, :], Vm[cs, bi, h, :])
                            k.stt(St[:, h, :], St[:, h, :], sm[:, 16 + 2 * h + cc:17 + 2 * h + cc], pu_[:, 0:129],
                                  ALU.mult, ALU.add)
                            k.copy(k.act, Cbf[:, h, :], St[:, h, 0:128])
                            k.ts(k.dve, nbc[:, h, :], cf[:, 5, :], St[:, h, 128:129], None, ALU.mult)
                    pdn = PS[0][:, 0:512]
                    k.actf(dd[:, :, :], V(pdn.ap.rearrange("p (h t) -> p h t", h=4), pdn.hs), AF.Abs)
                    k.ts(k.dve, dd[:, :, :], dd[:, :, :], 1.0, None, ALU.max)
                    k.actf(dd[:, :, :], dd[:, :, :], AF.Ln)
                    k.actf(dd[:, :, :], dd[:, :, :], AF.Exp, scale=-1.0)
                    for h in range(4):
                        if dirn > 0:
                            k.tt(k.dve, Hf[:, h, c0:c0 + 128], PSN[h][:, 0:128], dd[:, h, :], ALU.mult)
                        else:
                            k.tt(k.dve, hsum[:, h, bo:bo + 128], PSN[h][:, 0:128], dd[:, h, :], ALU.mult)
                    if dirn < 0:
                        k.tt(k.dve, hsum[:, :, bo:bo + 128], hsum[:, :, bo:bo + 128], Hf[:, :, c0:c0 + 128], ALU.add)

            def attention(t0, n, isc, mixer):
                qsrc = qa if mixer == 0 else qb
                nct = CTX // 128
                kts = [(j, 0, n, None) for j in range(nct)]
                if not isc:
                    ql0 = t0 - CTX
                    if mixer == 1:
                        kts += [(nct + j, 0, n, None) for j in range(T // 128)]
                    else:
                        for j in range(ql0 // 128 - 1, (ql0 + n) // 128 + 1):
                            if j < 0 or j >= T // 128:
                                continue
                            a = max(128 * (j - 1), ql0)
                            e = min(128 * (j + 2), ql0 + n)
                            kts.append((nct + j, a - ql0, e - ql0, a - (128 * j - 128)))
                pairs = [(c, ki) for c in range(4) for ki in range(len(kts))]
                nk = len(kts)
                PSS = [PS[0], PS[1], PS[6]]

                def stage_a(pi):
                    c, ki = pairs[pi]
                    kt, a, e, mo = kts[ki]
                    m = e - a
                    out = []
                    for hf in range(2):
                        idx = 2 * pi + hf
                        pr = slice(64 * hf, 64 * hf + 64)
                        pss = PSS[idx % 3]
                        k.mm(pss[:, :m], Kst[pr, mixer, kt * 128:(kt + 1) * 128], qsrc[pr, c, a:e])
                    for hf in range(2):
                        idx = 2 * pi + hf
                        pss = PSS[idx % 3]
                        pt = Pt[:, idx % 4, :m]
                        k.actf(pt, pss[:, :m], AF.Exp, scale=0.125)
                        if mo is not None:
                            pm = Pm[:, idx % 4, :m]
                            k.tt(k.dve, pm, pt, winmask[:, mo:mo + m], ALU.mult)
                            pt = pm
                        out.append(pt)
                    return out

                def stage_b(pi, pts_):
                    c, ki = pairs[pi]
                    kt, a, e, mo = kts[ki]
                    pn, pd = PS[2 + 2 * (c % 2)], PS[3 + 2 * (c % 2)]
                    lastk = ki == nk - 1
                    for hf in range(2):
                        pr = slice(64 * hf, 64 * hf + 64)
                        k.mm(pn[pr, a:e], Vst[:, mixer, kt, hf * 64:(hf + 1) * 64], pts_[hf], start=(ki == 0), stop=lastk)
                    for hf in range(2):
                        pr = slice(64 * hf, 64 * hf + 64)
                        k.mm(pd[pr, a:e], cbf[:, 0, 0:64], pts_[hf], start=(ki == 0), stop=lastk)
                    if lastk:
                        for hf in range(2):
                            h = c + 4 * hf
                            pr = slice(64 * hf, 64 * hf + 64)
                            rd = rden[pr, hf, :n]
                            if mixer == 0:
                                k.actf(rd, pd[pr, :n], AF.Ln, bias=esink[pr, l, h:h + 1])
                            else:
                                k.actf(rd, pd[pr, :n], AF.Ln)
                            k.actf(rd, rd, AF.Exp, scale=-1.0)
                            k.tt(k.dve, yy[pr, mixer, c, :n], pn[pr, :n], rd, ALU.mult)

                held = {}
                for pi in range(len(pairs) + 1):
                    if pi < len(pairs):
                        held[pi] = stage_a(pi)
                    if pi >= 1:
                        stage_b(pi - 1, held.pop(pi - 1))

            k.memset(k.dve, St[:, :, :], 0.0)
            k.memset(k.dve, Cbf[:, :, :], 0.0)
            k.memset(k.dve, nbc[:, :, :], 0.0)
            for (t0, n, isc) in tiles:
                s0, s1 = seqb(isc)
                lo, hi = max(t0 - 2, s0), min(t0 + n + 2, s1)
                norm_tile(t0, n, isc, lo, hi)
                rope_tile(t0, n, isc)
                for mixer in range(2):
                    wt = wtile(wq_p, l, U_KA if mixer == 0 else U_KB)
                    ps = PS[2 + mixer]
                    proj_fm(wt, 2, n, ps)
                    qk_post(ps, n, (1 if mixer == 1 else None), Kst[:, mixer, t0:t0 + n])
                wt = wtile(wv_p, l, U_VAB, 2)
                for bi in range(n // 128):
                    o = 2 + bi * 128
                    for c in range(KC):
                        k.mm(PS[4][:, :256], hnb[:, c, o:o + 128], wt[:, :, c, :], start=(c == 0), stop=(c == KC - 1))
                    pv = PS[4][:, :256]
                    kt = (t0 + bi * 128) // 128
                    k.copy(k.act, Vst[:, :, kt, :], V(pv.ap.rearrange("p (m d) -> p m d", m=2), pv.hs))
                mlstm_tile(t0, n, isc, 1, False)

            k.memset(k.dve, St[:, :, :], 0.0)
            k.memset(k.dve, Cbf[:, :, :], 0.0)
            k.memset(k.dve, nbc[:, :, :], 0.0)
            tiles2 = [tiles[0]] + tiles[:0:-1]
            for (t0, n, isc) in tiles2:
                cl = CLS_CTX if isc else b
                s0, s1 = seqb(isc)
                lo = max(t0 - 2, s0)
                norm_tile(t0, n, isc, lo, t0 + n)
                mlstm_tile(t0, n, isc, -1, True)
                if isc and skip_ctx_out:
                    continue
                rope_tile(t0, n, isc)
                for h in range(4):
                    k.actf(sq[:, h % 2, :n], hsum[:, h, :n], AF.Square)
                    k.mm(PS[0][:, :n], cbf[:, 0, :], sq[:, h % 2, :n])
                    k.actf(rstd[:, :n], PS[0][:, :n], AF.Ln, bias=epsb[:, 0:1], scale=1.0 / 128)
                    k.actf(rstd[:, :n], rstd[:, :n], AF.Exp, scale=-0.5)
                    wt = wtile(wq_p, l, U_OM + h)
                    proj_fm(wt, 2, n, PS[1])
                    k.actf(sig[:, h % 2, :n], PS[1][:, :n], AF.Sigmoid)
                    tb = tmp[:, h % 2, :n]
                    k.stt(tb, hsum[:, h, :n], mgs[:, l, h:h + 1], rstd[:, :n], ALU.mult, ALU.mult)
                    k.tt(k.dve, yy[:, 2, h, :n], tb, sig[:, h % 2, :n], ALU.mult)
                for mixer in range(2):
                    for c in range(4):
                        wt = wtile(wq_p, l, (U_QA if mixer == 0 else U_QB) + c)
                        ps = PS[2 + c % 2]
                        proj_fm(wt, 2, n, ps)
                        qk_post(ps, n, (0 if mixer == 1 else None), (qa if mixer == 0 else qb)[:, c, :n])
                attention(t0, n, isc, 0)
                attention(t0, n, isc, 1)
                for i in range(KC):
                    for br in range(3):
                        def ldb(s_, dma, br=br, i=i):
                            dma(s_[:, :, :], wreg(wb_d, l, 0)[l, br, i, :, :, :])
                        wb = wb_p.load(ldb)
                        pz = PS[2 + br % 2]
                        for c in range(4):
                            k.mm(pz[:, :n], wb[:, c, :], yy[:, br, c, :n], start=(c == 0), stop=(c == 3))
                        wt = wtile(wq_p, l, U_GL + br * 8 + i)
                        pg = PS[4 + br % 2]
                        proj_fm(wt, 2, n, pg)
                        sg_ = sig[:, br % 2, :n]
                        k.actf(sg_, pg[:, :n], AF.Sigmoid)
                        if br == 0:
                            k.tt(k.dve, gz[:, 0, :n], sg_, pz[:, :n], ALU.mult)
                        else:
                            k.tt(k.dve, gz[:, 1, :n], sg_, pz[:, :n], ALU.mult)
                            dst = msum[:, i, :n] if br == 2 else gz[:, 0, :n]
                            k.tt(k.dve, dst, gz[:, 0, :n], gz[:, 1, :n], ALU.add)
                for o_ in range(KC):
                    def ldo(s_, dma, o_=o_):
                        dma(s_[:, :, :], wreg(wo_d, l, 0)[l, o_, :, :, :])
                    wo = wo_p.load(ldo)
                    py = PS[2 + o_ % 2]
                    for i in range(KC):
                        k.mm(py[:, :n], wo[:, i, :], msum[:, i, :n], start=(i == 0), stop=(i == KC - 1))
                    k.copy(k.act, ybuf[:, o_, :n], py[:, :n])
                norm_stats(lambda c: ybuf[:, c, :n], n, PS[0], sq, rstd[:, :n])
                for c in range(KC):
                    tb = tmp[:, c % 2, :n]
                    k.stt(tb, ybuf[:, c, :n], tabG[:, l, 1, cl, c:c + 1], rstd[:, :n], ALU.mult, ALU.mult)
                    k.tt(k.dve, Rv(t0, n)[:, c, t0:t0 + n], Rv(t0, n)[:, c, t0:t0 + n], tb, ALU.add)
            k.barrier()
        k.es = es

    for b in range(NB):
        k.dma(k.sp, R.rs(range(CTX // 256))[:, :, 0:CTX], ctxT[b, :, :, :], sem_in)
        k.dma(k.sp, R.rs(range(CTX // 256, NRB))[:, :, CTX:NT], xT[b, :, :, :], sem_in)
        for l in range(L):
            ffn_phase(l, 0, b)
            if cfg.stop_after == "ffn1":
                break
            mixer_phase(l, b)
            if cfg.stop_after == "mixer":
                break
            ffn_phase(l, 2, b)
        k.dma(k.sp, outT[b, :, :, :], R.rs(range(CTX // 256, NRB))[:, :, CTX:NT], sem_out)
    k.final_wait(k.sp, [sem_out])
    return nc, es, k


def make_consts():
    c = np.zeros((128, 8, 128), np.float32)
    c[:, 0, :] = 1.0
    p = np.arange(128)
    c[:, 1, :] = (p[:, None] // 64 == p[None, :] // 64)
    c[:, 2, :] = (p[:, None] == (p[None, :] ^ 16))
    c[:, 3, :] = np.eye(128)
    same = (p[:, None] // 64 == p[None, :] // 64)
    c[:, 4, :] = same & (p[:, None] <= p[None, :])
    c[:, 5, :] = same & (p[:, None] >= p[None, :])
    return c


def make_consts_f():
    c = np.zeros((128, 8, 128), np.float32)
    p = np.arange(128)
    same = (p[:, None] // 64 == p[None, :] // 64)
    c[:, 0, :] = same & (p[:, None] <= p[None, :])
    c[:, 1, :] = same & (p[:, None] >= p[None, :])
    c[:, 2, :] = np.where(c[:, 0, :] > 0, 0.0, NEG)
    c[:, 3, :] = np.where(c[:, 1, :] > 0, 0.0, NEG)
    c[:, 4, :] = same
    c[:, 5, :] = 1.0
    c[:, 6, 0] = p < 64
    c[:, 6, 1] = p >= 64
    return c


def make_winmask():
    s_ = np.arange(128)[:, None]
    q = np.arange(384)[None, :]
    return (np.abs(q - 128 - s_) <= 128).astype(np.float32)


def make_rope(T):
    rows = T // GRID_W
    freqs = (np.float32(10000.0) ** (-np.arange(16, dtype=np.float32) / np.float32(16))).astype(np.float32)
    r = np.zeros((128, 4, 64), np.float32)
    for p in range(128):
        d = p % 64
        axis, half, pair = d // 32, (d % 32) // 16, d % 16
        sign = -1.0 if half == 0 else 1.0
        if axis == 0:
            ang = (np.arange(rows, dtype=np.float32) * freqs[pair]).astype(np.float32)
            r[p, 0, :rows] = np.cos(ang)
            r[p, 2, :rows] = sign * np.sin(ang)
        else:
            ang = (np.arange(64, dtype=np.float32) * freqs[pair]).astype(np.float32)
            r[p, 1, :] = np.cos(ang)
            r[p, 3, :] = sign * np.sin(ang)
    return r


def to_bf16(a):
    import ml_dtypes
    return a.astype(ml_dtypes.bfloat16)


def fm(a):
    a = np.swapaxes(a, -1, -2)
    sh = a.shape
    a = a.reshape(sh[:-2] + (KC, 128, sh[-1]))
    return np.ascontiguousarray(np.swapaxes(a, -3, -2))


def vec_t(a, nchunk):
    sh = a.shape
    a = a.reshape(sh[:-1] + (nchunk, 128))
    return np.ascontiguousarray(np.moveaxis(a, -1, 0))


def host_inputs(cfg, inp, core):
    NB = cfg.NB
    bs = slice(core * NB, (core + 1) * NB)
    L = cfg.L
    m = {}
    m["xT"] = fm(inp["x"][bs])
    m["ctxT"] = fm(inp["ctx"][bs])
    cc = np.concatenate([inp["c"][bs], inp["c_ctx"][None, :]], axis=0)
    m["c_in"] = np.ascontiguousarray(cc.T.reshape(KC, 128, NB + 1).transpose(1, 0, 2))
    m["w_ada"] = inp["w_ada"][:L]
    m["b_ada_t"] = vec_t(inp["b_ada"][:L], 72)
    m["norm_g_t"] = vec_t(inp["norm_g"][:L], KC)
    m["ffn_w_gate"] = inp["ffn_w_gate"][:L]
    m["ffn_w_up"] = inp["ffn_w_up"][:L]
    m["ffn_w_down"] = inp["ffn_w_down"][:L]
    m["w_in"] = inp["w_in"][:L]
    m["w_branch"] = inp["w_branch"][:L]
    m["w_out"] = inp["w_out"][:L]
    m["consts_bf"] = make_consts()
    m["consts_f"] = make_consts_f()
    m["winmask"] = make_winmask()
    m["rope"] = make_rope(cfg.T)
    cw = inp["conv_w"][:L]
    m["convw_t"] = np.ascontiguousarray(cw.reshape(L, 5, 8, 128).transpose(3, 0, 2, 1))
    m["convb_t"] = vec_t(inp["conv_b"][:L], 8)
    m["mg_t"] = vec_t(inp["mlstm_norm_g"][:L], 4)
    qg = inp["qk_norm_g"][:L]
    m["qkg_t"] = np.ascontiguousarray(np.tile(qg, (1, 1, 2)).transpose(2, 0, 1))
    m["sink_t"] = np.ascontiguousarray(np.broadcast_to(inp["attn_sink"][:L][None], (128, L, 8)))
    m["gateb_t"] = np.ascontiguousarray(np.broadcast_to(inp["mlstm_gate_b"][:L][None], (128, L, 16)))
    return m


_CACHE = {}


def run(cfg, inputs):
    key = (cfg.T, cfg.CTX, cfg.L, cfg.NB, cfg.n_cores, cfg.stop_after)
    if key not in _CACHE:
        _CACHE[key] = build_program(cfg)
    nc, es, kk = _CACHE[key]
    inp = {n: np.asarray(v) for n, v in inputs.items()}
    in_maps = [host_inputs(cfg, inp, core) for core in range(cfg.n_cores)]
    res = run_bass_kernel_spmd(nc, in_maps, core_ids=list(range(cfg.n_cores)))
    outs = []
    for core in range(cfg.n_cores):
        o = res.results[core]["outT"]
        o = np.swapaxes(o, 1, 2).reshape(cfg.NB, D, cfg.T)
        outs.append(np.swapaxes(o, 1, 2))
    return np.ascontiguousarray(np.concatenate(outs, axis=0)).astype(np.float32)


def kernel(**inputs):
    cfg = Cfg()
    return run(cfg, inputs)
```
